# Optimizing a Trainium2 kernel written in Bass

```python
import jax
import jax.numpy as jnp
from jax import lax
import numpy as np

D_MODEL = 1024
BATCH = 4
SEQ = 4096
DEPTH = 1

GRID_W = 64
CTX_LEN = 256
HEAD_DIM = 64
ATTN_WIDTH = D_MODEL // 2
N_Q_HEADS = ATTN_WIDTH // HEAD_DIM
N_KV_HEADS = N_Q_HEADS // 4
Q_PER_KV = N_Q_HEADS // N_KV_HEADS
KV_WIDTH = N_KV_HEADS * HEAD_DIM
WINDOW = 128
BLOCK = 128
ROPE_BASE = 10000.0
RWKV_WIDTH = D_MODEL // 2
RWKV_HEAD_DIM = 64
RWKV_HEADS = RWKV_WIDTH // RWKV_HEAD_DIM
DECAY_LORA = 32
ICLR_LORA = 32
GATE_LORA = 96
RWKV_IN_WIDTH = 3 * RWKV_WIDTH + DECAY_LORA + ICLR_LORA + GATE_LORA
SHORT_CONV = 3
N_BRANCHES = 2
IN_WIDTH = ATTN_WIDTH + 2 * KV_WIDTH + RWKV_IN_WIDTH + N_BRANCHES * D_MODEL
D_FF = 4 * D_MODEL
N_MOD = 6
NORM_EPS = 1e-6
LNX_EPS = 1e-5 * RWKV_HEAD_DIM
IN_SPLITS = [ATTN_WIDTH, ATTN_WIDTH + KV_WIDTH, ATTN_WIDTH + 2 * KV_WIDTH,
             ATTN_WIDTH + 2 * KV_WIDTH + RWKV_IN_WIDTH]
RW_SPLITS = [RWKV_WIDTH, 2 * RWKV_WIDTH, 3 * RWKV_WIDTH,
             3 * RWKV_WIDTH + DECAY_LORA, 3 * RWKV_WIDTH + DECAY_LORA + ICLR_LORA]

kernel_name = "hybrid_gqa_rwkv7_dit_block"


def rmsnorm(x, g):
    xf = x.astype(jnp.float32)
    y = xf * lax.rsqrt(jnp.mean(xf * xf, axis=-1, keepdims=True) + NORM_EPS)
    return (y * g.astype(jnp.float32)).astype(x.dtype)


def modulate(h, shift, scale):
    return h * (1 + scale) + shift


def axial_rope_tables(rows):
    n_freq = HEAD_DIM // 4
    inv_freq = jnp.power(ROPE_BASE, -jnp.arange(n_freq, dtype=jnp.float32) / n_freq)
    row = jnp.repeat(jnp.arange(rows, dtype=jnp.float32), GRID_W)
    col = jnp.tile(jnp.arange(GRID_W, dtype=jnp.float32), rows)
    ang = jnp.concatenate([row[:, None] * inv_freq, col[:, None] * inv_freq], axis=-1)
    return jnp.cos(ang), jnp.sin(ang)


def apply_rope(x, cos, sin):
    half = x.shape[-1] // 2
    x1, x2 = x[..., :half], x[..., half:]
    c = cos[:, None, :].astype(x.dtype)
    s = sin[:, None, :].astype(x.dtype)
    return jnp.concatenate([x1 * c - x2 * s, x1 * s + x2 * c], axis=-1)


def sink_column(sink, lead_shape):
    s = sink.astype(jnp.float32).reshape(N_KV_HEADS, Q_PER_KV)
    return jnp.broadcast_to(s[:, :, None, None], lead_shape + (1,))


def latent_window_attention(q, k, v, kc, vc, sink):
    B, L, _, dh = q.shape
    Lc = kc.shape[1]
    nb = L // BLOCK
    scale = dh ** -0.5
    qb = q.reshape(B, nb, BLOCK, N_KV_HEADS, Q_PER_KV, dh)
    pad = ((0, 0), (BLOCK, BLOCK), (0, 0), (0, 0))

    def band(t):
        tb = jnp.pad(t, pad).reshape(B, nb + 2, BLOCK, N_KV_HEADS, dh)
        return jnp.concatenate([tb[:, :-2], tb[:, 1:-1], tb[:, 2:]], axis=2)

    kb, vb = band(k), band(v)
    s_loc = jnp.einsum('bnqgrd,bnkgd->bngrqk', qb, kb).astype(jnp.float32) * scale
    s_ctx = jnp.einsum('bnqgrd,bcgd->bngrqc', qb, kc).astype(jnp.float32) * scale
    blk = jnp.arange(nb)[:, None, None]
    qi = jnp.arange(BLOCK)[None, :, None]
    kj = jnp.arange(3 * BLOCK)[None, None, :]
    i_pos = blk * BLOCK + qi
    j_pos = blk * BLOCK - BLOCK + kj
    valid = (jnp.abs(i_pos - j_pos) <= WINDOW) & (j_pos >= 0) & (j_pos < L)
    s_loc = jnp.where(valid[None, :, None, None], s_loc, -jnp.inf)
    s_sink = jnp.broadcast_to(sink.astype(jnp.float32).reshape(N_KV_HEADS, Q_PER_KV)[None, None, :, :, None, None],
                              s_loc.shape[:-1] + (1,))
    p = jax.nn.softmax(jnp.concatenate([s_loc, s_ctx, s_sink], axis=-1), axis=-1)
    p_loc = p[..., :3 * BLOCK].astype(v.dtype)
    p_ctx = p[..., 3 * BLOCK:3 * BLOCK + Lc].astype(v.dtype)
    o = (jnp.einsum('bngrqk,bnkgd->bnqgrd', p_loc, vb)
         + jnp.einsum('bngrqc,bcgd->bnqgrd', p_ctx, vc))
    return o.reshape(B, L, N_Q_HEADS * dh)


def context_attention(qc, kc, vc, sink):
    B, Lc, _, dh = qc.shape
    qg = qc.reshape(B, Lc, N_KV_HEADS, Q_PER_KV, dh)
    s = jnp.einsum('bqgrd,bkgd->bgrqk', qg, kc).astype(jnp.float32) * dh ** -0.5
    s_sink = sink_column(sink, s.shape[1:-1])[None]
    s_sink = jnp.broadcast_to(s_sink, s.shape[:-1] + (1,))
    p = jax.nn.softmax(jnp.concatenate([s, s_sink], axis=-1), axis=-1)
    o = jnp.einsum('bgrqk,bkgd->bqgrd', p[..., :Lc].astype(vc.dtype), vc)
    return o.reshape(B, Lc, N_Q_HEADS * dh)


def short_conv(u, w):
    L = u.shape[1]
    half = SHORT_CONV // 2
    up = jnp.pad(u, ((0, 0), (half, half), (0, 0)))
    return sum(up[:, t:t + L] * w[t] for t in range(SHORT_CONV))


def rwkv_prepare(u, decay_w0, decay_w2, iclr_a0, iclr_a2, k_k, k_a):
    B, L, _ = u.shape
    r, k, v, hw, ha, hg = jnp.split(u, RW_SPLITS, axis=-1)

    def heads(t):
        return t.astype(jnp.float32).reshape(B, L, RWKV_HEADS, RWKV_HEAD_DIM)

    kk = heads(k * k_k)
    kk = kk * lax.rsqrt(jnp.maximum(jnp.sum(kk * kk, axis=-1, keepdims=True), 1e-24))
    dirs = []
    for d in range(2):
        w_log = -jax.nn.softplus(-(decay_w0[d] + jnp.tanh(hw) @ decay_w2[d])) - 0.5
        a = jax.nn.sigmoid(iclr_a0[d] + ha @ iclr_a2[d])
        decay = jnp.exp(-jnp.exp(heads(w_log)))
        k_d = heads(k * (1 + (a - 1) * k_a))
        dirs.append((decay, kk * heads(a), k_d))
    return heads(r), heads(v), hg, kk, dirs


def wkv7_scan(S0, r, decay, kk, b, k, v, reverse, emit):
    xs = tuple(jnp.moveaxis(t, 1, 0) for t in (r, decay, kk, b, k, v))

    def step(S, inp):
        r_t, w_t, kk_t, b_t, k_t, v_t = inp
        sa = jnp.einsum('bhvk,bhk->bhv', S, -kk_t)
        S = S * w_t[:, :, None, :] + sa[..., None] * b_t[:, :, None, :] + v_t[..., None] * k_t[:, :, None, :]
        y = jnp.einsum('bhvk,bhk->bhv', S, r_t) if emit else None
        return S, y

    S, ys = lax.scan(step, S0, xs, reverse=reverse)
    return S, (jnp.moveaxis(ys, 0, 1) if emit else None)


def rwkv_output(y, r, v, hg, dirs, gate_g2, r_k, lnx_w, lnx_b):
    B, L = y.shape[:2]
    mu = jnp.mean(y, axis=-1, keepdims=True)
    var = jnp.mean(jnp.square(y - mu), axis=-1, keepdims=True)
    yn = (y - mu) * lax.rsqrt(var + LNX_EPS)
    k_sum = dirs[0][2] + dirs[1][2]
    bonus = jnp.sum(r * k_sum * r_k.astype(jnp.float32), axis=-1, keepdims=True) * v
    out = (yn.reshape(B, L, RWKV_WIDTH) * lnx_w + lnx_b + bonus.reshape(B, L, RWKV_WIDTH)).astype(hg.dtype)
    g = jax.nn.sigmoid(hg) @ gate_g2
    return out * g


def merge_branches(ya, yr, gate_logits, w_branch_attn, w_branch_rwkv, w_out):
    ga, gr = jnp.split(jax.nn.sigmoid(gate_logits), N_BRANCHES, axis=-1)
    return (ga * (ya @ w_branch_attn) + gr * (yr @ w_branch_rwkv)) @ w_out


def sq_relu_mlp(h, w_up, w_down):
    return jnp.square(jax.nn.relu(h @ w_up)) @ w_down


def trunk_layer(x, xc, c, c_ctx, cos, sin, w_ada, b_ada, norm1_g, w_in, sink, conv_w,
                decay_w0, decay_w2, iclr_a0, iclr_a2, gate_g2, k_k, k_a, r_k, lnx_w, lnx_b,
                w_branch_attn, w_branch_rwkv, w_out, norm2_g, w_mlp_up, w_mlp_down, update_ctx):
    B, L, D = x.shape
    Lc = xc.shape[1]
    mod = (jax.nn.silu(c) @ w_ada + b_ada).reshape(B, 1, N_MOD, D)
    shift1, scale1, gate1, shift2, scale2, gate2 = [mod[:, :, i] for i in range(N_MOD)]
    modc = (jax.nn.silu(c_ctx) @ w_ada + b_ada).reshape(N_MOD, D)

    h = modulate(rmsnorm(x, norm1_g), shift1, scale1)
    hc = modulate(rmsnorm(xc, norm1_g), modc[0], modc[1])
    q, k, v, rw, gl = jnp.split(h @ w_in, IN_SPLITS, axis=-1)
    qc, kc, vc, rwc, glc = jnp.split(hc @ w_in, IN_SPLITS, axis=-1)

    q = apply_rope(q.reshape(B, L, N_Q_HEADS, HEAD_DIM), cos, sin)
    k = apply_rope(k.reshape(B, L, N_KV_HEADS, HEAD_DIM), cos, sin)
    v = v.reshape(B, L, N_KV_HEADS, HEAD_DIM)
    kc = kc.reshape(B, Lc, N_KV_HEADS, HEAD_DIM)
    vc = vc.reshape(B, Lc, N_KV_HEADS, HEAD_DIM)
    ya = latent_window_attention(q, k, v, kc, vc, sink)

    r_l, v_l, hg_l, kk_l, dirs_l = rwkv_prepare(short_conv(rw, conv_w), decay_w0, decay_w2,
                                               iclr_a0, iclr_a2, k_k, k_a)
    r_c, v_c, hg_c, kk_c, dirs_c = rwkv_prepare(short_conv(rwc, conv_w), decay_w0, decay_w2,
                                               iclr_a0, iclr_a2, k_k, k_a)
    S0 = jnp.zeros((B, RWKV_HEADS, RWKV_HEAD_DIM, RWKV_HEAD_DIM), jnp.float32)
    y_lat = []
    y_ctx = []
    for d, rev in enumerate((False, True)):
        dec_c, b_c, k_c = dirs_c[d]
        S_ctx, yc_d = wkv7_scan(S0, r_c, dec_c, kk_c, b_c, k_c, v_c, rev, update_ctx)
        dec_l, b_l, k_l = dirs_l[d]
        _, y_d = wkv7_scan(S_ctx, r_l, dec_l, kk_l, b_l, k_l, v_l, rev, True)
        y_lat.append(y_d)
        y_ctx.append(yc_d)
    yr = rwkv_output(y_lat[0] + y_lat[1], r_l, v_l, hg_l, dirs_l, gate_g2, r_k, lnx_w, lnx_b)

    x = x + gate1 * merge_branches(ya, yr, gl, w_branch_attn, w_branch_rwkv, w_out)
    x = x + gate2 * sq_relu_mlp(modulate(rmsnorm(x, norm2_g), shift2, scale2), w_mlp_up, w_mlp_down)

    if update_ctx:
        yac = context_attention(qc.reshape(B, Lc, N_Q_HEADS, HEAD_DIM), kc, vc, sink)
        yrc = rwkv_output(y_ctx[0] + y_ctx[1], r_c, v_c, hg_c, dirs_c, gate_g2, r_k, lnx_w, lnx_b)
        xc = xc + modc[2] * merge_branches(yac, yrc, glc, w_branch_attn, w_branch_rwkv, w_out)
        xc = xc + modc[5] * sq_relu_mlp(modulate(rmsnorm(xc, norm2_g), modc[3], modc[4]), w_mlp_up, w_mlp_down)
    return x, xc


def setup_inputs(seed: int = 0) -> dict:
    key = jax.random.key(seed)
    ks = iter(jax.random.split(key, 32))

    def nrm(shape, s):
        return jax.random.normal(next(ks), shape, jnp.float32) * s

    D = D_MODEL
    L = DEPTH
    conv_base = jnp.asarray([0.25, 1.0, 0.25], jnp.float32)[None, :, None]
    return {
        "x": nrm((BATCH, SEQ, D), 1.0),
        "c": nrm((BATCH, D), 1.0),
        "ctx": nrm((BATCH, CTX_LEN, D), 1.0),
        "c_ctx": nrm((D,), 1.0),
        "w_ada": nrm((L, D, N_MOD * D), 0.5 * D ** -0.5),
        "b_ada": nrm((L, N_MOD * D), 0.02),
        "norm1_g": 1.0 + nrm((L, D), 0.05),
        "w_in": nrm((L, D, IN_WIDTH), D ** -0.5),
        "sink": nrm((L, N_Q_HEADS), 0.5),
        "conv_w": conv_base + nrm((L, SHORT_CONV, RWKV_IN_WIDTH), 0.05),
        "decay_w0": nrm((L, 2, RWKV_WIDTH), 0.5),
        "decay_w2": nrm((L, 2, DECAY_LORA, RWKV_WIDTH), DECAY_LORA ** -0.5),
        "iclr_a0": nrm((L, 2, RWKV_WIDTH), 0.5),
        "iclr_a2": nrm((L, 2, ICLR_LORA, RWKV_WIDTH), ICLR_LORA ** -0.5),
        "gate_g2": nrm((L, GATE_LORA, RWKV_WIDTH), GATE_LORA ** -0.5),
        "k_k": 1.0 + nrm((L, RWKV_WIDTH), 0.1),
        "k_a": 1.0 + nrm((L, RWKV_WIDTH), 0.1),
        "r_k": nrm((L, RWKV_HEADS, RWKV_HEAD_DIM), 0.1),
        "lnx_w": 1.0 + nrm((L, RWKV_WIDTH), 0.05),
        "lnx_b": nrm((L, RWKV_WIDTH), 0.02),
        "w_branch_attn": nrm((L, ATTN_WIDTH, D), ATTN_WIDTH ** -0.5),
        "w_branch_rwkv": nrm((L, RWKV_WIDTH, D), RWKV_WIDTH ** -0.5),
        "w_out": nrm((L, D, D), D ** -0.5),
        "norm2_g": 1.0 + nrm((L, D), 0.05),
        "w_mlp_up": nrm((L, D, D_FF), D ** -0.5),
        "w_mlp_down": nrm((L, D_FF, D), D_FF ** -0.5),
        "norm_f_g": 1.0 + nrm((D,), 0.05),
    }


def reference(x, c, ctx, c_ctx, w_ada, b_ada, norm1_g, w_in, sink, conv_w, decay_w0, decay_w2,
              iclr_a0, iclr_a2, gate_g2, k_k, k_a, r_k, lnx_w, lnx_b, w_branch_attn, w_branch_rwkv,
              w_out, norm2_g, w_mlp_up, w_mlp_down, norm_f_g):
    rows = x.shape[1] // GRID_W
    cos, sin = axial_rope_tables(rows)
    xc = ctx
    for l in range(DEPTH):
        x, xc = trunk_layer(x, xc, c, c_ctx, cos, sin, w_ada[l], b_ada[l], norm1_g[l], w_in[l], sink[l],
                            conv_w[l], decay_w0[l], decay_w2[l], iclr_a0[l], iclr_a2[l], gate_g2[l],
                            k_k[l], k_a[l], r_k[l], lnx_w[l], lnx_b[l], w_branch_attn[l], w_branch_rwkv[l],
                            w_out[l], norm2_g[l], w_mlp_up[l], w_mlp_down[l], update_ctx=(l < DEPTH - 1))
    return rmsnorm(x, norm_f_g)
```

```python
import contextlib
import numpy as np
import concourse.bass as bass
import concourse.mybir as mybir
from concourse.bass_utils import run_bass_kernel_spmd

F32 = mybir.dt.float32
BF16 = mybir.dt.bfloat16
U8 = mybir.dt.uint8
AF = mybir.ActivationFunctionType
ALU = mybir.AluOpType
AX = mybir.AxisListType
KB = 1024
NDMASEM = 16
EMBED_WAIT = True

D = 1024
NTOK = 2432
CTX0, MAIN0, HALO0 = 0, 256, 2304
NMAIN = 2048
GT = 256
NORM_EPS = 1e-6
LNX_EPS = 1e-5 * 64

PV_BADA, PV_G1, PV_G2, PV_W0, PV_A0, PV_KK, PV_KA, PV_RK, PV_LW, PV_LB, PV_CW, PV_FSEL, PV_N = \
    0, 48, 56, 64, 72, 80, 84, 88, 92, 96, 100, 142, 144
CM_BPREV, CM_BNEXT, CM_MA1, CM_MB1, CM_MA2, CM_MB2, CM_IDB, CM_ONESB, CM_BF_N = 0, 512, 1024, 1408, 1792, 2048, 2304, 2432, 2560
CF_RESET, CF_ID, CF_BONES, CF_IDZ, CF_ONES, CF_N = 0, 256, 384, 512, 640, 768


class Prog:
    ENGS = ("pe", "act", "dve", "pool", "sp")

    def __init__(self, nc, same_engine_sync=True):
        self.nc = nc
        self.ops = []
        self.same_engine_sync = same_engine_sync

    def add(self, eng, emit, r=(), w=(), dma=False):
        self.ops.append(dict(eng=eng, emit=emit, r=tuple(r), w=tuple(w), dma=dma, bar=False))
        return len(self.ops) - 1

    def cap_begin(self):
        if not hasattr(self, "_cap_stack"):
            self._cap_stack = []
        self._cap_stack.append(self.ops)
        self.ops = []

    def cap_end(self):
        cap = self.ops
        self.ops = self._cap_stack.pop()
        return cap

    @staticmethod
    def merge(a, b):
        out = []
        ia = ib = 0
        na, nb = len(a), len(b)
        while ia < na or ib < nb:
            if ib >= nb or (ia < na and ia * nb <= ib * na):
                out.append(a[ia]); ia += 1
            else:
                out.append(b[ib]); ib += 1
        return out

    def barrier(self):
        for e in self.ENGS:
            self.ops.append(dict(eng=e, emit=None, r=(), w=(), dma=False, bar=True))

    def mm(self, out, lhsT, rhs, start=True, stop=True, tp=None, r=(), w=(), sg=False):
        def e(pe):
            kw = {}
            if tp is not None:
                kw["tile_position"] = tp
            if sg:
                kw["skip_group_check"] = True
            return pe.matmul(out, lhsT=lhsT, rhs=rhs, start=start, stop=stop, **kw)
        return self.add("pe", e, r, w)

    def tr(self, out, in_, ident, r=(), w=()):
        return self.add("pe", lambda pe: pe.transpose(out, in_, ident), r, w)

    def act(self, out, in_, func, bias=None, scale=None, accum_out=None, r=(), w=()):
        def e(a):
            kw = {}
            if bias is not None:
                kw["bias"] = bias
            if scale is not None:
                kw["scale"] = scale
            if accum_out is not None:
                kw["accum_out"] = accum_out
            return a.activation(out=out, in_=in_, func=func, **kw)
        return self.add("act", e, r, w)

    def tt(self, eng, out, in0, in1, op, r=(), w=()):
        return self.add(eng, lambda v: v.tensor_tensor(out=out, in0=in0, in1=in1, op=op), r, w)

    def ts(self, eng, out, in0, s1, s2, op0, op1=None, r=(), w=()):
        def e(v):
            if op1 is None:
                return v.tensor_scalar(out=out, in0=in0, scalar1=s1, scalar2=None, op0=op0)
            return v.tensor_scalar(out=out, in0=in0, scalar1=s1, scalar2=s2, op0=op0, op1=op1)
        return self.add(eng, e, r, w)

    def stt(self, out, in0, scalar, in1, op0, op1, r=(), w=()):
        return self.add("dve", lambda v: v.scalar_tensor_tensor(out=out, in0=in0, scalar=scalar, in1=in1,
                                                               op0=op0, op1=op1), r, w)

    def cp(self, eng, out, in_, r=(), w=()):
        if eng == "act":
            return self.add(eng, lambda a: a.activation(out=out, in_=in_, func=AF.Copy), r, w)
        return self.add(eng, lambda v: v.tensor_copy(out=out, in_=in_), r, w)

    def memset(self, eng, ap, val, r=(), w=()):
        return self.add(eng, lambda v: v.memset(ap, val), r, w)

    def dma(self, eng, out, in_, r=(), w=()):
        return self.add(eng, lambda q: q.dma_start(out=out, in_=in_), r, w, dma=True)

    def emit(self, final_keys=()):
        nc = self.nc
        ops = self.ops
        n = len(ops)
        last_w, readers = {}, {}
        deps = [None] * n
        last_compute = {}
        pending_dma = []
        for i, op in enumerate(ops):
            if op["bar"]:
                dd = [j for e2, j in last_compute.items()
                      if (e2 != op["eng"] or (self.same_engine_sync and e2 != "pe"))]
                dd += pending_dma
                deps[i] = dd
                if op["eng"] == self.ENGS[-1]:
                    pending_dma = []
                continue
            d = set()
            for k in op["r"]:
                if k in last_w:
                    d.add(last_w[k])
            for k in op["w"]:
                if k in last_w:
                    d.add(last_w[k])
                for j in readers.get(k, {}).values():
                    d.add(j)
            d.discard(i)
            dd = []
            for j in d:
                oj = ops[j]
                if (not oj["dma"]) and (not op["dma"]) and oj["eng"] == op["eng"]:
                    if op["eng"] == "pe" or not self.same_engine_sync:
                        continue
                dd.append(j)
            deps[i] = dd
            rk_ = ("dma", i) if op["dma"] else op["eng"]
            for k in op["r"]:
                readers.setdefault(k, {})[rk_] = i
            for k in op["w"]:
                last_w[k] = i
                readers[k] = {}
            if op["dma"]:
                pending_dma.append(i)
            else:
                last_compute[op["eng"]] = i
        signaled = set()
        for i in range(n):
            for j in deps[i]:
                signaled.add(j)
        final_waits = []
        for k in final_keys:
            if k in last_w:
                signaled.add(last_w[k])
                final_waits.append(last_w[k])
        stack = contextlib.ExitStack()
        esem = {e: stack.enter_context(nc.semaphore("sem_" + e)) for e in ("pe", "act", "dve", "pool")}
        dsem = {e: [stack.enter_context(nc.semaphore("dsem_%s%d" % (e, k))) for k in range(NDMASEM)]
                for e in ("sp", "pool", "act")}
        ecount = {e: 0 for e in esem}
        sig, reuse_wait = {}, {}
        dma_n = {"sp": 0, "pool": 0, "act": 0}
        dma_i = 0
        for i, op in enumerate(ops):
            if op["bar"]:
                continue
            if op["dma"]:
                qe = op["eng"]
                di = dma_n[qe]
                k = di % NDMASEM
                val = 16 * (di // NDMASEM + 1)
                if di >= NDMASEM:
                    reuse_wait[i] = (dsem[qe][k], val - 16)
                sig[i] = (dsem[qe][k], val, 16)
                dma_n[qe] += 1
                dma_i += 1
            elif i in signaled:
                ecount[op["eng"]] += 1
                sig[i] = (esem[op["eng"]], ecount[op["eng"]], 1)
        self.stats = dict(nops=n, nsig=dict(ecount), ndma=dma_i)
        per_eng = {e: [] for e in self.ENGS}
        for i, op in enumerate(ops):
            per_eng[op["eng"]].append(i)

        def run_engine(ename, h):
            waited = {}

            def wait(sem, val):
                key = id(sem)
                if waited.get(key, 0) >= val:
                    return
                h.wait_ge(sem, val)
                waited[key] = val
            for i in per_eng[ename]:
                op = ops[i]
                need = {}
                for j in deps[i]:
                    s = sig[j]
                    if waited.get(id(s[0]), 0) < s[1] and need.get(id(s[0]), (None, 0))[1] < s[1]:
                        need[id(s[0])] = (s[0], s[1])
                if i in reuse_wait:
                    s = reuse_wait[i]
                    if waited.get(id(s[0]), 0) < s[1] and need.get(id(s[0]), (None, 0))[1] < s[1]:
                        need[id(s[0])] = (s[0], s[1])
                need = list(need.values())
                embed = None
                if need and EMBED_WAIT and not op["bar"] and not op["dma"]:
                    embed = need.pop()
                for s in need:
                    wait(s[0], s[1])
                if op["bar"]:
                    continue
                ins = op["emit"](h)
                if embed is not None:
                    ins._wait_ge(embed[0], embed[1])
                    waited[id(embed[0])] = embed[1]
                if i in sig:
                    ins.then_inc(sig[i][0], sig[i][2])
            if ename == "sp":
                for j in final_waits:
                    s = sig[j]
                    wait(s[0], s[1])

        with nc.Block() as block:
            @block.sync
            def _(h):
                run_engine("sp", h)

            @block.scalar
            def _(h):
                run_engine("act", h)

            @block.vector
            def _(h):
                run_engine("dve", h)

            @block.gpsimd
            def _(h):
                run_engine("pool", h)

            @block.tensor
            def _(h):
                run_engine("pe", h)
        stack.close()


PHASES = ("H", "RW", "ATT", "B", "MERGE", "MLP")


def build_program(stop_after="MLP", dbg=None, ngroups=None, no_cc=False):
    nc = bass.Bass("TRN2", target_bir_lowering=False)
    es = contextlib.ExitStack()

    def din(name, shape, dt=F32):
        return nc.dram_tensor(name, list(shape), dt, kind="ExternalInput").ap()

    xin = din("xin", [NTOK, D])
    cc_d = din("cc", [128, 16])
    w_ada = din("w_ada", [D, 6144])
    pv_d = din("pv", [128, PV_N])
    sinkb_d = din("sinkb", [128, 512])
    nfbc_d = din("nfbc", [128, D])
    w_att_d = din("w_att", [D, 1664])
    w_rw_d = din("w_rw", [D, 1792])
    loraw_d = din("loraw", [64, 2048])
    g2_d = din("g2", [96, 512])
    w_g_d = din("w_g", [D, 2048])
    w_ba_d = din("w_ba", [512, D])
    w_br_d = din("w_br", [512, D])
    w_out_d = din("w_out", [D, D])
    w_up_d = din("w_up", [D, 4096])
    w_down_d = din("w_down", [4096, D])
    cos_d = din("cosT", [128, 2176])
    sin_d = din("sinT", [128, 2176])
    cmb_d = din("cmb", [128, CM_BF_N])
    cmf_d = din("cmf", [128, CF_N])
    out_d = nc.dram_tensor("out", [NMAIN, D], F32, kind="ExternalOutput").ap()
    bops_d = nc.dram_tensor("bops", [64, 128, 384], BF16, kind="Internal").ap()
    cin_d = nc.dram_tensor("cin", [128, 256], F32, kind="Internal").ap()
    cout_d = nc.dram_tensor("cout", [256, 256], F32, kind="Internal").ap()
    dbg_out = {}

    ARENA = 207 * KB
    arena = es.enter_context(nc.sbuf_tensor("arena", [128, ARENA], U8))
    psb = [es.enter_context(nc.psum_tensor("psb%d" % i, [128, 512], F32))[:, :] for i in range(8)]

    def carve(off, shape, dt):
        esz = 4 if dt == F32 else 2
        n = 1
        for s in shape[1:]:
            n *= s
        assert off % 4 == 0 and off + n * esz <= ARENA, (off, shape)
        ap = arena[0:shape[0], off:off + n * esz].bitcast(dt)
        if len(shape) == 3:
            ap = ap.rearrange("p (a b) -> p a b", b=shape[2])
        elif len(shape) == 4:
            ap = ap.rearrange("p (a b c) -> p a b c", b=shape[2], c=shape[3])
        return ap

    class Bump:
        def __init__(self, lo, hi):
            self.lo, self.hi, self.cur = lo, hi, lo

        def __call__(self, shape, dt):
            esz = 4 if dt == F32 else 2
            n = 1
            for s in shape[1:]:
                n *= s
            nb = (n * esz + 31) // 32 * 32
            assert self.cur + nb <= self.hi, ("SBUF region overflow", self.cur, nb, self.hi)
            ap = carve(self.cur, shape, dt)
            self.cur += nb
            return ap

    p = Prog(nc)
    marks = {}

    def finish():
        import os as _os
        mo = _os.environ.get("MAXOPS")
        if mo:
            print("phase marks", marks, "total", len(p.ops))
            del p.ops[int(mo):]
        keys = [("dbgout", n) for n in dbg_out] + [("out", i) for i in range(16)]
        p.emit(final_keys=keys)
        es.close()
        return nc, dbg_out, p.stats

    def dump(name, ap, shape, dt=F32):
        if dbg is None or name not in dbg:
            return
        t = nc.dram_tensor("dbg_" + name, list(shape), dt, kind="ExternalOutput").ap()
        dbg_out[name] = t
        p.barrier()
        p.dma("sp", t, ap, w=[("dbgout", name)])
        p.barrier()

    def v3(ap, b):
        return ap.rearrange("p (a b) -> p a b", b=b)

    PB = Bump(0, 8 * KB)
    pv = PB([128, PV_N], F32)
    modT = PB([128, 48, 2], F32)
    sc1 = PB([128, 8, 2], F32)
    sc2 = PB([128, 8, 2], F32)
    scT = PB([128, 8, 2], F32)
    cmf = PB([128, CF_N], F32)
    ident_b = PB([128, 128], BF16)
    ones_b = PB([128, 128], BF16)
    sstat = [PB([128, 8], F32) for _ in range(6)]
    rl = [PB([128, 512], BF16) for _ in range(2)]
    cm_reset = cmf[:, CF_RESET:CF_RESET + 256]
    ident_f = cmf[:, CF_ID:CF_ID + 128]
    bones_f = cmf[:, CF_BONES:CF_BONES + 128]
    idz_f = cmf[:, CF_IDZ:CF_IDZ + 128]
    ones_f = cmf[:, CF_ONES:CF_ONES + 128]
    MB_ = Bump(8 * KB, 24 * KB)
    cmb = MB_([128, 2304], BF16)
    band_prev = cmb[:, CM_BPREV:CM_BPREV + 512]
    band_next = cmb[:, CM_BNEXT:CM_BNEXT + 512]
    MASK1 = [cmb[:, CM_MA1:CM_MA1 + 384], cmb[:, CM_MB1:CM_MB1 + 384]]
    MASK2 = [cmb[:, CM_MA2:CM_MA2 + 256], cmb[:, CM_MB2:CM_MB2 + 256]]
    Sst = [MB_([128, 4, 2, 64], BF16) for _ in range(2)]
    tmpS = MB_([128, 4, 64], F32)
    Sf32 = MB_([128, 256], F32)
    cslots = MB_([128, 2, 256], F32)
    hT = carve(24 * KB, [128, 8, NTOK], BF16)
    yaT = carve(63 * KB, [128, 4, NMAIN], BF16)
    yrT = carve(79 * KB, [128, 4, NMAIN], BF16)
    yacc = carve(95 * KB, [128, 16, 512], BF16)
    bonusT = carve(111 * KB, [128, 4, NMAIN], BF16)
    sgT = carve(127 * KB, [128, NMAIN], BF16)

    p.dma("sp", pv, pv_d, w=["pv"])
    p.dma("sp", cmf, cmf_d, w=["cmf"])
    p.dma("pool", cmb, cmb_d[:, 0:2304], w=["cmb"])
    p.dma("pool", ident_b, cmb_d[:, CM_IDB:CM_IDB + 128], w=["identb"])
    p.dma("pool", ones_b, cmb_d[:, CM_ONESB:CM_ONESB + 128], w=["onesb"])

    L = Bump(131 * KB, 207 * KB)
    w_rw_early = L([128, 8, 1792], BF16)
    w_rw_v = w_rw_d.rearrange("(kc p) n -> p kc n", p=128)
    for q4 in range(4):
        p.dma("pool", w_rw_early[:, :, q4 * 448:(q4 + 1) * 448], w_rw_v[:, :, q4 * 448:(q4 + 1) * 448], w=[("w_rw", q4)])
    ccs = L([128, 8, 2], F32)
    wst = [L([128, 8, 512], F32) for _ in range(2)]
    xst = [L([128, D], F32) for _ in range(2)]
    xnb = [L([128, D], BF16) for _ in range(2)]
    junk = L([128, D], BF16)
    p.dma("sp", ccs, cc_d.rearrange("p (a b) -> p a b", b=2), w=["ccs"])
    p.act(scT, ccs, AF.Silu, r=["ccs"], w=["scT"])
    w_ada_v = w_ada.rearrange("(kc p) n -> p kc n", p=128)
    ps_ada_lo = v3(psb[7][:, 0:32], 2)
    ps_ada_hi = v3(psb[6][:, 0:64], 2)

    def ada_ps(jj):
        return (ps_ada_lo[:, jj, :], ("ps", 7)) if jj < 16 else (ps_ada_hi[:, jj - 16, :], ("ps", 6))

    def ada_finish(j0, j1):
        src = ps_ada_lo[:, 0:16, :] if j0 == 0 else ps_ada_hi[:, 0:32, :]
        p.tt("dve", modT[:, j0:j1, :], src,
             pv[:, PV_BADA + j0:PV_BADA + j1].unsqueeze(2).to_broadcast([128, j1 - j0, 2]), ALU.add,
             r=[("ps", 7 if j0 == 0 else 6), "pv"], w=[("modT", j0)])

    def ada_block(jg):
        st_ = wst[jg % 2]
        p.dma("sp", st_, w_ada_v[:, :, jg * 512:(jg + 1) * 512], w=[("wst", jg % 2)])
        for j in range(4):
            jj = jg * 4 + j
            for kc in range(8):
                apo, apk = ada_ps(jj)
                p.mm(apo, st_[:, kc, j * 128:(j + 1) * 128], scT[:, kc, :],
                     start=(kc == 0), stop=(kc == 7), r=[("wst", jg % 2), "scT"], w=[apk])

    for jg in range(4):
        ada_block(jg)
    ada_finish(0, 16)
    p.stt(sc1, modT[:, 8:16, :], 1.0, pv[:, PV_G1:PV_G1 + 8].unsqueeze(2).to_broadcast([128, 8, 2]),
          ALU.add, ALU.mult, r=[("modT", 0), "pv"], w=["sc1"])

    def ada_tail():
        ada_finish(16, 48)
        p.stt(sc2, modT[:, 32:40, :], 1.0, pv[:, PV_G2:PV_G2 + 8].unsqueeze(2).to_broadcast([128, 8, 2]),
              ALU.add, ALU.mult, r=[("modT", 16), "pv"], w=["sc2"])
    MODK = ["sc1", "sc2", ("modT", 0), ("modT", 16)]

    def rms_rstd(src, ssq, eps, scale, keys_r, key_w, junk_ap, junk_key):
        p.act(junk_ap, src, AF.Square, accum_out=ssq, r=keys_r, w=[key_w, junk_key])
        p.ts("dve", ssq, ssq, scale, eps, ALU.mult, ALU.add, r=[key_w], w=[key_w])
        p.act(ssq, ssq, AF.Sqrt, r=[key_w], w=[key_w])
        p.add("dve", lambda v: v.reciprocal(out=ssq, in_=ssq), r=[key_w], w=[key_w])

    def transpose_modulate(xn_ap, xn_key, dst_fn, sc, sh, col, bank, dst_keys):
        pst = v3(psb[bank][:].bitcast(BF16), 128)
        for kc in range(8):
            p.tr(pst[:, kc, :], xn_ap[:, kc * 128:(kc + 1) * 128], ident_b, r=[xn_key, "identb"], w=[("ps", bank)])
        for kc in range(8):
            if kc % 2 == 0:
                p.act(dst_fn(kc), pst[:, kc, :], AF.Identity, bias=sh[:, kc, col:col + 1],
                      scale=sc[:, kc, col:col + 1], r=[("ps", bank)] + MODK, w=dst_keys)
            else:
                p.ts("dve", dst_fn(kc), pst[:, kc, :], sc[:, kc, col:col + 1], sh[:, kc, col:col + 1],
                     ALU.mult, ALU.add, r=[("ps", bank)] + MODK, w=dst_keys)

    for tt_ in range(19):
        xs = xst[tt_ % 2]
        xk = ("xst", tt_ % 2)
        ssq = sstat[tt_ % 3][:, 0:1]
        sk = ("sst", tt_ % 3)
        p.dma("sp", xs, xin[tt_ * 128:(tt_ + 1) * 128, :], w=[xk])
        rms_rstd(xs, ssq, NORM_EPS, 1.0 / D, [xk], sk, junk, "junk")
        xn = xnb[tt_ % 2]
        nk = ("xnb", tt_ % 2)
        p.ts("dve", xn, xs, ssq, None, ALU.mult, r=[xk, sk], w=[nk])
        col = 1 if tt_ < 2 else 0
        transpose_modulate(xn, nk, lambda kc, t=tt_: hT[:, kc, t * 128:(t + 1) * 128], sc1, modT[:, 0:8, :], col,
                           tt_ % 2, [("hT", tt_)])
        if tt_ % 2 == 1 and 4 + tt_ // 2 < 12:
            ada_block(4 + tt_ // 2)
    ada_tail()
    HT_ALL = [("hT", t) for t in range(19)]
    dump("hT", hT, [128, 8, NTOK], BF16)
    dump("modT", modT, [128, 48, 2], F32)
    if stop_after == "H":
        return finish()
    p.barrier()

    marks["RW"] = len(p.ops)
    L = Bump(131 * KB, 207 * KB)
    L2 = Bump(63 * KB, 95 * KB)
    w_rw = L([128, 8, 1792], BF16)
    loraw = L([64, 2048], BF16)
    ur = [L([128, 260], F32) for _ in range(2)]
    Cb = [[L([128, GT], F32) for _ in range(3)] for _ in range(2)]
    lo_t = L([128, GT], F32)
    hg_t = L([128, GT], F32)
    loraT = L([64, GT], BF16)
    T = [L([128, GT], F32) for _ in range(11)]
    BH = [L([128, GT], BF16) for _ in range(2)]
    KH = [L([128, GT], BF16) for _ in range(2)]
    vb = L([128, GT], BF16)
    WC = [[L([128, 4], F32) for _ in range(2)] for _ in range(2)]
    FM = [[[L2([128, 4, 128], BF16) for _ in range(2)] for _ in range(2)] for _ in range(2)]
    TMt = [[L([128, 7, 128], BF16) for _ in range(2)] for _ in range(2)]
    Wk = [[L2([128, 384], BF16) for _ in range(2)] for _ in range(4)]
    TX = [L2([128, GT], F32) for _ in range(2)]
    RbT = [L2([128, 128], BF16) for _ in range(4)]
    A2 = [L2([128, 256], BF16) for _ in range(4)]
    GU = [L2([128, 128], BF16) for _ in range(4)]
    BHBD = [L2([128, 128], BF16) for _ in range(4)]
    VBD = [L2([128, 128], BF16) for _ in range(4)]
    U0BD = [L2([128, 128], BF16) for _ in range(4)]
    bdm = v3(bones_f, 64)
    PPA = [L([128, 2, 128], BF16) for _ in range(2)]
    QTA = [L([128, 128], BF16) for _ in range(2)]
    BO = [L([128, 384], BF16) for _ in range(4)]

    WRW = [("w_rw", q4) for q4 in range(4)]
    p.dma("pool", loraw, loraw_d, w=["loraw"])
    p.memset("pool", Sst[0], 0.0, w=[("S", 0, j) for j in range(4)])
    sp_par = [0, 0, 0, 0]

    groups = [("ctx", 0)] + [("main", MAIN0 + GT * i) for i in range(NMAIN // GT)]
    if ngroups is not None:
        groups = groups[:ngroups]
    ucnt = [0]
    unit_cnt = [0]
    ycnt = [0]
    bocnt = [0]
    pmcnt = [0]

    def pm_region():
        k = pmcnt[0] % 2
        pmcnt[0] += 1
        return psb[1][:, k * 256:(k + 1) * 256], ("ps", 1)


    def unit_gen(st_, buf, j, pr, half, dpos, is_main, ppa, ppak, qta, qtak, bo, bok):
        hp = slice(half * 64, (half + 1) * 64)
        hc = hp
        ub, ubk = psb[4 + st_], ("ps", 4 + st_)
        fm = FM[buf][dpos][pr]
        fk = ("FM", buf, dpos, pr)
        tm = TMt[buf][pr]
        tmk = ("TM", buf, pr)
        R_, KKn, Bt, Kt, RK = fm[hp, 0, :], fm[hp, 1, :], fm[hp, 2, :], fm[hp, 3, :], fm[hp, 0:2, :]
        tpk = (half * 64, 0)
        rbt, rbtk = RbT[st_], ("RbT", st_)
        a2, a2k = A2[st_], ("A2", st_)
        gu, guk = GU[st_], ("GU", st_)
        wk0, wk1 = Wk[st_]
        wkk = [("Wk", st_, 0), ("Wk", st_, 1)]
        M1, M2 = MASK1[dpos], MASK2[dpos]
        p14 = ub[:, 0:384]
        p.mm(p14[:, 0:128], KKn, Bt, tp=tpk, r=[fk], w=[ubk])
        p.mm(p14[:, 128:384], Bt, RK, tp=tpk, r=[fk], w=[ubk])
        yield
        p.tt("dve", v3(wk0, 128)[:, 0:3:2, :], v3(p14, 128)[:, 0:3:2, :], v3(M1, 128)[:, 0:3:2, :], ALU.mult,
             r=[ubk, "cmb"], w=[wkk[0]])
        p.tt("dve", rbt, p14[:, 128:256], M1[:, 128:256], ALU.mult, r=[ubk, "cmb"], w=[rbtk])
        p25 = ub[:, 0:256]
        p.mm(p25, Kt, RK, tp=tpk, r=[fk], w=[ubk])
        p.tt("pool", v3(BHBD[st_], 64), tm[:, dpos * 3 + 1, hc].unsqueeze(1).to_broadcast([128, 2, 64]), bdm, ALU.mult,
             r=[tmk, "cmf"], w=[("BHBD", st_)])
        p.tt("pool", v3(VBD[st_], 64), tm[:, 6, hc].unsqueeze(1).to_broadcast([128, 2, 64]), bdm, ALU.mult,
             r=[tmk, "cmf"], w=[("VBD", st_)])
        yield
        p.tt("dve", a2, p25, M2, ALU.mult, r=[ubk, "cmb"], w=[a2k])
        P3 = ub[:, 256:320]
        p.mm(P3, a2[:, 128:256], tm[:, 6, hc], r=[a2k, tmk], w=[ubk])
        p.cp("act", wk0[:, 128:192], tm[:, dpos * 3, hc], r=[tmk], w=[wkk[0]])
        yield
        p.cp("act", wk0[:, 192:256], P3, r=[ubk], w=[wkk[0]])
        wks = [wk0, wk1]
        for lv in range(5):
            cur, nxt = wks[lv % 2], wks[(lv + 1) % 2]
            ck_, nk_ = wkk[lv % 2], wkk[(lv + 1) % 2]
            p.mm(p14[:, 0:256], cur[:, 256:384], cur[:, 0:256], r=[ck_], w=[ubk])
            p.mm(p14[:, 256:384], cur[:, 0:128], cur[:, 256:384], r=[ck_], w=[ubk])
            yield
            p.cp("act", v3(nxt, 128)[:, 0:3:2, :], v3(p14, 128)[:, 0:3:2, :], r=[ubk], w=[nk_])
            p.tt("dve", nxt[:, 128:256], p14[:, 128:256], cur[:, 128:256], ALU.add, r=[ubk, ck_], w=[nk_])
        cur, ck_ = wks[1], wkk[1]
        p.mm(p14[:, 128:256], cur[:, 256:384], cur[:, 128:256], r=[ck_], w=[ubk])
        yield
        p.tt("dve", gu, p14[:, 128:256], cur[:, 128:256], ALU.add, r=[ubk, ck_], w=[guk])
        bh, vb_, ub_ = BHBD[st_], VBD[st_], U0BD[st_]
        bhk, vbk, ubk_ = ("BHBD", st_), ("VBD", st_), ("U0BD", st_)
        p.tt("pool", v3(ub_, 64), gu[:, 64:128].unsqueeze(1).to_broadcast([128, 2, 64]), bdm, ALU.mult,
             r=[guk, "cmf"], w=[ubk_])
        tp2 = (0, half * 64)
        p.mm(p25[hp, 0:128], gu[:, 0:64], bh, tp=tp2, r=[guk, bhk], w=[ubk])
        p.mm(p25[hp, 128:256], tm[:, dpos * 3 + 1, hc], ub_, start=True, stop=False, tp=tp2, r=[tmk, ubk_], w=[ubk])
        p.mm(p25[hp, 128:256], tm[:, dpos * 3 + 2, hc], vb_, start=False, stop=True, tp=tp2, r=[tmk, vbk], w=[ubk])
        P6 = ub[:, 256:384]
        if is_main:
            p.mm(P6[hp, :], gu[:, 0:64], rbt, tp=(0, half * 64), r=[guk, rbtk], w=[ubk])
        yield
        for c2 in range(2):
            ci = pr * 2 + c2
            if dpos == 0:
                dst, dk = ppa[hp, c2, 0:64], ppak
            else:
                dst, dk = bo[hp, c2 * 128:c2 * 128 + 64], bok
            p.stt(dst, idz_f[hp, 0:64], WC[buf][dpos][hp, ci:ci + 1], p25[hp, c2 * 64:(c2 + 1) * 64], ALU.mult, ALU.add,
                  r=["cmf", ("WC", buf, dpos), ubk], w=[dk])
        if dpos == 0:
            dst, dk = ppa[hp, :, 64:128], ppak
        else:
            dst, dk = v3(bo[hp, 0:256], 128)[:, :, 64:128], bok
        p.cp("act", dst, v3(p25[hp, 128:256], 64), r=[ubk], w=[dk])
        if is_main:
            if dpos == 0:
                dst, dk = qta[hp, :], qtak
            else:
                dst, dk = bo[hp, 256:384], bok
            p.tt("dve", dst, P6[hp, :], R_, ALU.add, r=[ubk, fk], w=[dk])

    pbcnt = [0]
    LD = 0.6065306597126334
    stage_seq = []
    carry_post = [None]
    for gi, (kind, t0) in enumerate(groups):
        n = GT
        is_main = kind == "main"
        left_ok = is_main and t0 > MAIN0
        right_ok = is_main
        m0 = t0 - MAIN0
        dirs = (0, 1) if is_main else (0,)
        hkeys = [("hT", t0 // 128 + i) for i in range(3) if t0 // 128 + i < 19]
        if left_ok:
            hkeys.append(("hT", t0 // 128 - 1))

        def project_conv(cidx, dst, dst_key):
            k = ucnt[0] % 2
            ucnt[0] += 1
            lo_ = 1 if left_ok else 0
            ro_ = 1 if right_ok else 0
            ncol = n + lo_ + ro_
            psu = psb[k][:, 0:ncol]
            puk = ("ps", k)
            u = ur[k]
            uk = ("ur", k)
            for kc in range(8):
                p.mm(psu, w_rw[:, kc, cidx * 128:(cidx + 1) * 128], hT[:, kc, t0 - lo_:t0 + n + ro_], start=(kc == 0),
                     stop=(kc == 7), r=WRW + hkeys, w=[puk])
            p.cp("act", u[:, 1 - lo_:1 - lo_ + ncol], psu, r=[puk], w=[uk])
            if not left_ok:
                p.memset("pool", u[:, 0:1], 0.0, w=[uk])
            if not right_ok:
                p.memset("pool", u[:, n + 1:n + 2], 0.0, w=[uk])
            cw = PV_CW + cidx * 3
            p.act(dst, u[:, 1:n + 1], AF.Identity, scale=pv[:, cw + 1:cw + 2], r=[uk, "pv"], w=[dst_key])
            p.stt(dst, u[:, 0:n], pv[:, cw:cw + 1], dst, ALU.mult, ALU.add, r=[uk, "pv", dst_key], w=[dst_key])
            p.stt(dst, u[:, 2:n + 2], pv[:, cw + 2:cw + 3], dst, ALU.mult, ALU.add, r=[uk, "pv", dst_key], w=[dst_key])

        p.cap_begin()
        project_conv(12, lo_t, "lo")
        p.act(loraT[0:32, :], lo_t[0:32, :], AF.Tanh, r=["lo"], w=["loraT"])
        p.cp("dve", loraT[32:64, :], lo_t[32:64, :], r=["lo"], w=["loraT"])
        if is_main:
            project_conv(13, hg_t, "hg")
            p.act(sgT[0:96, m0:m0 + n], hg_t[0:96, :], AF.Sigmoid, r=["hg"], w=[("sgT", gi)])

        for j in range(4):
            if j > 0:
                p.cap_begin()
            buf = (gi * 4 + j) % 2
            r_, k_, v_ = Cb[buf]
            ck = [("C", buf, i) for i in range(3)]
            for i3 in range(3):
                project_conv(3 * j + i3, Cb[buf][i3], ck[i3])
            tk = [("T", i) for i in range(11)]
            p.cp("act", vb, v_, r=[ck[2]], w=["vb"])
            SW, SWk = [T[1], TX[0]], [tk[1], ("TX", 0)]
            AA, AAk = [T[2], TX[1]], [tk[2], ("TX", 1)]
            for dpos in dirs:
                pmw, pmwk = pm_region()
                p.mm(pmw, loraw[0:64, (dpos * 2) * 512 + j * 128:(dpos * 2) * 512 + (j + 1) * 128], loraT[0:64, :],
                     r=["loraw", "loraT"], w=[pmwk])
                p.act(SW[dpos], pmw, AF.Sigmoid, bias=pv[:, PV_W0 + j * 2 + dpos:PV_W0 + j * 2 + dpos + 1],
                      r=[pmwk, "pv"], w=[SWk[dpos]])
                pma, pmak = pm_region()
                p.mm(pma, loraw[0:64, (dpos * 2 + 1) * 512 + j * 128:(dpos * 2 + 1) * 512 + (j + 1) * 128], loraT[0:64, :],
                     r=["loraw", "loraT"], w=[pmak])
                p.act(AA[dpos], pma, AF.Sigmoid, bias=pv[:, PV_A0 + j * 2 + dpos:PV_A0 + j * 2 + dpos + 1],
                      r=[pmak, "pv"], w=[AAk[dpos]])
            p.act(T[0], k_, AF.Identity, scale=pv[:, PV_KK + j:PV_KK + j + 1], r=[ck[1], "pv"], w=[tk[0]])
            p.act(T[4], T[0], AF.Square, r=[tk[0]], w=[tk[4]])
            pm, pmk = pm_region()
            p.mm(pm, bones_f, T[4], r=["cmf", tk[4]], w=[pmk])
            p.ts("dve", T[4], pm, 1e-24, None, ALU.max, r=[pmk], w=[tk[4]])
            p.act(T[4], T[4], AF.Ln, r=[tk[4]], w=[tk[4]])
            p.act(T[4], T[4], AF.Exp, scale=-0.5, r=[tk[4]], w=[tk[4]])
            p.tt("pool", T[0], T[0], T[4], ALU.mult, r=[tk[0], tk[4]], w=[tk[0]])
            for dpos in dirs:
                SWt, SWk_, AAt, AAk_ = SW[dpos], SWk[dpos], AA[dpos], AAk[dpos]
                p.add("dve", lambda v, o=T[3], a=cm_reset, b=SWt: v.tensor_tensor_scan(
                    out=o, data0=a, data1=b, initial=0.0, op0=ALU.mult, op1=ALU.add), r=["cmf", SWk_], w=[tk[3]])
                c3 = v3(T[3], 64)
                tot_bc = c3[:, :, 63:64].to_broadcast([128, 4, 64])
                if dpos == 0:
                    clu, cluk = T[3], tk[3]
                    p.act(T[5], T[3], AF.Exp, scale=-LD, r=[tk[3]], w=[tk[5]])
                    p.tt("pool", T[6], T[3], SWt, ALU.subtract, r=[tk[3], SWk_], w=[tk[6]])
                    p.act(T[6], T[6], AF.Exp, scale=-LD, r=[tk[6]], w=[tk[6]])
                    p.act(T[7], T[3], AF.Exp, scale=LD, r=[tk[3]], w=[tk[7]])
                    p.tt("pool", v3(T[8], 64), tot_bc, c3, ALU.subtract, r=[tk[3]], w=[tk[8]])
                    p.act(T[8], T[8], AF.Exp, scale=-LD, r=[tk[8]], w=[tk[8]])
                else:
                    p.tt("pool", T[4], SWt, T[3], ALU.subtract, r=[tk[3], SWk_], w=[tk[4]])
                    p.tt("pool", v3(T[4], 64), v3(T[4], 64), tot_bc, ALU.add, r=[tk[3], tk[4]], w=[tk[4]])
                    p.act(T[5], T[4], AF.Exp, scale=-LD, r=[tk[4]], w=[tk[5]])
                    p.tt("pool", v3(T[6], 64), tot_bc, c3, ALU.subtract, r=[tk[3]], w=[tk[6]])
                    p.act(T[6], T[6], AF.Exp, scale=-LD, r=[tk[6]], w=[tk[6]])
                    p.act(T[7], T[4], AF.Exp, scale=LD, r=[tk[4]], w=[tk[7]])
                    p.tt("pool", T[8], T[3], SWt, ALU.subtract, r=[tk[3], SWk_], w=[tk[8]])
                    p.act(T[8], T[8], AF.Exp, scale=-LD, r=[tk[8]], w=[tk[8]])
                wc = WC[buf][dpos]
                wck = ("WC", buf, dpos)
                p.act(wc.unsqueeze(2), c3[:, :, 63:64], AF.Exp, scale=-LD, r=[tk[3]], w=[wck])
                p.tt("pool", T[9], T[0], AAt, ALU.mult, r=[tk[0], AAk_], w=[tk[9]])
                p.ts("dve", AAt, AAt, -1.0, pv[:, PV_KA + j:PV_KA + j + 1], ALU.add, ALU.mult, r=[AAk_, "pv"], w=[AAk_])
                if dpos == 0:
                    KD, KDk = T[10], tk[10]
                else:
                    KD, KDk = AAt, AAk_
                p.stt(KD, AAt, 1.0, k_, ALU.add, ALU.mult, r=[AAk_, ck[1]], w=[KDk])
                for pr in range(2):
                    sl = slice(pr * 128, (pr + 1) * 128)
                    fm = FM[buf][dpos][pr]
                    fk = ("FM", buf, dpos, pr)
                    p.tt("pool", fm[:, 0, :], r_[:, sl], T[5][:, sl], ALU.mult, r=[ck[0], tk[5]], w=[fk])
                    p.stt(fm[:, 1, :], T[0][:, sl], -1.0, T[6][:, sl], ALU.mult, ALU.mult, r=[tk[0], tk[6]], w=[fk])
                    p.tt("pool", fm[:, 2, :], T[9][:, sl], T[7][:, sl], ALU.mult, r=[tk[9], tk[7]], w=[fk])
                    p.tt("dve", fm[:, 3, :], KD[:, sl], T[7][:, sl], ALU.mult, r=[KDk, tk[7]], w=[fk])
                p.tt("pool", BH[dpos], T[9], T[8], ALU.mult, r=[tk[9], tk[8]], w=[("BH", dpos)])
                p.tt("dve", KH[dpos], KD, T[8], ALU.mult, r=[KDk, tk[8]], w=[("KH", dpos)])
                if dpos == 1:
                    p.tt("pool", T[10], T[10], AAt, ALU.add, r=[tk[10], AAk_], w=[tk[10]])
            if is_main:
                p.stt(T[1], r_, pv[:, PV_RK + j:PV_RK + j + 1], T[10], ALU.mult, ALU.mult, r=[ck[0], "pv", tk[10]], w=[tk[1]])
                pm, pmk = pm_region()
                p.mm(pm, bones_f, T[1], r=["cmf", tk[1]], w=[pmk])
                p.tt("dve", bonusT[:, j, m0:m0 + n], pm, v_, ALU.mult, r=[pmk, ck[2]], w=[("bonus", gi, j)])
            for pr in range(2):
                sl = slice(pr * 128, (pr + 1) * 128)
                pst = v3(psb[1][:, 0:448].bitcast(BF16), 128)
                pstk = ("ps", 1)
                tmk = ("TM", buf, pr)
                srcs = [(0, FM[buf][0][pr][:, 1, :], ("FM", buf, 0, pr)), (1, BH[0][:, sl], ("BH", 0)),
                        (2, KH[0][:, sl], ("KH", 0)), (6, vb[:, sl], "vb")]
                if is_main:
                    srcs += [(3, FM[buf][1][pr][:, 1, :], ("FM", buf, 1, pr)), (4, BH[1][:, sl], ("BH", 1)),
                             (5, KH[1][:, sl], ("KH", 1))]
                for slot, src, skey in srcs:
                    p.tr(pst[:, slot, :], src, ident_b, r=[skey, "identb"], w=[pstk])
                if is_main:
                    p.cp("act" if pr == 0 else "dve", TMt[buf][pr], pst, r=[pstk], w=[tmk])
                else:
                    p.cp("act", TMt[buf][pr][:, 0:3, :], pst[:, 0:3, :], r=[pstk], w=[tmk])
                    p.cp("dve", TMt[buf][pr][:, 6, :], pst[:, 6, :], r=[pstk], w=[tmk])
            stage_seq.append(p.cap_end())
            p.cap_begin()
            if is_main:
                batches = [[(pr, half, dpos) for half in range(2) for dpos in (0, 1)] for pr in range(2)]
            else:
                batches = [[(pr, half, 0) for pr in range(2) for half in range(2)]]
            bp_list = []
            for batch in batches:
                p.cap_begin()
                gens = []
                uinfo = []
                pbuf = pbcnt[0] % 2
                pbcnt[0] += 1
                bos = None
                if is_main:
                    bos = bocnt[0] % 4
                    bocnt[0] += 1
                for st_, (pr, half, dpos) in enumerate(batch):
                    uinfo.append((st_, pr, half, dpos))
                    gens.append(unit_gen(st_, buf, j, pr, half, dpos, is_main, PPA[pr if not is_main else pbuf],
                                         ("PPA", pr if not is_main else pbuf), QTA[pbuf], ("QTA", pbuf),
                                         BO[bos] if is_main else None, ("BO", bos)))
                alive = list(gens)
                while alive:
                    nxt_alive = []
                    for g_ in alive:
                        try:
                            next(g_)
                            nxt_alive.append(g_)
                        except StopIteration:
                            pass
                    alive = nxt_alive
                bp_list.append(p.cap_end())
                p.cap_begin()
                prs = sorted(set(pr for (_, pr, _, _) in uinfo))
                for pr in prs:
                    gp = m0 // 128 + pr
                    tm = TMt[buf][pr]
                    tmk = ("TM", buf, pr)
                    ppa_i = pr if not is_main else pbuf
                    ppa, ppak = PPA[ppa_i], ("PPA", ppa_i)
                    qta, qtak = QTA[pbuf], ("QTA", pbuf)
                    ybank, ybk = psb[2], ("ps", 2)
                    p8bank, p8k = psb[3], ("ps", 3)
                    psy = ybank[:, 0:128]
                    first_y = [True]
                    if is_main:
                        for (st_, pr_u, half, dpos) in uinfo:
                            hc = slice(half * 64, (half + 1) * 64)
                            p.mm(psy[:, hc], RbT[st_], GU[st_][:, 64:128], start=first_y[0], stop=False, sg=True,
                                 r=[("RbT", st_), ("GU", st_)], w=[ybk])
                            first_y[0] = False
                            p.mm(psy[:, hc], A2[st_][:, 0:128], tm[:, 6, hc], start=False, stop=False, sg=True,
                                 r=[("A2", st_), tmk], w=[ybk])
                    if pr == prs[0]:
                        bp_list.append(p.cap_end())
                        p.cap_begin()
                    for c2 in range(2):
                        cr = slice(c2 * 64, (c2 + 1) * 64)
                        sin_, sout = Sst[sp_par[j]], Sst[1 - sp_par[j]]
                        sink_, soutk = ("S", sp_par[j], j), ("S", 1 - sp_par[j], j)
                        P8 = p8bank[:, c2 * 64:(c2 + 1) * 64]
                        p.mm(P8[0:64, :], ppa[0:64, c2, 0:64], sin_[0:64, j, 0, :], tp=(0, 0), r=[ppak, sink_], w=[p8k])
                        p.mm(P8[64:128, :], ppa[64:128, c2, 0:64], sin_[64:128, j, 1, :], tp=(64, 64), r=[ppak, sink_], w=[p8k])
                        if is_main:
                            p.mm(psy[cr, :], qta[:, c2 * 64:(c2 + 1) * 64], sin_[:, j].rearrange("p a b -> p (a b)"),
                                 start=False, sg=True, stop=(c2 == 1), tp=(0, c2 * 64), r=[qtak, sink_], w=[ybk])
                        p.tt("dve", tmpS[:, j, :], P8, ppa[:, c2, 64:128], ALU.add, r=[p8k, ppak], w=[("tmpS", j)])
                        p.tt("dve", sout[:, j], tmpS[:, j, :].unsqueeze(1).to_broadcast([128, 2, 64]), bdm, ALU.mult,
                             r=[("tmpS", j), "cmf"], w=[soutk])
                        sp_par[j] = 1 - sp_par[j]
                    if is_main:
                        p.cp("act", yacc[:, gp, j * 128:(j + 1) * 128], psy, r=[ybk], w=[("yacc", gp, j)])
                        p.dma("sp", bops_d[gp * 4 + j], BO[bos], r=[("BO", bos)], w=[("bops", gp, j)])
                bp_list.append(p.cap_end())
            p.ops.extend(Prog.merge(carry_post[0], bp_list[0]) if carry_post[0] else bp_list[0])
            p.ops.extend(bp_list[1])
            if len(bp_list) == 6:
                p.ops.extend(Prog.merge(bp_list[2], bp_list[3]))
                p.ops.extend(bp_list[4])
                carry_post[0] = bp_list[5]
            else:
                carry_post[0] = bp_list[2]
            stage_seq.append(p.cap_end())
    p.ops.extend(stage_seq[0])
    for k_ in range(1, len(stage_seq) - 1, 2):
        p.ops.extend(Prog.merge(stage_seq[k_], stage_seq[k_ + 1]))
    p.ops.extend(stage_seq[-1])
    p.ops.extend(carry_post[0])
    dump("yacc", yacc, [128, 16, 512], BF16)
    dump("bonusT", bonusT, [128, 4, NMAIN], BF16)
    for j in range(4):
        p.tt("dve", Sf32[:, j * 64:(j + 1) * 64], Sst[sp_par[j]][:, j, 0, :], Sst[sp_par[j]][:, j, 1, :], ALU.add,
             r=[("S", sp_par[j], j)], w=["Sf32"])
    dump("SA", Sf32, [128, 256], F32)
    p.dma("pool", cin_d, Sf32, r=["Sf32"], w=["cin"])
    groups_cc = [[0, 1], [2, 3], [4, 5], [6, 7]]
    if not no_cc:
      p.add("pool", lambda g: g.collective_compute("AllGather", ALU.bypass, replica_groups=groups_cc, ins=[cin_d],
                                                 outs=[cout_d]), r=["cin"], w=["cout"])
    p.dma("pool", cslots, cout_d.rearrange("(r p) c -> p r c", r=2), r=["cout"], w=["cslots"])
    if stop_after == "RW":
        return finish()
    p.barrier()

    L = Bump(131 * KB, 207 * KB)
    wch = [L([128, 8, 128], BF16) for _ in range(4)]
    qT = L([128, 4, NMAIN], BF16)
    kT = L([128, 2, 2176], BF16)
    kcT = L([128, 2, 256], BF16)
    v_sb = L([128, 19, 128], BF16)
    cosT = L([128, 2176], F32)
    sinT = L([128, 2176], F32)
    rt = [[L([128, 512], F32) for _ in range(2)] for _ in range(2)]
    PT = [L([128, 512], BF16) for _ in range(4)]
    esink = L([128, 512], F32)
    den = [L([128, 256], F32) for _ in range(2)]
    qbd = [L([128, 2, 128], BF16) for _ in range(4)]
    bdm128 = v3(bones_f, 64)[:, :, 0:1].to_broadcast([128, 2, 128])
    p.dma("sp", cosT, cos_d, w=["cosT"])
    p.dma("sp", sinT, sin_d, w=["sinT"])
    p.dma("sp", esink, sinkb_d, w=["esink"])
    p.act(esink, esink, AF.Exp, r=["esink"], w=["esink"])
    w_att_v = w_att_d.rearrange("(kc p) n -> p kc n", p=128)
    wcnt = [0]

    def load_wch(col0):
        k = wcnt[0] % 4
        wcnt[0] += 1
        p.dma("pool", wch[k], w_att_v[:, :, col0:col0 + 128], w=[("wch", k)])
        return wch[k], ("wch", k)

    pcnt = [0]

    def proj_rope(wa, wak, wb, wbk, tok0, ntok, dst, dst_key):
        k = pcnt[0] % 2
        pcnt[0] += 1
        pa, pak = psb[2 * k][:, 0:ntok], ("ps", 2 * k)
        pb, pbk = psb[2 * k + 1][:, 0:ntok], ("ps", 2 * k + 1)
        for kc in range(8):
            p.mm(pa, wa[:, kc, :], hT[:, kc, tok0:tok0 + ntok], start=(kc == 0), stop=(kc == 7), r=[wak] + HT_ALL, w=[pak])
        for kc in range(8):
            p.mm(pb, wb[:, kc, :], hT[:, kc, tok0:tok0 + ntok], start=(kc == 0), stop=(kc == 7), r=[wbk] + HT_ALL, w=[pbk])
        m = tok0 - MAIN0
        t1, t2 = rt[k]
        p.tt("dve", t1[:, 0:ntok], pa, cosT[:, m:m + ntok], ALU.mult, r=[pak, "cosT"], w=[("rt", k, 0)])
        p.tt("dve", t2[:, 0:ntok], pb, sinT[:, m:m + ntok], ALU.mult, r=[pbk, "sinT"], w=[("rt", k, 1)])
        p.tt("pool", dst, t1[:, 0:ntok], t2[:, 0:ntok], ALU.add, r=[("rt", k, 0), ("rt", k, 1)], w=[dst_key])

    wloads = [(c * 128, 512 + c * 128) for c in range(4)] + [(1024 + g * 128, 1280 + g * 128) for g in range(2)] + [(1536,)]
    wloaded = {}

    def ensure_w(n):
        if n < len(wloads) and n not in wloaded:
            wloaded[n] = [load_wch(col) for col in wloads[n]]

    ensure_w(0)
    for c in range(4):
        ensure_w(c + 1)
        (wa, wak), (wb, wbk) = wloaded[c]
        for tg in range(4):
            proj_rope(wa, wak, wb, wbk, MAIN0 + tg * 512, 512, qT[:, c, tg * 512:(tg + 1) * 512], ("qT", c, tg))
    for g in range(2):
        ensure_w(4 + g + 1)
        (wa, wak), (wb, wbk) = wloaded[4 + g]
        for tg in range(5):
            nt = 512 if tg < 4 else 128
            proj_rope(wa, wak, wb, wbk, MAIN0 + tg * 512, nt, kT[:, g, tg * 512:tg * 512 + nt], ("kT", g, tg))
        k = pcnt[0] % 2
        pcnt[0] += 1
        pa, pak = psb[2 * k][:, 0:256], ("ps", 2 * k)
        for kc in range(8):
            p.mm(pa, wa[:, kc, :], hT[:, kc, 0:256], start=(kc == 0), stop=(kc == 7), r=[wak] + HT_ALL, w=[pak])
        p.cp("act", kcT[:, g, :], pa, r=[pak], w=[("kcT", g)])
    (wv, wvk), = wloaded[6]
    for tt_ in range(19):
        k = tt_ % 2
        pa, pak = psb[4 + k][:, 0:128], ("ps", 4 + k)
        for kc in range(8):
            p.mm(pa, hT[:, kc, tt_ * 128:(tt_ + 1) * 128], wv[:, kc, :], start=(kc == 0), stop=(kc == 7),
                 r=[wvk] + HT_ALL, w=[pak])
        p.cp("act" if k == 0 else "dve", v_sb[:, tt_, :], pa, r=[pak], w=[("v_sb", tt_)])
    dump("qT", qT, [128, 4, NMAIN], BF16)
    dump("kT", kT, [128, 2, 2176], BF16)
    dump("v_sb", v_sb, [128, 19, 128], BF16)
    QK_ALL = [("qT", c, tg) for c in range(4) for tg in range(4)] + [("kT", g, tg) for g in range(2) for tg in range(5)] + \
             [("kcT", g) for g in range(2)] + [("v_sb", t) for t in range(19)]
    scnt = [0]
    ocnt = [0]
    def build_qb(i, g, ko):
        out_ = []
        for cqi in range(2):
            qi_ = (2 * ko + cqi) % 4
            p.tt("dve", qbd[qi_], qT[:, 2 * g + cqi, i * 128:(i + 1) * 128].unsqueeze(1).to_broadcast([128, 2, 128]),
                 bdm128, ALU.mult, r=QK_ALL + ["cmf"], w=[("qbd", qi_)])
            out_.append((qbd[qi_].rearrange("p a b -> p (a b)"), ("qbd", qi_)))
        return out_

    ig_list = [(i, g) for i in range(16) for g in range(2)]
    qb_next = build_qb(0, 0, 0)
    for n_ig, (i, g) in enumerate(ig_list):
        if True:
            kbs = [("c", 0), ("c", 1)] + ([("l", i - 1)] if i > 0 else []) + [("l", i), ("l", i + 1)]
            ko = ocnt[0] % 2
            ocnt[0] += 1
            pso, psok = psb[6 + ko], ("ps", 6 + ko)
            qb = qb_next
            def emit_scores(kk_, kb):
                ks = scnt[0] % 3
                kp = scnt[0] % 4
                scnt[0] += 1
                pss, pssk = psb[ks], ("ps", ks)
                if kk_ == "c":
                    ksrc = kcT[:, g, kb * 128:(kb + 1) * 128]
                    vt = v_sb[:, kb, g * 64:(g + 1) * 64]
                else:
                    ksrc = kT[:, g, kb * 128:(kb + 1) * 128]
                    vt = v_sb[:, 2 + kb, g * 64:(g + 1) * 64]
                for cqi in range(2):
                    p.mm(pss[:, cqi * 256:(cqi + 1) * 256], ksrc, qb[cqi][0], r=QK_ALL + [qb[cqi][1]], w=[pssk])
                return pss, pssk, kp, vt

            pendq = [emit_scores(*kbs[0]), emit_scores(*kbs[1])]
            for bi, (kk_, kb) in enumerate(kbs):
                pss, pssk, kp, vt = pendq.pop(0)
                if bi + 2 < len(kbs):
                    pendq.append(emit_scores(*kbs[bi + 2]))
                if bi + 3 == len(kbs) and n_ig + 1 < len(ig_list):
                    qb_next = build_qb(ig_list[n_ig + 1][0], ig_list[n_ig + 1][1], 1 - ko)
                pt, ptk = PT[kp], ("PT", kp)
                p.act(pt, pss, AF.Exp, scale=0.125, r=[pssk], w=[ptk])
                if kk_ == "l" and kb == i - 1:
                    p.tt("dve", pt, pt, band_prev, ALU.mult, r=[ptk, "cmb"], w=[ptk])
                if kk_ == "l" and kb == i + 1:
                    p.tt("dve", pt, pt, band_next, ALU.mult, r=[ptk, "cmb"], w=[ptk])
                pt4 = v3(pt, 128)
                first, last = bi == 0, bi == len(kbs) - 1
                for par in range(2):
                    ps_ = slice(par * 64, (par + 1) * 64)
                    p.mm(pso[ps_, 0:256], vt, pt4[:, par:4:2, :], start=first, stop=False, tp=(0, par * 64), sg=True,
                         r=[ptk] + QK_ALL, w=[psok])
                    p.mm(pso[ps_, 256:512], ones_b[:, 0:64], pt4[:, par:4:2, :], start=False, stop=last, tp=(0, par * 64),
                         sg=True, r=[ptk, "onesb"], w=[psok])
            dn, dnk = den[ko], ("den", ko)
            p.tt("dve", dn, pso[:, 256:512], esink[:, g * 256:(g + 1) * 256], ALU.add, r=[psok, "esink"], w=[dnk])
            p.add("dve", lambda v, o=dn: v.reciprocal(out=o, in_=o), r=[dnk], w=[dnk])
            p.tt("dve", yaT[:, 2 * g:2 * g + 2, i * 128:(i + 1) * 128], v3(pso[:, 0:256], 128), v3(dn, 128), ALU.mult,
                 r=[psok, dnk], w=[("yaT", i, g)])
    dump("yaT", yaT, [128, 4, NMAIN], BF16)
    if stop_after == "ATT":
        return finish()
    p.barrier()

    L = Bump(131 * KB, 207 * KB)
    g2 = L([96, 512], BF16)
    BOs = [L([128, 4, 384], BF16) for _ in range(3)]
    ytot = [L([128, 512], F32) for _ in range(2)]
    ycen = [L([128, 512], F32) for _ in range(2)]
    ysq = [L([128, 512], F32) for _ in range(2)]
    ynb = [L([128, 512], BF16) for _ in range(2)]
    lin = [L([128, 4, 128], F32) for _ in range(2)]
    gst = [[L([128, 8], F32) for _ in range(3)] for _ in range(2)]
    p.dma("pool", g2, g2_d, w=["g2"])
    p.ts("dve", Sf32, cslots[:, 0, :], pv[:, PV_FSEL:PV_FSEL + 1], None, ALU.mult, r=["cslots", "pv"], w=["Sf32"])
    p.stt(Sf32, cslots[:, 1, :], pv[:, PV_FSEL + 1:PV_FSEL + 2], Sf32, ALU.mult, ALU.add,
          r=["cslots", "pv", "Sf32"], w=["Sf32"])
    for j in range(4):
        p.tt("pool", Sst[0][:, j], Sf32[:, j * 64:(j + 1) * 64].unsqueeze(1).to_broadcast([128, 2, 64]), bdm, ALU.mult,
             r=["Sf32", "cmf"], w=["SB0"])
    dump("SB0", Sf32, [128, 256], F32)
    spb = 0
    SBK = ["SB0", "SB1"]
    b_rec, b_out = [], []

    def load_bo(gi2):
        gp2 = 15 - gi2
        p.dma("sp", BOs[gi2 % 3], bops_d[gp2 * 4:(gp2 + 1) * 4].rearrange("j p c -> p j c"),
              r=[("bops", gp2, j) for j in range(4)], w=[("BOs", gi2 % 3)])

    load_bo(0)
    load_bo(1)
    for gi_, gp in enumerate(range(15, -1, -1)):
        p.cap_begin()
        sl_ = gi_ % 3
        bo, bok = BOs[sl_], ("BOs", sl_)
        if gi_ + 2 < 16:
            load_bo(gi_ + 2)
        kb_ = gi_ % 2
        psyb, psybk = psb[kb_], ("ps", kb_)
        for c2 in (1, 0):
            cr = slice(c2 * 64, (c2 + 1) * 64)
            sin_, sout = Sst[spb], Sst[1 - spb]
            sink_, soutk = SBK[spb], SBK[1 - spb]
            P8 = psb[2 + c2][:, 0:256]
            P8k = ("ps", 2 + c2)
            for j in range(4):
                p.mm(psyb[cr, j * 128:(j + 1) * 128], bo[:, j, 256 + c2 * 64:256 + (c2 + 1) * 64],
                     sin_[:, j].rearrange("p a b -> p (a b)"), tp=(0, c2 * 64), r=[bok, sink_], w=[psybk])
            for j in range(4):
                for half in range(2):
                    hp = slice(half * 64, (half + 1) * 64)
                    p.mm(P8[hp, j * 64:(j + 1) * 64], bo[hp, j, c2 * 128:c2 * 128 + 64], sin_[hp, j, half, :],
                         tp=(half * 64, half * 64), r=[bok, sink_], w=[P8k])
            p.tt("dve", tmpS, v3(P8, 64), bo[:, :, c2 * 128 + 64:c2 * 128 + 128], ALU.add, r=[P8k, bok], w=["tmpSB"])
            p.tt("dve", sout, tmpS.unsqueeze(2).to_broadcast([128, 4, 2, 64]),
                 bdm.unsqueeze(1).to_broadcast([128, 4, 2, 64]), ALU.mult, r=["tmpSB", "cmf"], w=[soutk])
            spb = 1 - spb
        b_rec.append(p.cap_end())
        p.cap_begin()
        yt, ytk = ytot[kb_], ("ytot", kb_)
        p.tt("dve", yt, psyb, yacc[:, gp, :], ALU.add, r=[psybk] + [("yacc", gp, j) for j in range(4)], w=[ytk])
        y3 = v3(yt, 64)
        s1, s2, rs = gst[kb_]
        gk = ("gst", kb_)
        p.add("dve", lambda v, o=s1, i_=y3: v.reduce_sum(out=o, in_=i_, axis=AX.X), r=[ytk], w=[gk])
        p.ts("pool", s1, s1, 1.0 / 64, None, ALU.mult, r=[gk], w=[gk])
        yc, yck = ycen[kb_], ("ycen", kb_)
        p.tt("pool", v3(yc, 64), y3, s1.unsqueeze(2).to_broadcast([128, 8, 64]), ALU.subtract, r=[ytk, gk], w=[yck])
        sq, sqk = ysq[kb_], ("ysq", kb_)
        p.tt("pool", sq, yc, yc, ALU.mult, r=[yck], w=[sqk])
        p.add("dve", lambda v, o=s2, i_=v3(sq, 64): v.reduce_sum(out=o, in_=i_, axis=AX.X), r=[sqk], w=[gk])
        p.ts("dve", rs, s2, 1.0 / 64, LNX_EPS, ALU.mult, ALU.add, r=[gk], w=[gk])
        p.act(rs, rs, AF.Sqrt, r=[gk], w=[gk])
        p.add("dve", lambda v, o=rs: v.reciprocal(out=o, in_=o), r=[gk], w=[gk])
        yn, ynk = ynb[kb_], ("ynb", kb_)
        p.tt("dve", v3(yn, 64), v3(yc, 64), rs.unsqueeze(2).to_broadcast([128, 8, 64]), ALU.mult, r=[yck, gk], w=[ynk])
        pst = v3(psb[4 + kb_][:, 0:256].bitcast(BF16), 128)
        pstk = ("ps", 4 + kb_)
        for j in range(4):
            p.tr(pst[:, j, :], yn[:, j * 128:(j + 1) * 128], ident_b, r=[ynk, "identb"], w=[pstk])
        psg, psgk = psb[6 + kb_], ("ps", 6 + kb_)
        for j in range(4):
            p.mm(psg[:, j * 128:(j + 1) * 128], g2[0:96, j * 128:(j + 1) * 128], sgT[0:96, gp * 128:(gp + 1) * 128],
                 r=["g2"] + [("sgT", 1 + gp // 2)], w=[psgk])
        ln_, lnk = lin[kb_], ("lin", kb_)
        for j in range(4):
            p.act(ln_[:, j, :], pst[:, j, :], AF.Identity, bias=pv[:, PV_LB + j:PV_LB + j + 1],
                  scale=pv[:, PV_LW + j:PV_LW + j + 1], r=[pstk, "pv"], w=[lnk])
        p.tt("pool", ln_, ln_, bonusT[:, :, gp * 128:(gp + 1) * 128], ALU.add,
             r=[lnk] + [("bonus", 1 + gp // 2, j) for j in range(4)], w=[lnk])
        p.tt("dve", yrT[:, :, gp * 128:(gp + 1) * 128], ln_, v3(psg[:, :], 128), ALU.mult, r=[lnk, psgk], w=[("yrT", gp)])
        b_out.append(p.cap_end())
    p.ops.extend(b_rec[0])
    for k_ in range(16):
        if k_ + 1 < 16:
            p.ops.extend(Prog.merge(b_out[k_], b_rec[k_ + 1]))
        else:
            p.ops.extend(b_out[k_])
    dump("yrT", yrT, [128, 4, NMAIN], BF16)
    if stop_after == "B":
        return finish()
    p.barrier()

    w_g = carve(95 * KB, [128, 8, 2048], BF16)
    w_ba = carve(127 * KB, [128, 4, D], BF16)
    w_br = carve(135 * KB, [128, 4, D], BF16)
    L = Bump(143 * KB, 175 * KB)
    mergedT = carve(175 * KB, [128, 8, NMAIN], BF16)
    gat = [[L([128, 512], F32) for _ in range(2)] for _ in range(2)]
    mt = [[L([128, 512], F32) for _ in range(2)] for _ in range(2)]
    w_g_v = w_g_d.rearrange("(kc p) n -> p kc n", p=128)
    for q4 in range(4):
        p.dma("pool", w_g[:, :, q4 * 512:(q4 + 1) * 512], w_g_v[:, :, q4 * 512:(q4 + 1) * 512], w=[("w_g", q4)])
    p.dma("pool", w_ba, w_ba_d.rearrange("(c p) n -> p c n", p=128), w=["w_ba"])
    p.dma("pool", w_br, w_br_d.rearrange("(c p) n -> p c n", p=128), w=["w_br"])
    WG = [("w_g", q4) for q4 in range(4)]
    YA = [("yaT", i, g) for i in range(16) for g in range(2)]
    YR = [("yrT", gp) for gp in range(16)]
    mc = 0
    for gm in range(4):
        tok = MAIN0 + gm * 512
        m0 = gm * 512
        for dc in range(8):
            s_ = mc % 2
            mc += 1
            pga, pgr, pza, pzr = psb[4 * s_], psb[4 * s_ + 1], psb[4 * s_ + 2], psb[4 * s_ + 3]
            kga, kgr, kza, kzr = [("ps", 4 * s_ + i) for i in range(4)]
            for kc in range(8):
                p.mm(pga, w_g[:, kc, dc * 128:(dc + 1) * 128], hT[:, kc, tok:tok + 512], start=(kc == 0), stop=(kc == 7),
                     r=WG + HT_ALL, w=[kga])
            for kc in range(8):
                p.mm(pgr, w_g[:, kc, 1024 + dc * 128:1024 + (dc + 1) * 128], hT[:, kc, tok:tok + 512], start=(kc == 0),
                     stop=(kc == 7), r=WG + HT_ALL, w=[kgr])
            for c in range(4):
                p.mm(pza, w_ba[:, c, dc * 128:(dc + 1) * 128], yaT[:, c, m0:m0 + 512], start=(c == 0), stop=(c == 3),
                     r=["w_ba"] + YA, w=[kza])
            for c in range(4):
                p.mm(pzr, w_br[:, c, dc * 128:(dc + 1) * 128], yrT[:, c, m0:m0 + 512], start=(c == 0), stop=(c == 3),
                     r=["w_br"] + YR, w=[kzr])
            ga, gr = gat[s_]
            p.act(ga, pga, AF.Sigmoid, r=[kga], w=[("gat", s_, 0)])
            p.act(gr, pgr, AF.Sigmoid, r=[kgr], w=[("gat", s_, 1)])
            m1, m2 = mt[s_]
            p.tt("dve", m1, pza, ga, ALU.mult, r=[kza, ("gat", s_, 0)], w=[("mt", s_, 0)])
            p.tt("dve", m2, pzr, gr, ALU.mult, r=[kzr, ("gat", s_, 1)], w=[("mt", s_, 1)])
            p.tt("pool", mergedT[:, dc, m0:m0 + 512], m1, m2, ALU.add, r=[("mt", s_, 0), ("mt", s_, 1)], w=[("merged", gm, dc)])
    dump("mergedT", mergedT, [128, 8, NMAIN], BF16)
    if stop_after == "MERGE":
        return finish()
    p.barrier()

    w_up = carve(8 * KB, [128, 8, 4096], BF16)
    w_down = carve(72 * KB, [128, 32, D], BF16)
    w_out = carve(136 * KB, [128, 8, D], BF16)
    nfbc = carve(152 * KB, [128, D], F32)
    g1bc = carve(8 * KB, [128, D], F32)
    g2bc = carve(12 * KB, [128, D], F32)
    dgt = carve(16 * KB, [128, 128], F32)
    stg = [carve(156 * KB, [128, 2, D], F32), carve(164 * KB, [128, 2, D], F32)]
    x1b = [carve(156 * KB, [128, D], F32), carve(160 * KB, [128, D], F32)]
    xn2b = [carve(164 * KB, [128, D], BF16), carve(166 * KB, [128, D], BF16)]
    h2Tb = [carve(168 * KB, [128, 8, 128], BF16), carve(170 * KB, [128, 8, 128], BF16)]
    actR = [carve(172 * KB, [128, 4, 128], BF16), carve(173 * KB, [128, 4, 128], BF16)]
    p.dma("sp", nfbc, nfbc_d, w=["nfbc"])
    for gi_, (gbc, jbase, gkey) in enumerate(((g1bc, 16, "g1bc"), (g2bc, 40, "g2bc"))):
        for kc in range(8):
            bank = 2 * gi_ + kc // 4
            p.ts("dve", dgt, ident_f, modT[:, jbase + kc, 0:1], None, ALU.mult, r=["cmf"] + MODK, w=["dgt"])
            p.mm(psb[bank][:, (kc % 4) * 128:(kc % 4 + 1) * 128], ones_f, dgt, r=["cmf", "dgt"], w=[("ps", bank)])
        p.cp("act", gbc[:, 0:512], psb[2 * gi_], r=[("ps", 2 * gi_)], w=[gkey])
        p.cp("act", gbc[:, 512:1024], psb[2 * gi_ + 1], r=[("ps", 2 * gi_ + 1)], w=[gkey])
    w_out_v = w_out_d.rearrange("(kc p) n -> p kc n", p=128)
    w_down_v = w_down_d.rearrange("(f p) n -> p f n", p=128)
    w_up_v = w_up_d.rearrange("(kc p) n -> p kc n", p=128)
    for kc in range(2, 8):
        p.dma("pool", w_up[:, kc, :], w_up_v[:, kc, :], w=[("w_up", kc)])
    sc_ = 0
    for q in range(4):
        s_, sk_ = stg[sc_ % 2], ("stg", sc_ % 2)
        sc_ += 1
        p.dma("sp", s_, w_out_v[:, 2 * q:2 * q + 2, :], w=[sk_])
        p.tt("dve", w_out[:, 2 * q:2 * q + 2, :], s_, g1bc.unsqueeze(1).to_broadcast([128, 2, D]), ALU.mult,
             r=[sk_, "g1bc"], w=[("w_out", q)])
    for q in range(16):
        s_, sk_ = stg[sc_ % 2], ("stg", sc_ % 2)
        sc_ += 1
        p.dma("sp", s_, w_down_v[:, 2 * q:2 * q + 2, :], w=[sk_])
        p.tt("dve", w_down[:, 2 * q:2 * q + 2, :], s_, g2bc.unsqueeze(1).to_broadcast([128, 2, D]),
             ALU.mult, r=[sk_, "g2bc"], w=[("w_down", q)])
    for kc in range(2):
        p.dma("pool", w_up[:, kc, :], w_up_v[:, kc, :], r=["g1bc", "g2bc", "dgt"],
              w=[("w_up", kc), "g1bc", "g2bc", "dgt"])
    p.barrier()
    WOUT = [("w_out", q) for q in range(4)]
    WDN = [("w_down", q) for q in range(16)]
    WUP = [("w_up", kc) for kc in range(8)]
    def mlp_head(tI):
        b = tI % 2
        m = tI * 128
        x1, xk = x1b[b], ("x1", b)
        xn2, xnk = xn2b[b], ("xn2", b)
        h2T, hk = h2Tb[b], ("h2T", b)
        p.dma("sp", x1, xin[MAIN0 + m:MAIN0 + m + 128, :], w=[xk])
        for hf in range(2):
            ps_, psk = psb[hf], ("ps", hf)
            for dc in range(8):
                p.mm(ps_, mergedT[:, dc, m:m + 128], w_out[:, dc, hf * 512:(hf + 1) * 512], start=(dc == 0), stop=(dc == 7),
                     r=WOUT + [("merged", tI // 4, dc)], w=[psk])
            p.tt("dve", x1[:, hf * 512:(hf + 1) * 512], ps_, x1[:, hf * 512:(hf + 1) * 512], ALU.add, r=[psk, xk], w=[xk])
        ssq = sstat[b][:, 0:1]
        rms_rstd(x1, ssq, NORM_EPS, 1.0 / D, [xk], ("ssq2", b), xn2, xnk)
        p.ts("dve", xn2, x1, ssq, None, ALU.mult, r=[xk, ("ssq2", b)], w=[xnk])
        transpose_modulate(xn2, xnk, lambda kc: h2T[:, kc, :], sc2, modT[:, 24:32, :], 0, 2, [hk])

    def mlp_mid(tI):
        b = tI % 2
        h2T, hk = h2Tb[b], ("h2T", b)

        def up_mm(fq):
            bank = 3 + fq % 2
            for kc in range(8):
                p.mm(psb[bank], h2T[:, kc, :], w_up[:, kc, fq * 512:(fq + 1) * 512], start=(kc == 0), stop=(kc == 7),
                     r=WUP + [hk], w=[("ps", bank)])

        up_mm(0)
        for fq in range(8):
            bank = 3 + fq % 2
            ps_, psk = psb[bank], ("ps", bank)
            r_, rk_ = rl[fq % 2], ("rl", fq % 2)
            p.act(r_, ps_, AF.Relu, r=[psk], w=[rk_])
            if fq + 1 < 8:
                up_mm(fq + 1)
            p.tt("pool", r_, r_, r_, ALU.mult, r=[rk_], w=[rk_])
            pstT = v3(psb[7][:, 0:256].bitcast(BF16), 128)
            for f4 in range(4):
                p.tr(pstT[:, f4, :], r_[:, f4 * 128:(f4 + 1) * 128], ident_b, r=[rk_, "identb"], w=[("ps", 7)])
            aR, aRk = actR[fq % 2], ("actR", fq % 2)
            p.cp("act" if fq % 2 else "dve", aR, pstT, r=[("ps", 7)], w=[aRk])
            for hf in range(2):
                for f4 in range(4):
                    f = fq * 4 + f4
                    p.mm(psb[5 + hf], aR[:, f4, :], w_down[:, f, hf * 512:(hf + 1) * 512], start=(f == 0), stop=(f == 31),
                         r=WDN + [aRk], w=[("ps", 5 + hf)])

    def mlp_tail(tI):
        b = tI % 2
        m = tI * 128
        x1, xk = x1b[b], ("x1", b)
        xn2, xnk = xn2b[b], ("xn2", b)
        for hf in range(2):
            ps_, psk = psb[5 + hf], ("ps", 5 + hf)
            p.tt("dve", x1[:, hf * 512:(hf + 1) * 512], ps_, x1[:, hf * 512:(hf + 1) * 512], ALU.add, r=[psk, xk], w=[xk])
        ssq3 = sstat[2 + b][:, 0:1]
        rms_rstd(x1, ssq3, NORM_EPS, 1.0 / D, [xk], ("ssq3", b), xn2, xnk)
        p.stt(x1, x1, ssq3, nfbc, ALU.mult, ALU.mult, r=[xk, ("ssq3", b), "nfbc"], w=[xk])
        p.dma("sp", out_d[m:m + 128, :], x1, r=[xk], w=[("out", tI)])

    mlp_head(0)
    for tI in range(16):
        mlp_mid(tI)
        if tI + 1 < 16:
            mlp_head(tI + 1)
        mlp_tail(tI)
    return finish()


def _bd(blk):
    m = np.zeros((128, 128), np.float32)
    m[0:64, 0:64] = blk
    m[64:128, 64:128] = blk
    return m


def _const_tables():
    i = np.arange(64)
    row, col = i[:, None], i[None, :]
    sl = (col < row).astype(np.float32)
    su = (col > row).astype(np.float32)
    iu = (col >= row).astype(np.float32)
    il = (col <= row).astype(np.float32)
    cmb = np.zeros((128, CM_BF_N), np.float32)
    kj = np.arange(128)[:, None]
    qi = np.arange(128)[None, :]
    cmb[:, CM_BPREV:CM_BPREV + 512] = np.tile((kj >= qi).astype(np.float32), (1, 4))
    cmb[:, CM_BNEXT:CM_BNEXT + 512] = np.tile((kj <= qi).astype(np.float32), (1, 4))
    cmb[:, CM_MA1:CM_MA1 + 384] = np.concatenate([_bd(sl), _bd(iu), _bd(su)], 1)
    cmb[:, CM_MB1:CM_MB1 + 384] = np.concatenate([_bd(su), _bd(il), _bd(sl)], 1)
    cmb[:, CM_MA2:CM_MA2 + 256] = np.concatenate([_bd(iu), _bd(su)], 1)
    cmb[:, CM_MB2:CM_MB2 + 256] = np.concatenate([_bd(il), _bd(sl)], 1)
    cmb[:, CM_IDB:CM_IDB + 128] = np.eye(128, dtype=np.float32)
    cmb[:, CM_ONESB:CM_ONESB + 128] = 1.0
    cmf = np.zeros((128, CF_N), np.float32)
    rs = np.ones(256, np.float32)
    rs[0::64] = 0.0
    cmf[:, CF_RESET:CF_RESET + 256] = rs[None, :]
    cmf[:, CF_ID:CF_ID + 128] = np.eye(128, dtype=np.float32)
    cmf[:, CF_BONES:CF_BONES + 128] = _bd(np.ones((64, 64), np.float32))
    cmf[:, CF_IDZ:CF_IDZ + 64] = np.tile(np.eye(64, dtype=np.float32), (2, 1))
    cmf[:, CF_ONES:CF_ONES + 128] = 1.0
    return cmb, cmf


def _rope_tables(half):
    m = np.arange(2176)
    pos = m if half == 0 else 4095 - m
    n_freq = 16
    inv_freq = np.power(np.float32(10000.0), -np.arange(n_freq, dtype=np.float32) / n_freq).astype(np.float32)
    rowp = (pos // 64).astype(np.float32)
    colp = (pos % 64).astype(np.float32)
    ang = np.concatenate([rowp[:, None] * inv_freq, colp[:, None] * inv_freq], axis=-1).astype(np.float32)
    cos, sin = np.cos(ang).astype(np.float32), np.sin(ang).astype(np.float32)
    cosd = np.concatenate([cos, cos], 1).T
    sind = np.concatenate([-sin, sin], 1).T
    return np.ascontiguousarray(np.tile(cosd, (2, 1))), np.ascontiguousarray(np.tile(sind, (2, 1)))


def _fm(vec, n):
    return np.ascontiguousarray(np.asarray(vec, np.float32).reshape(n, 128).T)


def prep_inputs(x, c, ctx, c_ctx, w_ada, b_ada, norm1_g, w_in, sink, conv_w, decay_w0, decay_w2,
                iclr_a0, iclr_a2, gate_g2, k_k, k_a, r_k, lnx_w, lnx_b, w_branch_attn, w_branch_rwkv,
                w_out, norm2_g, w_mlp_up, w_mlp_down, norm_f_g):
    f = lambda a: np.asarray(a, np.float32)
    x, c, ctx, c_ctx = f(x), f(c), f(ctx), f(c_ctx)
    w_in = f(w_in)[0]
    conv_w = f(conv_w)[0]
    cmb, cmf = _const_tables()
    QO, KO, VO, RWO, GO = 0, 512, 640, 768, 2464
    def swap_halves(w, nheads):
        w4 = w.reshape(1024, nheads, 2, 32)
        return w4[:, :, ::-1, :].reshape(1024, nheads * 64)
    wq = w_in[:, QO:QO + 512]
    wk = w_in[:, KO:KO + 128]
    wv = w_in[:, VO:VO + 128]
    wkp = swap_halves(wk, 2)
    kd = np.concatenate([wk[:, 0:64], wk[:, 0:64], wk[:, 64:128], wk[:, 64:128]], 1)
    kpd = np.concatenate([wkp[:, 0:64], wkp[:, 0:64], wkp[:, 64:128], wkp[:, 64:128]], 1)
    w_att = np.ascontiguousarray(np.concatenate([wq, swap_halves(wq, 8), kd, kpd, wv], 1))
    rw = w_in[:, RWO:RWO + 1696]
    w_rw = np.zeros((1024, 1792), np.float32)
    rwcol = np.full((14, 128), -1, np.int64)
    for j in range(4):
        for i3 in range(3):
            rwcol[3 * j + i3] = i3 * 512 + j * 128 + np.arange(128)
    rwcol[12, 0:32] = 1536 + np.arange(32)
    rwcol[12, 32:64] = 1568 + np.arange(32)
    rwcol[13, 0:96] = 1600 + np.arange(96)
    for ci in range(14):
        for pp in range(128):
            if rwcol[ci, pp] >= 0:
                w_rw[:, ci * 128 + pp] = rw[:, rwcol[ci, pp]]
    w_g = np.ascontiguousarray(w_in[:, GO:GO + 2048])
    g2 = np.ascontiguousarray(f(gate_g2)[0])
    nfbc = np.ascontiguousarray(np.tile(f(norm_f_g)[None, :], (128, 1)))
    w0, a0 = f(decay_w0)[0], f(iclr_a0)[0]
    w2, a2 = f(decay_w2)[0], f(iclr_a2)[0]
    in_maps = []
    for core in range(8):
        b, half = core // 2, core % 2
        if half == 0:
            xs = x[b, 0:2176]
            cs = ctx[b]
        else:
            xs = x[b, 4095:1919:-1]
            cs = ctx[b, ::-1]
        xin = np.ascontiguousarray(np.concatenate([cs, xs], 0))
        cc = np.zeros((128, 8, 2), np.float32)
        cc[:, :, 0] = c[b].reshape(8, 128).T
        cc[:, :, 1] = c_ctx.reshape(8, 128).T
        pv = np.zeros((128, PV_N), np.float32)
        pv[:, PV_BADA:PV_BADA + 48] = _fm(f(b_ada)[0], 48)
        pv[:, PV_G1:PV_G1 + 8] = _fm(f(norm1_g)[0], 8)
        pv[:, PV_G2:PV_G2 + 8] = _fm(f(norm2_g)[0], 8)
        dirs = (half, 1 - half)
        for dpos, dd in enumerate(dirs):
            pv[:, PV_W0 + dpos:PV_W0 + 8:2] = _fm(w0[dd], 4)
            pv[:, PV_A0 + dpos:PV_A0 + 8:2] = _fm(a0[dd], 4)
        pv[:, PV_KK:PV_KK + 4] = _fm(f(k_k)[0], 4)
        pv[:, PV_KA:PV_KA + 4] = _fm(f(k_a)[0], 4)
        pv[:, PV_RK:PV_RK + 4] = _fm(f(r_k)[0].reshape(512), 4)
        pv[:, PV_LW:PV_LW + 4] = _fm(f(lnx_w)[0], 4)
        pv[:, PV_LB:PV_LB + 4] = _fm(f(lnx_b)[0], 4)
        for ci in range(14):
            for tap in range(3):
                tp_ = tap if half == 0 else 2 - tap
                valid = rwcol[ci] >= 0
                pv[valid, PV_CW + ci * 3 + tap] = conv_w[tp_, rwcol[ci][valid]]
        pv[:, PV_FSEL] = float(half)
        pv[:, PV_FSEL + 1] = float(1 - half)
        loraw = np.zeros((64, 2, 2, 512), np.float32)
        for dpos, dd in enumerate(dirs):
            loraw[0:32, dpos, 0, :] = w2[dd]
            loraw[32:64, dpos, 1, :] = a2[dd]
        sk = f(sink)[0]
        sinkb = np.zeros((128, 2, 2, 128), np.float32)
        for g in range(2):
            for rr in range(2):
                for par in range(2):
                    sinkb[par * 64:(par + 1) * 64, g, rr, :] = sk[4 * g + 2 * rr + par]
        cosT, sinT = _rope_tables(half)
        in_maps.append({
            "xin": xin, "cc": cc.reshape(128, 16), "w_ada": f(w_ada)[0], "pv": pv, "sinkb": sinkb.reshape(128, 512),
            "nfbc": nfbc, "w_att": w_att, "w_rw": w_rw, "loraw": loraw.reshape(64, 2048), "g2": g2, "w_g": w_g,
            "w_ba": f(w_branch_attn)[0], "w_br": f(w_branch_rwkv)[0], "w_out": f(w_out)[0], "w_up": f(w_mlp_up)[0],
            "w_down": f(w_mlp_down)[0], "cosT": cosT, "sinT": sinT, "cmb": cmb, "cmf": cmf,
        })
    return in_maps


_NC_CACHE = {}


def kernel(**inputs):
    in_maps = prep_inputs(**inputs)
    if "nc" not in _NC_CACHE:
        _NC_CACHE["nc"] = build_program()[0]
    nc = _NC_CACHE["nc"]
    res = run_bass_kernel_spmd(nc, in_maps, core_ids=list(range(8)))
    out = np.zeros((4, 4096, 1024), np.float32)
    for core in range(8):
        b, half = core // 2, core % 2
        y = np.asarray(res.results[core]["out"], np.float32)
        if half == 0:
            out[b, 0:2048] = y
        else:
            out[b, 2048:4096] = y[::-1]
    return out
```

```python
import contextlib
import numpy as np
import concourse.bass as bass
import concourse.mybir as mybir
from concourse.bass_utils import run_bass_kernel_spmd

F32 = mybir.dt.float32
BF16 = mybir.dt.bfloat16
U8 = mybir.dt.uint8
AF = mybir.ActivationFunctionType
ALU = mybir.AluOpType
AX = mybir.AxisListType
KB = 1024
NDMASEM = 16
EMBED_WAIT = True

D = 1024
NTOK = 2432
CTX0, MAIN0, HALO0 = 0, 256, 2304
NMAIN = 2048
GT = 256
NORM_EPS = 1e-6
LNX_EPS = 1e-5 * 64

PV_BADA, PV_G1, PV_G2, PV_W0, PV_A0, PV_KK, PV_KA, PV_RK, PV_LW, PV_LB, PV_CW, PV_FSEL, PV_N = \
    0, 48, 56, 64, 72, 80, 84, 88, 92, 96, 100, 142, 144
CM_BPREV, CM_BNEXT, CM_MA1, CM_MB1, CM_MA2, CM_MB2, CM_IDB, CM_ONESB, CM_BF_N = 0, 512, 1024, 1408, 1792, 2048, 2304, 2432, 2560
CF_RESET, CF_ID, CF_BONES, CF_IDZ, CF_ONES, CF_N = 0, 256, 384, 512, 640, 768


class Prog:
    ENGS = ("pe", "act", "dve", "pool", "sp")

    def __init__(self, nc, same_engine_sync=True):
        self.nc = nc
        self.ops = []
        self.same_engine_sync = same_engine_sync

    def add(self, eng, emit, r=(), w=(), dma=False):
        self.ops.append(dict(eng=eng, emit=emit, r=tuple(r), w=tuple(w), dma=dma, bar=False))
        return len(self.ops) - 1

    def cap_begin(self):
        if not hasattr(self, "_cap_stack"):
            self._cap_stack = []
        self._cap_stack.append(self.ops)
        self.ops = []

    def cap_end(self):
        cap = self.ops
        self.ops = self._cap_stack.pop()
        return cap

    @staticmethod
    def merge(a, b):
        out = []
        ia = ib = 0
        na, nb = len(a), len(b)
        while ia < na or ib < nb:
            if ib >= nb or (ia < na and ia * nb <= ib * na):
                out.append(a[ia]); ia += 1
            else:
                out.append(b[ib]); ib += 1
        return out

    def barrier(self):
        for e in self.ENGS:
            self.ops.append(dict(eng=e, emit=None, r=(), w=(), dma=False, bar=True))

    def mm(self, out, lhsT, rhs, start=True, stop=True, tp=None, r=(), w=(), sg=False):
        def e(pe):
            kw = {}
            if tp is not None:
                kw["tile_position"] = tp
            if sg:
                kw["skip_group_check"] = True
            return pe.matmul(out, lhsT=lhsT, rhs=rhs, start=start, stop=stop, **kw)
        return self.add("pe", e, r, w)

    def tr(self, out, in_, ident, r=(), w=()):
        return self.add("pe", lambda pe: pe.transpose(out, in_, ident), r, w)

    def act(self, out, in_, func, bias=None, scale=None, accum_out=None, r=(), w=()):
        def e(a):
            kw = {}
            if bias is not None:
                kw["bias"] = bias
            if scale is not None:
                kw["scale"] = scale
            if accum_out is not None:
                kw["accum_out"] = accum_out
            return a.activation(out=out, in_=in_, func=func, **kw)
        return self.add("act", e, r, w)

    def tt(self, eng, out, in0, in1, op, r=(), w=()):
        return self.add(eng, lambda v: v.tensor_tensor(out=out, in0=in0, in1=in1, op=op), r, w)

    def ts(self, eng, out, in0, s1, s2, op0, op1=None, r=(), w=()):
        def e(v):
            if op1 is None:
                return v.tensor_scalar(out=out, in0=in0, scalar1=s1, scalar2=None, op0=op0)
            return v.tensor_scalar(out=out, in0=in0, scalar1=s1, scalar2=s2, op0=op0, op1=op1)
        return self.add(eng, e, r, w)

    def stt(self, out, in0, scalar, in1, op0, op1, r=(), w=()):
        return self.add("dve", lambda v: v.scalar_tensor_tensor(out=out, in0=in0, scalar=scalar, in1=in1,
                                                               op0=op0, op1=op1), r, w)

    def cp(self, eng, out, in_, r=(), w=()):
        if eng == "act":
            return self.add(eng, lambda a: a.activation(out=out, in_=in_, func=AF.Copy), r, w)
        return self.add(eng, lambda v: v.tensor_copy(out=out, in_=in_), r, w)

    def memset(self, eng, ap, val, r=(), w=()):
        return self.add(eng, lambda v: v.memset(ap, val), r, w)

    def dma(self, eng, out, in_, r=(), w=()):
        return self.add(eng, lambda q: q.dma_start(out=out, in_=in_), r, w, dma=True)

    def emit(self, final_keys=()):
        nc = self.nc
        ops = self.ops
        n = len(ops)
        last_w, readers = {}, {}
        deps = [None] * n
        last_compute = {}
        pending_dma = []
        for i, op in enumerate(ops):
            if op["bar"]:
                dd = [j for e2, j in last_compute.items()
                      if (e2 != op["eng"] or (self.same_engine_sync and e2 != "pe"))]
                dd += pending_dma
                deps[i] = dd
                if op["eng"] == self.ENGS[-1]:
                    pending_dma = []
                continue
            d = set()
            for k in op["r"]:
                if k in last_w:
                    d.add(last_w[k])
            for k in op["w"]:
                if k in last_w:
                    d.add(last_w[k])
                for j in readers.get(k, {}).values():
                    d.add(j)
            d.discard(i)
            dd = []
            for j in d:
                oj = ops[j]
                if (not oj["dma"]) and (not op["dma"]) and oj["eng"] == op["eng"]:
                    if op["eng"] == "pe" or not self.same_engine_sync:
                        continue
                dd.append(j)
            deps[i] = dd
            rk_ = ("dma", i) if op["dma"] else op["eng"]
            for k in op["r"]:
                readers.setdefault(k, {})[rk_] = i
            for k in op["w"]:
                last_w[k] = i
                readers[k] = {}
            if op["dma"]:
                pending_dma.append(i)
            else:
                last_compute[op["eng"]] = i
        signaled = set()
        for i in range(n):
            for j in deps[i]:
                signaled.add(j)
        final_waits = []
        for k in final_keys:
            if k in last_w:
                signaled.add(last_w[k])
                final_waits.append(last_w[k])
        stack = contextlib.ExitStack()
        esem = {e: stack.enter_context(nc.semaphore("sem_" + e)) for e in ("pe", "act", "dve", "pool")}
        dsem = {e: [stack.enter_context(nc.semaphore("dsem_%s%d" % (e, k))) for k in range(NDMASEM)]
                for e in ("sp", "pool", "act")}
        ecount = {e: 0 for e in esem}
        sig, reuse_wait = {}, {}
        dma_n = {"sp": 0, "pool": 0, "act": 0}
        dma_i = 0
        for i, op in enumerate(ops):
            if op["bar"]:
                continue
            if op["dma"]:
                qe = op["eng"]
                di = dma_n[qe]
                k = di % NDMASEM
                val = 16 * (di // NDMASEM + 1)
                if di >= NDMASEM:
                    reuse_wait[i] = (dsem[qe][k], val - 16)
                sig[i] = (dsem[qe][k], val, 16)
                dma_n[qe] += 1
                dma_i += 1
            elif i in signaled:
                ecount[op["eng"]] += 1
                sig[i] = (esem[op["eng"]], ecount[op["eng"]], 1)
        self.stats = dict(nops=n, nsig=dict(ecount), ndma=dma_i)
        per_eng = {e: [] for e in self.ENGS}
        for i, op in enumerate(ops):
            per_eng[op["eng"]].append(i)

        def run_engine(ename, h):
            waited = {}

            def wait(sem, val):
                key = id(sem)
                if waited.get(key, 0) >= val:
                    return
                h.wait_ge(sem, val)
                waited[key] = val
            for i in per_eng[ename]:
                op = ops[i]
                need = {}
                for j in deps[i]:
                    s = sig[j]
                    if waited.get(id(s[0]), 0) < s[1] and need.get(id(s[0]), (None, 0))[1] < s[1]:
                        need[id(s[0])] = (s[0], s[1])
                if i in reuse_wait:
                    s = reuse_wait[i]
                    if waited.get(id(s[0]), 0) < s[1] and need.get(id(s[0]), (None, 0))[1] < s[1]:
                        need[id(s[0])] = (s[0], s[1])
                need = list(need.values())
                embed = None
                if need and EMBED_WAIT and not op["bar"] and not op["dma"]:
                    embed = need.pop()
                for s in need:
                    wait(s[0], s[1])
                if op["bar"]:
                    continue
                ins = op["emit"](h)
                if embed is not None:
                    ins._wait_ge(embed[0], embed[1])
                    waited[id(embed[0])] = embed[1]
                if i in sig:
                    ins.then_inc(sig[i][0], sig[i][2])
            if ename == "sp":
                for j in final_waits:
                    s = sig[j]
                    wait(s[0], s[1])

        with nc.Block() as block:
            @block.sync
            def _(h):
                run_engine("sp", h)

            @block.scalar
            def _(h):
                run_engine("act", h)

            @block.vector
            def _(h):
                run_engine("dve", h)

            @block.gpsimd
            def _(h):
                run_engine("pool", h)

            @block.tensor
            def _(h):
                run_engine("pe", h)
        stack.close()


PHASES = ("H", "RW", "ATT", "B", "MERGE", "MLP")


def build_program(stop_after="MLP", dbg=None, ngroups=None, no_cc=False):
    nc = bass.Bass("TRN2", target_bir_lowering=False)
    es = contextlib.ExitStack()

    def din(name, shape, dt=F32):
        return nc.dram_tensor(name, list(shape), dt, kind="ExternalInput").ap()

    xin = din("xin", [NTOK, D])
    cc_d = din("cc", [128, 16])
    w_ada = din("w_ada", [D, 6144])
    pv_d = din("pv", [128, PV_N])
    sinkb_d = din("sinkb", [128, 512])
    nfbc_d = din("nfbc", [128, D])
    w_att_d = din("w_att", [D, 1664])
    w_rw_d = din("w_rw", [D, 1792])
    loraw_d = din("loraw", [64, 2048])
    g2_d = din("g2", [96, 512])
    w_g_d = din("w_g", [D, 2048])
    w_ba_d = din("w_ba", [512, D])
    w_br_d = din("w_br", [512, D])
    w_out_d = din("w_out", [D, D])
    w_up_d = din("w_up", [D, 4096])
    w_down_d = din("w_down", [4096, D])
    cos_d = din("cosT", [128, 2176])
    sin_d = din("sinT", [128, 2176])
    cmb_d = din("cmb", [128, CM_BF_N])
    cmf_d = din("cmf", [128, CF_N])
    out_d = nc.dram_tensor("out", [NMAIN, D], F32, kind="ExternalOutput").ap()
    bops_d = nc.dram_tensor("bops", [64, 128, 384], BF16, kind="Internal").ap()
    cin_d = nc.dram_tensor("cin", [128, 256], F32, kind="Internal").ap()
    cout_d = nc.dram_tensor("cout", [256, 256], F32, kind="Internal").ap()
    dbg_out = {}

    ARENA = 207 * KB
    arena = es.enter_context(nc.sbuf_tensor("arena", [128, ARENA], U8))
    psb = [es.enter_context(nc.psum_tensor("psb%d" % i, [128, 512], F32))[:, :] for i in range(8)]

    def carve(off, shape, dt):
        esz = 4 if dt == F32 else 2
        n = 1
        for s in shape[1:]:
            n *= s
        assert off % 4 == 0 and off + n * esz <= ARENA, (off, shape)
        ap = arena[0:shape[0], off:off + n * esz].bitcast(dt)
        if len(shape) == 3:
            ap = ap.rearrange("p (a b) -> p a b", b=shape[2])
        elif len(shape) == 4:
            ap = ap.rearrange("p (a b c) -> p a b c", b=shape[2], c=shape[3])
        return ap

    class Bump:
        def __init__(self, lo, hi):
            self.lo, self.hi, self.cur = lo, hi, lo

        def __call__(self, shape, dt):
            esz = 4 if dt == F32 else 2
            n = 1
            for s in shape[1:]:
                n *= s
            nb = (n * esz + 31) // 32 * 32
            assert self.cur + nb <= self.hi, ("SBUF region overflow", self.cur, nb, self.hi)
            ap = carve(self.cur, shape, dt)
            self.cur += nb
            return ap

    p = Prog(nc)
    marks = {}

    def finish():
        import os as _os
        mo = _os.environ.get("MAXOPS")
        if mo:
            print("phase marks", marks, "total", len(p.ops))
            del p.ops[int(mo):]
        keys = [("dbgout", n) for n in dbg_out] + [("out", i) for i in range(16)]
        p.emit(final_keys=keys)
        es.close()
        return nc, dbg_out, p.stats

    def dump(name, ap, shape, dt=F32):
        if dbg is None or name not in dbg:
            return
        t = nc.dram_tensor("dbg_" + name, list(shape), dt, kind="ExternalOutput").ap()
        dbg_out[name] = t
        p.barrier()
        p.dma("sp", t, ap, w=[("dbgout", name)])
        p.barrier()

    def v3(ap, b):
        return ap.rearrange("p (a b) -> p a b", b=b)

    PB = Bump(0, 8 * KB)
    pv = PB([128, PV_N], F32)
    modT = PB([128, 48, 2], F32)
    sc1 = PB([128, 8, 2], F32)
    sc2 = PB([128, 8, 2], F32)
    scT = PB([128, 8, 2], F32)
    cmf = PB([128, CF_N], F32)
    ident_b = PB([128, 128], BF16)
    ones_b = PB([128, 128], BF16)
    sstat = [PB([128, 8], F32) for _ in range(6)]
    rl = [PB([128, 512], BF16) for _ in range(2)]
    cm_reset = cmf[:, CF_RESET:CF_RESET + 256]
    ident_f = cmf[:, CF_ID:CF_ID + 128]
    bones_f = cmf[:, CF_BONES:CF_BONES + 128]
    idz_f = cmf[:, CF_IDZ:CF_IDZ + 128]
    ones_f = cmf[:, CF_ONES:CF_ONES + 128]
    MB_ = Bump(8 * KB, 24 * KB)
    cmb = MB_([128, 2304], BF16)
    band_prev = cmb[:, CM_BPREV:CM_BPREV + 512]
    band_next = cmb[:, CM_BNEXT:CM_BNEXT + 512]
    MASK1 = [cmb[:, CM_MA1:CM_MA1 + 384], cmb[:, CM_MB1:CM_MB1 + 384]]
    MASK2 = [cmb[:, CM_MA2:CM_MA2 + 256], cmb[:, CM_MB2:CM_MB2 + 256]]
    Sst = [MB_([128, 4, 2, 64], BF16) for _ in range(2)]
    tmpS = MB_([128, 4, 64], F32)
    Sf32 = MB_([128, 256], F32)
    cslots = MB_([128, 2, 256], F32)
    hT = carve(24 * KB, [128, 8, NTOK], BF16)
    yaT = carve(63 * KB, [128, 4, NMAIN], BF16)
    yrT = carve(79 * KB, [128, 4, NMAIN], BF16)
    yacc = carve(95 * KB, [128, 16, 512], BF16)
    bonusT = carve(111 * KB, [128, 4, NMAIN], BF16)
    sgT = carve(127 * KB, [128, NMAIN], BF16)

    p.dma("sp", pv, pv_d, w=["pv"])
    p.dma("sp", cmf, cmf_d, w=["cmf"])
    p.dma("pool", cmb, cmb_d[:, 0:2304], w=["cmb"])
    p.dma("pool", ident_b, cmb_d[:, CM_IDB:CM_IDB + 128], w=["identb"])
    p.dma("pool", ones_b, cmb_d[:, CM_ONESB:CM_ONESB + 128], w=["onesb"])

    L = Bump(131 * KB, 207 * KB)
    w_rw_early = L([128, 8, 1792], BF16)
    w_rw_v = w_rw_d.rearrange("(kc p) n -> p kc n", p=128)
    for q4 in range(4):
        p.dma("pool", w_rw_early[:, :, q4 * 448:(q4 + 1) * 448], w_rw_v[:, :, q4 * 448:(q4 + 1) * 448], w=[("w_rw", q4)])
    ccs = L([128, 8, 2], F32)
    wst = [L([128, 8, 512], F32) for _ in range(2)]
    xst = [L([128, D], F32) for _ in range(2)]
    xnb = [L([128, D], BF16) for _ in range(2)]
    junk = L([128, D], BF16)
    p.dma("sp", ccs, cc_d.rearrange("p (a b) -> p a b", b=2), w=["ccs"])
    p.act(scT, ccs, AF.Silu, r=["ccs"], w=["scT"])
    w_ada_v = w_ada.rearrange("(kc p) n -> p kc n", p=128)
    ps_ada_lo = v3(psb[7][:, 0:32], 2)
    ps_ada_hi = v3(psb[6][:, 0:64], 2)

    def ada_ps(jj):
        return (ps_ada_lo[:, jj, :], ("ps", 7)) if jj < 16 else (ps_ada_hi[:, jj - 16, :], ("ps", 6))

    def ada_finish(j0, j1):
        src = ps_ada_lo[:, 0:16, :] if j0 == 0 else ps_ada_hi[:, 0:32, :]
        p.tt("dve", modT[:, j0:j1, :], src,
             pv[:, PV_BADA + j0:PV_BADA + j1].unsqueeze(2).to_broadcast([128, j1 - j0, 2]), ALU.add,
             r=[("ps", 7 if j0 == 0 else 6), "pv"], w=[("modT", j0)])

    def ada_block(jg):
        st_ = wst[jg % 2]
        p.dma("sp", st_, w_ada_v[:, :, jg * 512:(jg + 1) * 512], w=[("wst", jg % 2)])
        for j in range(4):
            jj = jg * 4 + j
            for kc in range(8):
                apo, apk = ada_ps(jj)
                p.mm(apo, st_[:, kc, j * 128:(j + 1) * 128], scT[:, kc, :],
                     start=(kc == 0), stop=(kc == 7), r=[("wst", jg % 2), "scT"], w=[apk])

    for jg in range(4):
        ada_block(jg)
    ada_finish(0, 16)
    p.stt(sc1, modT[:, 8:16, :], 1.0, pv[:, PV_G1:PV_G1 + 8].unsqueeze(2).to_broadcast([128, 8, 2]),
          ALU.add, ALU.mult, r=[("modT", 0), "pv"], w=["sc1"])

    def ada_tail():
        ada_finish(16, 48)
        p.stt(sc2, modT[:, 32:40, :], 1.0, pv[:, PV_G2:PV_G2 + 8].unsqueeze(2).to_broadcast([128, 8, 2]),
              ALU.add, ALU.mult, r=[("modT", 16), "pv"], w=["sc2"])
    MODK = ["sc1", "sc2", ("modT", 0), ("modT", 16)]

    def rms_rstd(src, ssq, eps, scale, keys_r, key_w, junk_ap, junk_key):
        p.act(junk_ap, src, AF.Square, accum_out=ssq, r=keys_r, w=[key_w, junk_key])
        p.ts("dve", ssq, ssq, scale, eps, ALU.mult, ALU.add, r=[key_w], w=[key_w])
        p.act(ssq, ssq, AF.Sqrt, r=[key_w], w=[key_w])
        p.add("dve", lambda v: v.reciprocal(out=ssq, in_=ssq), r=[key_w], w=[key_w])

    def transpose_modulate(xn_ap, xn_key, dst_fn, sc, sh, col, bank, dst_keys):
        pst = v3(psb[bank][:].bitcast(BF16), 128)
        for kc in range(8):
            p.tr(pst[:, kc, :], xn_ap[:, kc * 128:(kc + 1) * 128], ident_b, r=[xn_key, "identb"], w=[("ps", bank)])
        for kc in range(8):
            if kc % 2 == 0:
                p.act(dst_fn(kc), pst[:, kc, :], AF.Identity, bias=sh[:, kc, col:col + 1],
                      scale=sc[:, kc, col:col + 1], r=[("ps", bank)] + MODK, w=dst_keys)
            else:
                p.ts("dve", dst_fn(kc), pst[:, kc, :], sc[:, kc, col:col + 1], sh[:, kc, col:col + 1],
                     ALU.mult, ALU.add, r=[("ps", bank)] + MODK, w=dst_keys)

    for tt_ in range(19):
        xs = xst[tt_ % 2]
        xk = ("xst", tt_ % 2)
        ssq = sstat[tt_ % 3][:, 0:1]
        sk = ("sst", tt_ % 3)
        p.dma("sp", xs, xin[tt_ * 128:(tt_ + 1) * 128, :], w=[xk])
        rms_rstd(xs, ssq, NORM_EPS, 1.0 / D, [xk], sk, junk, "junk")
        xn = xnb[tt_ % 2]
        nk = ("xnb", tt_ % 2)
        p.ts("dve", xn, xs, ssq, None, ALU.mult, r=[xk, sk], w=[nk])
        col = 1 if tt_ < 2 else 0
        transpose_modulate(xn, nk, lambda kc, t=tt_: hT[:, kc, t * 128:(t + 1) * 128], sc1, modT[:, 0:8, :], col,
                           tt_ % 2, [("hT", tt_)])
        if tt_ % 2 == 1 and 4 + tt_ // 2 < 12:
            ada_block(4 + tt_ // 2)
    ada_tail()
    HT_ALL = [("hT", t) for t in range(19)]
    dump("hT", hT, [128, 8, NTOK], BF16)
    dump("modT", modT, [128, 48, 2], F32)
    if stop_after == "H":
        return finish()
    p.barrier()

    marks["RW"] = len(p.ops)
    L = Bump(131 * KB, 207 * KB)
    L2 = Bump(63 * KB, 95 * KB)
    w_rw = L([128, 8, 1792], BF16)
    loraw = L([64, 2048], BF16)
    ur = [L([128, 260], F32) for _ in range(2)]
    Cb = [[L([128, GT], F32) for _ in range(3)] for _ in range(2)]
    lo_t = L([128, GT], F32)
    hg_t = L([128, GT], F32)
    loraT = L([64, GT], BF16)
    T = [L([128, GT], F32) for _ in range(11)]
    BH = [L([128, GT], BF16) for _ in range(2)]
    KH = [L([128, GT], BF16) for _ in range(2)]
    vb = L([128, GT], BF16)
    WC = [[L([128, 4], F32) for _ in range(2)] for _ in range(2)]
    FM = [[[L2([128, 4, 128], BF16) for _ in range(2)] for _ in range(2)] for _ in range(2)]
    TMt = [[L([128, 7, 128], BF16) for _ in range(2)] for _ in range(2)]
    Wk = [[L2([128, 384], BF16) for _ in range(2)] for _ in range(4)]
    TX = [L2([128, GT], F32) for _ in range(2)]
    RbT = [L2([128, 128], BF16) for _ in range(4)]
    A2 = [L2([128, 256], BF16) for _ in range(4)]
    GU = [L2([128, 128], BF16) for _ in range(4)]
    BHBD = [L2([128, 128], BF16) for _ in range(4)]
    VBD = [L2([128, 128], BF16) for _ in range(4)]
    U0BD = [L2([128, 128], BF16) for _ in range(4)]
    bdm = v3(bones_f, 64)
    PPA = [L([128, 2, 128], BF16) for _ in range(2)]
    QTA = [L([128, 128], BF16) for _ in range(2)]
    BO = [L([128, 384], BF16) for _ in range(4)]

    WRW = [("w_rw", q4) for q4 in range(4)]
    p.dma("pool", loraw, loraw_d, w=["loraw"])
    p.memset("pool", Sst[0], 0.0, w=[("S", 0, j) for j in range(4)])
    sp_par = [0, 0, 0, 0]

    groups = [("ctx", 0)] + [("main", MAIN0 + GT * i) for i in range(NMAIN // GT)]
    if ngroups is not None:
        groups = groups[:ngroups]
    ucnt = [0]
    unit_cnt = [0]
    ycnt = [0]
    bocnt = [0]
    pmcnt = [0]

    def pm_region():
        k = pmcnt[0] % 2
        pmcnt[0] += 1
        return psb[1][:, k * 256:(k + 1) * 256], ("ps", 1)


    def unit_gen(st_, buf, j, pr, half, dpos, is_main, ppa, ppak, qta, qtak, bo, bok):
        hp = slice(half * 64, (half + 1) * 64)
        hc = hp
        ub, ubk = psb[4 + st_], ("ps", 4 + st_)
        fm = FM[buf][dpos][pr]
        fk = ("FM", buf, dpos, pr)
        tm = TMt[buf][pr]
        tmk = ("TM", buf, pr)
        R_, KKn, Bt, Kt, RK = fm[hp, 0, :], fm[hp, 1, :], fm[hp, 2, :], fm[hp, 3, :], fm[hp, 0:2, :]
        tpk = (half * 64, 0)
        rbt, rbtk = RbT[st_], ("RbT", st_)
        a2, a2k = A2[st_], ("A2", st_)
        gu, guk = GU[st_], ("GU", st_)
        wk0, wk1 = Wk[st_]
        wkk = [("Wk", st_, 0), ("Wk", st_, 1)]
        M1, M2 = MASK1[dpos], MASK2[dpos]
        p14 = ub[:, 0:384]
        p.mm(p14[:, 0:128], KKn, Bt, tp=tpk, r=[fk], w=[ubk])
        p.mm(p14[:, 128:384], Bt, RK, tp=tpk, r=[fk], w=[ubk])
        yield
        p.tt("dve", v3(wk0, 128)[:, 0:3:2, :], v3(p14, 128)[:, 0:3:2, :], v3(M1, 128)[:, 0:3:2, :], ALU.mult,
             r=[ubk, "cmb"], w=[wkk[0]])
        p.tt("dve", rbt, p14[:, 128:256], M1[:, 128:256], ALU.mult, r=[ubk, "cmb"], w=[rbtk])
        p25 = ub[:, 0:256]
        p.mm(p25, Kt, RK, tp=tpk, r=[fk], w=[ubk])
        p.tt("pool", v3(BHBD[st_], 64), tm[:, dpos * 3 + 1, hc].unsqueeze(1).to_broadcast([128, 2, 64]), bdm, ALU.mult,
             r=[tmk, "cmf"], w=[("BHBD", st_)])
        p.tt("pool", v3(VBD[st_], 64), tm[:, 6, hc].unsqueeze(1).to_broadcast([128, 2, 64]), bdm, ALU.mult,
             r=[tmk, "cmf"], w=[("VBD", st_)])
        yield
        p.tt("dve", a2, p25, M2, ALU.mult, r=[ubk, "cmb"], w=[a2k])
        P3 = ub[:, 256:320]
        p.mm(P3, a2[:, 128:256], tm[:, 6, hc], r=[a2k, tmk], w=[ubk])
        p.cp("act", wk0[:, 128:192], tm[:, dpos * 3, hc], r=[tmk], w=[wkk[0]])
        yield
        p.cp("act", wk0[:, 192:256], P3, r=[ubk], w=[wkk[0]])
        wks = [wk0, wk1]
        for lv in range(5):
            cur, nxt = wks[lv % 2], wks[(lv + 1) % 2]
            ck_, nk_ = wkk[lv % 2], wkk[(lv + 1) % 2]
            p.mm(p14[:, 0:256], cur[:, 256:384], cur[:, 0:256], r=[ck_], w=[ubk])
            p.mm(p14[:, 256:384], cur[:, 0:128], cur[:, 256:384], r=[ck_], w=[ubk])
            yield
            p.cp("act", v3(nxt, 128)[:, 0:3:2, :], v3(p14, 128)[:, 0:3:2, :], r=[ubk], w=[nk_])
            p.tt("dve", nxt[:, 128:256], p14[:, 128:256], cur[:, 128:256], ALU.add, r=[ubk, ck_], w=[nk_])
        cur, ck_ = wks[1], wkk[1]
        p.mm(p14[:, 128:256], cur[:, 256:384], cur[:, 128:256], r=[ck_], w=[ubk])
        yield
        p.tt("dve", gu, p14[:, 128:256], cur[:, 128:256], ALU.add, r=[ubk, ck_], w=[guk])
        bh, vb_, ub_ = BHBD[st_], VBD[st_], U0BD[st_]
        bhk, vbk, ubk_ = ("BHBD", st_), ("VBD", st_), ("U0BD", st_)
        p.tt("pool", v3(ub_, 64), gu[:, 64:128].unsqueeze(1).to_broadcast([128, 2, 64]), bdm, ALU.mult,
             r=[guk, "cmf"], w=[ubk_])
        tp2 = (0, half * 64)
        p.mm(p25[hp, 0:128], gu[:, 0:64], bh, tp=tp2, r=[guk, bhk], w=[ubk])
        p.mm(p25[hp, 128:256], tm[:, dpos * 3 + 1, hc], ub_, start=True, stop=False, tp=tp2, r=[tmk, ubk_], w=[ubk])
        p.mm(p25[hp, 128:256], tm[:, dpos * 3 + 2, hc], vb_, start=False, stop=True, tp=tp2, r=[tmk, vbk], w=[ubk])
        P6 = ub[:, 256:384]
        if is_main:
            p.mm(P6[hp, :], gu[:, 0:64], rbt, tp=(0, half * 64), r=[guk, rbtk], w=[ubk])
        yield
        for c2 in range(2):
            ci = pr * 2 + c2
            if dpos == 0:
                dst, dk = ppa[hp, c2, 0:64], ppak
            else:
                dst, dk = bo[hp, c2 * 128:c2 * 128 + 64], bok
            p.stt(dst, idz_f[hp, 0:64], WC[buf][dpos][hp, ci:ci + 1], p25[hp, c2 * 64:(c2 + 1) * 64], ALU.mult, ALU.add,
                  r=["cmf", ("WC", buf, dpos), ubk], w=[dk])
        if dpos == 0:
            dst, dk = ppa[hp, :, 64:128], ppak
        else:
            dst, dk = v3(bo[hp, 0:256], 128)[:, :, 64:128], bok
        p.cp("act", dst, v3(p25[hp, 128:256], 64), r=[ubk], w=[dk])
        if is_main:
            if dpos == 0:
                dst, dk = qta[hp, :], qtak
            else:
                dst, dk = bo[hp, 256:384], bok
            p.tt("dve", dst, P6[hp, :], R_, ALU.add, r=[ubk, fk], w=[dk])

    pbcnt = [0]
    LD = 0.6065306597126334
    stage_seq = []
    carry_post = [None]
    for gi, (kind, t0) in enumerate(groups):
        n = GT
        is_main = kind == "main"
        left_ok = is_main and t0 > MAIN0
        right_ok = is_main
        m0 = t0 - MAIN0
        dirs = (0, 1) if is_main else (0,)
        hkeys = [("hT", t0 // 128 + i) for i in range(3) if t0 // 128 + i < 19]
        if left_ok:
            hkeys.append(("hT", t0 // 128 - 1))

        def project_conv(cidx, dst, dst_key):
            k = ucnt[0] % 2
            ucnt[0] += 1
            lo_ = 1 if left_ok else 0
            ro_ = 1 if right_ok else 0
            ncol = n + lo_ + ro_
            psu = psb[0][:, 0:ncol]
            puk = ("ps", 0)
            u = ur[k]
            uk = ("ur", k)
            for kc in range(8):
                p.mm(psu, w_rw[:, kc, cidx * 128:(cidx + 1) * 128], hT[:, kc, t0 - lo_:t0 + n + ro_], start=(kc == 0),
                     stop=(kc == 7), r=WRW + hkeys, w=[puk])
            p.cp("act", u[:, 1 - lo_:1 - lo_ + ncol], psu, r=[puk], w=[uk])
            if not left_ok:
                p.memset("pool", u[:, 0:1], 0.0, w=[uk])
            if not right_ok:
                p.memset("pool", u[:, n + 1:n + 2], 0.0, w=[uk])
            cw = PV_CW + cidx * 3
            p.act(dst, u[:, 1:n + 1], AF.Identity, scale=pv[:, cw + 1:cw + 2], r=[uk, "pv"], w=[dst_key])
            p.stt(dst, u[:, 0:n], pv[:, cw:cw + 1], dst, ALU.mult, ALU.add, r=[uk, "pv", dst_key], w=[dst_key])
            p.stt(dst, u[:, 2:n + 2], pv[:, cw + 2:cw + 3], dst, ALU.mult, ALU.add, r=[uk, "pv", dst_key], w=[dst_key])

        p.cap_begin()
        project_conv(12, lo_t, "lo")
        p.act(loraT[0:32, :], lo_t[0:32, :], AF.Tanh, r=["lo"], w=["loraT"])
        p.cp("dve", loraT[32:64, :], lo_t[32:64, :], r=["lo"], w=["loraT"])
        if is_main:
            project_conv(13, hg_t, "hg")
            p.act(sgT[0:96, m0:m0 + n], hg_t[0:96, :], AF.Sigmoid, r=["hg"], w=[("sgT", gi)])

        for j in range(4):
            if j > 0:
                p.cap_begin()
            buf = (gi * 4 + j) % 2
            r_, k_, v_ = Cb[buf]
            ck = [("C", buf, i) for i in range(3)]
            for i3 in range(3):
                project_conv(3 * j + i3, Cb[buf][i3], ck[i3])
            tk = [("T", i) for i in range(11)]
            p.cp("act", vb, v_, r=[ck[2]], w=["vb"])
            SW, SWk = [T[1], TX[0]], [tk[1], ("TX", 0)]
            AA, AAk = [T[2], TX[1]], [tk[2], ("TX", 1)]
            for dpos in dirs:
                pmw, pmwk = pm_region()
                p.mm(pmw, loraw[0:64, (dpos * 2) * 512 + j * 128:(dpos * 2) * 512 + (j + 1) * 128], loraT[0:64, :],
                     r=["loraw", "loraT"], w=[pmwk])
                p.act(SW[dpos], pmw, AF.Sigmoid, bias=pv[:, PV_W0 + j * 2 + dpos:PV_W0 + j * 2 + dpos + 1],
                      r=[pmwk, "pv"], w=[SWk[dpos]])
                pma, pmak = pm_region()
                p.mm(pma, loraw[0:64, (dpos * 2 + 1) * 512 + j * 128:(dpos * 2 + 1) * 512 + (j + 1) * 128], loraT[0:64, :],
                     r=["loraw", "loraT"], w=[pmak])
                p.act(AA[dpos], pma, AF.Sigmoid, bias=pv[:, PV_A0 + j * 2 + dpos:PV_A0 + j * 2 + dpos + 1],
                      r=[pmak, "pv"], w=[AAk[dpos]])
            p.act(T[0], k_, AF.Identity, scale=pv[:, PV_KK + j:PV_KK + j + 1], r=[ck[1], "pv"], w=[tk[0]])
            p.act(T[4], T[0], AF.Square, r=[tk[0]], w=[tk[4]])
            pm, pmk = pm_region()
            p.mm(pm, bones_f, T[4], r=["cmf", tk[4]], w=[pmk])
            p.ts("dve", T[4], pm, 1e-24, None, ALU.max, r=[pmk], w=[tk[4]])
            p.act(T[4], T[4], AF.Ln, r=[tk[4]], w=[tk[4]])
            p.act(T[4], T[4], AF.Exp, scale=-0.5, r=[tk[4]], w=[tk[4]])
            p.tt("pool", T[0], T[0], T[4], ALU.mult, r=[tk[0], tk[4]], w=[tk[0]])
            for dpos in dirs:
                SWt, SWk_, AAt, AAk_ = SW[dpos], SWk[dpos], AA[dpos], AAk[dpos]
                p.add("dve", lambda v, o=T[3], a=cm_reset, b=SWt: v.tensor_tensor_scan(
                    out=o, data0=a, data1=b, initial=0.0, op0=ALU.mult, op1=ALU.add), r=["cmf", SWk_], w=[tk[3]])
                c3 = v3(T[3], 64)
                tot_bc = c3[:, :, 63:64].to_broadcast([128, 4, 64])
                if dpos == 0:
                    clu, cluk = T[3], tk[3]
                    p.act(T[5], T[3], AF.Exp, scale=-LD, r=[tk[3]], w=[tk[5]])
                    p.tt("pool", T[6], T[3], SWt, ALU.subtract, r=[tk[3], SWk_], w=[tk[6]])
                    p.act(T[6], T[6], AF.Exp, scale=-LD, r=[tk[6]], w=[tk[6]])
                    p.act(T[7], T[3], AF.Exp, scale=LD, r=[tk[3]], w=[tk[7]])
                    p.tt("pool", v3(T[8], 64), tot_bc, c3, ALU.subtract, r=[tk[3]], w=[tk[8]])
                    p.act(T[8], T[8], AF.Exp, scale=-LD, r=[tk[8]], w=[tk[8]])
                else:
                    p.tt("pool", T[4], SWt, T[3], ALU.subtract, r=[tk[3], SWk_], w=[tk[4]])
                    p.tt("pool", v3(T[4], 64), v3(T[4], 64), tot_bc, ALU.add, r=[tk[3], tk[4]], w=[tk[4]])
                    p.act(T[5], T[4], AF.Exp, scale=-LD, r=[tk[4]], w=[tk[5]])
                    p.tt("pool", v3(T[6], 64), tot_bc, c3, ALU.subtract, r=[tk[3]], w=[tk[6]])
                    p.act(T[6], T[6], AF.Exp, scale=-LD, r=[tk[6]], w=[tk[6]])
                    p.act(T[7], T[4], AF.Exp, scale=LD, r=[tk[4]], w=[tk[7]])
                    p.tt("pool", T[8], T[3], SWt, ALU.subtract, r=[tk[3], SWk_], w=[tk[8]])
                    p.act(T[8], T[8], AF.Exp, scale=-LD, r=[tk[8]], w=[tk[8]])
                wc = WC[buf][dpos]
                wck = ("WC", buf, dpos)
                p.act(wc.unsqueeze(2), c3[:, :, 63:64], AF.Exp, scale=-LD, r=[tk[3]], w=[wck])
                p.tt("pool", T[9], T[0], AAt, ALU.mult, r=[tk[0], AAk_], w=[tk[9]])
                p.ts("dve", AAt, AAt, -1.0, pv[:, PV_KA + j:PV_KA + j + 1], ALU.add, ALU.mult, r=[AAk_, "pv"], w=[AAk_])
                if dpos == 0:
                    KD, KDk = T[10], tk[10]
                else:
                    KD, KDk = AAt, AAk_
                p.stt(KD, AAt, 1.0, k_, ALU.add, ALU.mult, r=[AAk_, ck[1]], w=[KDk])
                for pr in range(2):
                    sl = slice(pr * 128, (pr + 1) * 128)
                    fm = FM[buf][dpos][pr]
                    fk = ("FM", buf, dpos, pr)
                    p.tt("pool", fm[:, 0, :], r_[:, sl], T[5][:, sl], ALU.mult, r=[ck[0], tk[5]], w=[fk])
                    p.stt(fm[:, 1, :], T[0][:, sl], -1.0, T[6][:, sl], ALU.mult, ALU.mult, r=[tk[0], tk[6]], w=[fk])
                    p.tt("pool", fm[:, 2, :], T[9][:, sl], T[7][:, sl], ALU.mult, r=[tk[9], tk[7]], w=[fk])
                    p.tt("dve", fm[:, 3, :], KD[:, sl], T[7][:, sl], ALU.mult, r=[KDk, tk[7]], w=[fk])
                p.tt("pool", BH[dpos], T[9], T[8], ALU.mult, r=[tk[9], tk[8]], w=[("BH", dpos)])
                p.tt("dve", KH[dpos], KD, T[8], ALU.mult, r=[KDk, tk[8]], w=[("KH", dpos)])
                if dpos == 1:
                    p.tt("pool", T[10], T[10], AAt, ALU.add, r=[tk[10], AAk_], w=[tk[10]])
            if is_main:
                p.stt(T[1], r_, pv[:, PV_RK + j:PV_RK + j + 1], T[10], ALU.mult, ALU.mult, r=[ck[0], "pv", tk[10]], w=[tk[1]])
                pm, pmk = pm_region()
                p.mm(pm, bones_f, T[1], r=["cmf", tk[1]], w=[pmk])
                p.tt("dve", bonusT[:, j, m0:m0 + n], pm, v_, ALU.mult, r=[pmk, ck[2]], w=[("bonus", gi, j)])
            for pr in range(2):
                sl = slice(pr * 128, (pr + 1) * 128)
                pst = v3(psb[1][:, 0:448].bitcast(BF16), 128)
                pstk = ("ps", 1)
                tmk = ("TM", buf, pr)
                srcs = [(0, FM[buf][0][pr][:, 1, :], ("FM", buf, 0, pr)), (1, BH[0][:, sl], ("BH", 0)),
                        (2, KH[0][:, sl], ("KH", 0)), (6, vb[:, sl], "vb")]
                if is_main:
                    srcs += [(3, FM[buf][1][pr][:, 1, :], ("FM", buf, 1, pr)), (4, BH[1][:, sl], ("BH", 1)),
                             (5, KH[1][:, sl], ("KH", 1))]
                for slot, src, skey in srcs:
                    p.tr(pst[:, slot, :], src, ident_b, r=[skey, "identb"], w=[pstk])
                if is_main:
                    p.cp("act" if pr == 0 else "dve", TMt[buf][pr], pst, r=[pstk], w=[tmk])
                else:
                    p.cp("act", TMt[buf][pr][:, 0:3, :], pst[:, 0:3, :], r=[pstk], w=[tmk])
                    p.cp("dve", TMt[buf][pr][:, 6, :], pst[:, 6, :], r=[pstk], w=[tmk])
            stage_seq.append(p.cap_end())
            p.cap_begin()
            if is_main:
                batches = [[(pr, half, dpos) for half in range(2) for dpos in (0, 1)] for pr in range(2)]
            else:
                batches = [[(pr, half, 0) for pr in range(2) for half in range(2)]]
            bp_list = []
            for batch in batches:
                p.cap_begin()
                gens = []
                uinfo = []
                pbuf = pbcnt[0] % 2
                pbcnt[0] += 1
                bos = None
                if is_main:
                    bos = bocnt[0] % 4
                    bocnt[0] += 1
                for st_, (pr, half, dpos) in enumerate(batch):
                    uinfo.append((st_, pr, half, dpos))
                    gens.append(unit_gen(st_, buf, j, pr, half, dpos, is_main, PPA[pr if not is_main else pbuf],
                                         ("PPA", pr if not is_main else pbuf), QTA[pbuf], ("QTA", pbuf),
                                         BO[bos] if is_main else None, ("BO", bos)))
                alive = list(gens)
                while alive:
                    nxt_alive = []
                    for g_ in alive:
                        try:
                            next(g_)
                            nxt_alive.append(g_)
                        except StopIteration:
                            pass
                    alive = nxt_alive
                bp_list.append(p.cap_end())
                p.cap_begin()
                prs = sorted(set(pr for (_, pr, _, _) in uinfo))
                for pr in prs:
                    gp = m0 // 128 + pr
                    tm = TMt[buf][pr]
                    tmk = ("TM", buf, pr)
                    ppa_i = pr if not is_main else pbuf
                    ppa, ppak = PPA[ppa_i], ("PPA", ppa_i)
                    qta, qtak = QTA[pbuf], ("QTA", pbuf)
                    ybank, ybk = psb[2], ("ps", 2)
                    p8bank, p8k = psb[3], ("ps", 3)
                    psy = ybank[:, 0:128]
                    first_y = [True]
                    if is_main:
                        for (st_, pr_u, half, dpos) in uinfo:
                            hc = slice(half * 64, (half + 1) * 64)
                            p.mm(psy[:, hc], RbT[st_], GU[st_][:, 64:128], start=first_y[0], stop=False, sg=True,
                                 r=[("RbT", st_), ("GU", st_)], w=[ybk])
                            first_y[0] = False
                            p.mm(psy[:, hc], A2[st_][:, 0:128], tm[:, 6, hc], start=False, stop=False, sg=True,
                                 r=[("A2", st_), tmk], w=[ybk])
                    if pr == prs[0]:
                        bp_list.append(p.cap_end())
                        p.cap_begin()
                    for c2 in range(2):
                        cr = slice(c2 * 64, (c2 + 1) * 64)
                        sin_, sout = Sst[sp_par[j]], Sst[1 - sp_par[j]]
                        sink_, soutk = ("S", sp_par[j], j), ("S", 1 - sp_par[j], j)
                        P8 = p8bank[:, c2 * 64:(c2 + 1) * 64]
                        p.mm(P8[0:64, :], ppa[0:64, c2, 0:64], sin_[0:64, j, 0, :], tp=(0, 0), r=[ppak, sink_], w=[p8k])
                        p.mm(P8[64:128, :], ppa[64:128, c2, 0:64], sin_[64:128, j, 1, :], tp=(64, 64), r=[ppak, sink_], w=[p8k])
                        if is_main:
                            p.mm(psy[cr, :], qta[:, c2 * 64:(c2 + 1) * 64], sin_[:, j].rearrange("p a b -> p (a b)"),
                                 start=False, sg=True, stop=(c2 == 1), tp=(0, c2 * 64), r=[qtak, sink_], w=[ybk])
                        p.tt("dve", tmpS[:, j, :], P8, ppa[:, c2, 64:128], ALU.add, r=[p8k, ppak], w=[("tmpS", j)])
                        p.tt("dve", sout[:, j], tmpS[:, j, :].unsqueeze(1).to_broadcast([128, 2, 64]), bdm, ALU.mult,
                             r=[("tmpS", j), "cmf"], w=[soutk])
                        sp_par[j] = 1 - sp_par[j]
                    if is_main:
                        p.cp("act", yacc[:, gp, j * 128:(j + 1) * 128], psy, r=[ybk], w=[("yacc", gp, j)])
                        p.dma("sp", bops_d[gp * 4 + j], BO[bos], r=[("BO", bos)], w=[("bops", gp, j)])
                bp_list.append(p.cap_end())
            p.ops.extend(Prog.merge(carry_post[0], bp_list[0]) if carry_post[0] else bp_list[0])
            p.ops.extend(bp_list[1])
            if len(bp_list) == 6:
                p.ops.extend(Prog.merge(bp_list[2], bp_list[3]))
                p.ops.extend(bp_list[4])
                carry_post[0] = bp_list[5]
            else:
                carry_post[0] = bp_list[2]
            stage_seq.append(p.cap_end())
    p.ops.extend(stage_seq[0])
    for k_ in range(1, len(stage_seq) - 1, 2):
        p.ops.extend(Prog.merge(stage_seq[k_], stage_seq[k_ + 1]))
    p.ops.extend(stage_seq[-1])
    p.ops.extend(carry_post[0])
    dump("yacc", yacc, [128, 16, 512], BF16)
    dump("bonusT", bonusT, [128, 4, NMAIN], BF16)
    for j in range(4):
        p.tt("dve", Sf32[:, j * 64:(j + 1) * 64], Sst[sp_par[j]][:, j, 0, :], Sst[sp_par[j]][:, j, 1, :], ALU.add,
             r=[("S", sp_par[j], j)], w=["Sf32"])
    dump("SA", Sf32, [128, 256], F32)
    p.dma("pool", cin_d, Sf32, r=["Sf32"], w=["cin"])
    groups_cc = [[0, 1], [2, 3], [4, 5], [6, 7]]
    if not no_cc:
      p.add("pool", lambda g: g.collective_compute("AllGather", ALU.bypass, replica_groups=groups_cc, ins=[cin_d],
                                                 outs=[cout_d]), r=["cin"], w=["cout"])
    p.dma("pool", cslots, cout_d.rearrange("(r p) c -> p r c", r=2), r=["cout"], w=["cslots"])
    if stop_after == "RW":
        return finish()
    p.barrier()

    L = Bump(131 * KB, 207 * KB)
    wch = [L([128, 8, 128], BF16) for _ in range(4)]
    qT = L([128, 4, NMAIN], BF16)
    kT = L([128, 2, 2176], BF16)
    kcT = L([128, 2, 256], BF16)
    v_sb = L([128, 19, 128], BF16)
    cosT = L([128, 2176], F32)
    sinT = L([128, 2176], F32)
    rt = [[L([128, 512], F32) for _ in range(2)] for _ in range(2)]
    PT = [L([128, 512], BF16) for _ in range(4)]
    esink = L([128, 512], F32)
    den = [L([128, 256], F32) for _ in range(2)]
    qbd = [L([128, 2, 128], BF16) for _ in range(4)]
    bdm128 = v3(bones_f, 64)[:, :, 0:1].to_broadcast([128, 2, 128])
    p.dma("sp", cosT, cos_d, w=["cosT"])
    p.dma("sp", sinT, sin_d, w=["sinT"])
    p.dma("sp", esink, sinkb_d, w=["esink"])
    p.act(esink, esink, AF.Exp, r=["esink"], w=["esink"])
    w_att_v = w_att_d.rearrange("(kc p) n -> p kc n", p=128)
    wcnt = [0]

    def load_wch(col0):
        k = wcnt[0] % 4
        wcnt[0] += 1
        p.dma("pool", wch[k], w_att_v[:, :, col0:col0 + 128], w=[("wch", k)])
        return wch[k], ("wch", k)

    pcnt = [0]

    def proj_rope(wa, wak, wb, wbk, tok0, ntok, dst, dst_key):
        k = pcnt[0] % 2
        pcnt[0] += 1
        pa, pak = psb[2 * k][:, 0:ntok], ("ps", 2 * k)
        pb, pbk = psb[2 * k + 1][:, 0:ntok], ("ps", 2 * k + 1)
        for kc in range(8):
            p.mm(pa, wa[:, kc, :], hT[:, kc, tok0:tok0 + ntok], start=(kc == 0), stop=(kc == 7), r=[wak] + HT_ALL, w=[pak])
        for kc in range(8):
            p.mm(pb, wb[:, kc, :], hT[:, kc, tok0:tok0 + ntok], start=(kc == 0), stop=(kc == 7), r=[wbk] + HT_ALL, w=[pbk])
        m = tok0 - MAIN0
        t1, t2 = rt[k]
        p.tt("dve", t1[:, 0:ntok], pa, cosT[:, m:m + ntok], ALU.mult, r=[pak, "cosT"], w=[("rt", k, 0)])
        p.tt("dve", t2[:, 0:ntok], pb, sinT[:, m:m + ntok], ALU.mult, r=[pbk, "sinT"], w=[("rt", k, 1)])
        p.tt("pool", dst, t1[:, 0:ntok], t2[:, 0:ntok], ALU.add, r=[("rt", k, 0), ("rt", k, 1)], w=[dst_key])

    wloads = [(c * 128, 512 + c * 128) for c in range(4)] + [(1024 + g * 128, 1280 + g * 128) for g in range(2)] + [(1536,)]
    wloaded = {}

    def ensure_w(n):
        if n < len(wloads) and n not in wloaded:
            wloaded[n] = [load_wch(col) for col in wloads[n]]

    ensure_w(0)
    for c in range(4):
        ensure_w(c + 1)
        (wa, wak), (wb, wbk) = wloaded[c]
        for tg in range(4):
            proj_rope(wa, wak, wb, wbk, MAIN0 + tg * 512, 512, qT[:, c, tg * 512:(tg + 1) * 512], ("qT", c, tg))
    for g in range(2):
        ensure_w(4 + g + 1)
        (wa, wak), (wb, wbk) = wloaded[4 + g]
        for tg in range(5):
            nt = 512 if tg < 4 else 128
            proj_rope(wa, wak, wb, wbk, MAIN0 + tg * 512, nt, kT[:, g, tg * 512:tg * 512 + nt], ("kT", g, tg))
        k = pcnt[0] % 2
        pcnt[0] += 1
        pa, pak = psb[2 * k][:, 0:256], ("ps", 2 * k)
        for kc in range(8):
            p.mm(pa, wa[:, kc, :], hT[:, kc, 0:256], start=(kc == 0), stop=(kc == 7), r=[wak] + HT_ALL, w=[pak])
        p.cp("act", kcT[:, g, :], pa, r=[pak], w=[("kcT", g)])
    (wv, wvk), = wloaded[6]
    for tt_ in range(19):
        k = tt_ % 2
        pa, pak = psb[4 + k][:, 0:128], ("ps", 4 + k)
        for kc in range(8):
            p.mm(pa, hT[:, kc, tt_ * 128:(tt_ + 1) * 128], wv[:, kc, :], start=(kc == 0), stop=(kc == 7),
                 r=[wvk] + HT_ALL, w=[pak])
        p.cp("act" if k == 0 else "dve", v_sb[:, tt_, :], pa, r=[pak], w=[("v_sb", tt_)])
    dump("qT", qT, [128, 4, NMAIN], BF16)
    dump("kT", kT, [128, 2, 2176], BF16)
    dump("v_sb", v_sb, [128, 19, 128], BF16)
    QK_ALL = [("qT", c, tg) for c in range(4) for tg in range(4)] + [("kT", g, tg) for g in range(2) for tg in range(5)] + \
             [("kcT", g) for g in range(2)] + [("v_sb", t) for t in range(19)]
    scnt = [0]
    ocnt = [0]
    def build_qb(i, g, ko):
        out_ = []
        for cqi in range(2):
            qi_ = (2 * ko + cqi) % 4
            p.tt("dve", qbd[qi_], qT[:, 2 * g + cqi, i * 128:(i + 1) * 128].unsqueeze(1).to_broadcast([128, 2, 128]),
                 bdm128, ALU.mult, r=QK_ALL + ["cmf"], w=[("qbd", qi_)])
            out_.append((qbd[qi_].rearrange("p a b -> p (a b)"), ("qbd", qi_)))
        return out_

    ig_list = [(i, g) for i in range(16) for g in range(2)]
    qb_next = build_qb(0, 0, 0)
    for n_ig, (i, g) in enumerate(ig_list):
        if True:
            kbs = [("c", 0), ("c", 1)] + ([("l", i - 1)] if i > 0 else []) + [("l", i), ("l", i + 1)]
            ko = ocnt[0] % 2
            ocnt[0] += 1
            pso, psok = psb[6 + ko], ("ps", 6 + ko)
            qb = qb_next
            def emit_scores(kk_, kb):
                ks = scnt[0] % 3
                kp = scnt[0] % 4
                scnt[0] += 1
                pss, pssk = psb[ks], ("ps", ks)
                if kk_ == "c":
                    ksrc = kcT[:, g, kb * 128:(kb + 1) * 128]
                    vt = v_sb[:, kb, g * 64:(g + 1) * 64]
                else:
                    ksrc = kT[:, g, kb * 128:(kb + 1) * 128]
                    vt = v_sb[:, 2 + kb, g * 64:(g + 1) * 64]
                for cqi in range(2):
                    p.mm(pss[:, cqi * 256:(cqi + 1) * 256], ksrc, qb[cqi][0], r=QK_ALL + [qb[cqi][1]], w=[pssk])
                return pss, pssk, kp, vt

            pendq = [emit_scores(*kbs[0]), emit_scores(*kbs[1])]
            for bi, (kk_, kb) in enumerate(kbs):
                pss, pssk, kp, vt = pendq.pop(0)
                if bi + 2 < len(kbs):
                    pendq.append(emit_scores(*kbs[bi + 2]))
                if bi + 3 == len(kbs) and n_ig + 1 < len(ig_list):
                    qb_next = build_qb(ig_list[n_ig + 1][0], ig_list[n_ig + 1][1], 1 - ko)
                pt, ptk = PT[kp], ("PT", kp)
                p.act(pt, pss, AF.Exp, scale=0.125, r=[pssk], w=[ptk])
                if kk_ == "l" and kb == i - 1:
                    p.tt("dve", pt, pt, band_prev, ALU.mult, r=[ptk, "cmb"], w=[ptk])
                if kk_ == "l" and kb == i + 1:
                    p.tt("dve", pt, pt, band_next, ALU.mult, r=[ptk, "cmb"], w=[ptk])
                pt4 = v3(pt, 128)
                first, last = bi == 0, bi == len(kbs) - 1
                for par in range(2):
                    ps_ = slice(par * 64, (par + 1) * 64)
                    p.mm(pso[ps_, 0:256], vt, pt4[:, par:4:2, :], start=first, stop=False, tp=(0, par * 64), sg=True,
                         r=[ptk] + QK_ALL, w=[psok])
                    p.mm(pso[ps_, 256:512], ones_b[:, 0:64], pt4[:, par:4:2, :], start=False, stop=last, tp=(0, par * 64),
                         sg=True, r=[ptk, "onesb"], w=[psok])
            dn, dnk = den[ko], ("den", ko)
            p.tt("dve", dn, pso[:, 256:512], esink[:, g * 256:(g + 1) * 256], ALU.add, r=[psok, "esink"], w=[dnk])
            p.add("dve", lambda v, o=dn: v.reciprocal(out=o, in_=o), r=[dnk], w=[dnk])
            p.tt("dve", yaT[:, 2 * g:2 * g + 2, i * 128:(i + 1) * 128], v3(pso[:, 0:256], 128), v3(dn, 128), ALU.mult,
                 r=[psok, dnk], w=[("yaT", i, g)])
    dump("yaT", yaT, [128, 4, NMAIN], BF16)
    if stop_after == "ATT":
        return finish()
    p.barrier()

    L = Bump(131 * KB, 207 * KB)
    g2 = L([96, 512], BF16)
    BOs = [L([128, 4, 384], BF16) for _ in range(3)]
    ytot = [L([128, 512], F32) for _ in range(2)]
    ycen = [L([128, 512], F32) for _ in range(2)]
    ysq = [L([128, 512], F32) for _ in range(2)]
    ynb = [L([128, 512], BF16) for _ in range(2)]
    lin = [L([128, 4, 128], F32) for _ in range(2)]
    gst = [[L([128, 8], F32) for _ in range(3)] for _ in range(2)]
    p.dma("pool", g2, g2_d, w=["g2"])
    p.ts("dve", Sf32, cslots[:, 0, :], pv[:, PV_FSEL:PV_FSEL + 1], None, ALU.mult, r=["cslots", "pv"], w=["Sf32"])
    p.stt(Sf32, cslots[:, 1, :], pv[:, PV_FSEL + 1:PV_FSEL + 2], Sf32, ALU.mult, ALU.add,
          r=["cslots", "pv", "Sf32"], w=["Sf32"])
    for j in range(4):
        p.tt("pool", Sst[0][:, j], Sf32[:, j * 64:(j + 1) * 64].unsqueeze(1).to_broadcast([128, 2, 64]), bdm, ALU.mult,
             r=["Sf32", "cmf"], w=["SB0"])
    dump("SB0", Sf32, [128, 256], F32)
    spb = 0
    SBK = ["SB0", "SB1"]
    b_rec, b_out = [], []

    def load_bo(gi2):
        gp2 = 15 - gi2
        p.dma("sp", BOs[gi2 % 3], bops_d[gp2 * 4:(gp2 + 1) * 4].rearrange("j p c -> p j c"),
              r=[("bops", gp2, j) for j in range(4)], w=[("BOs", gi2 % 3)])

    load_bo(0)
    load_bo(1)
    for gi_, gp in enumerate(range(15, -1, -1)):
        p.cap_begin()
        sl_ = gi_ % 3
        bo, bok = BOs[sl_], ("BOs", sl_)
        if gi_ + 2 < 16:
            load_bo(gi_ + 2)
        kb_ = gi_ % 2
        psyb, psybk = psb[kb_], ("ps", kb_)
        for c2 in (1, 0):
            cr = slice(c2 * 64, (c2 + 1) * 64)
            sin_, sout = Sst[spb], Sst[1 - spb]
            sink_, soutk = SBK[spb], SBK[1 - spb]
            P8 = psb[2 + c2][:, 0:256]
            P8k = ("ps", 2 + c2)
            for j in range(4):
                p.mm(psyb[cr, j * 128:(j + 1) * 128], bo[:, j, 256 + c2 * 64:256 + (c2 + 1) * 64],
                     sin_[:, j].rearrange("p a b -> p (a b)"), tp=(0, c2 * 64), r=[bok, sink_], w=[psybk])
            for j in range(4):
                for half in range(2):
                    hp = slice(half * 64, (half + 1) * 64)
                    p.mm(P8[hp, j * 64:(j + 1) * 64], bo[hp, j, c2 * 128:c2 * 128 + 64], sin_[hp, j, half, :],
                         tp=(half * 64, half * 64), r=[bok, sink_], w=[P8k])
            p.tt("dve", tmpS, v3(P8, 64), bo[:, :, c2 * 128 + 64:c2 * 128 + 128], ALU.add, r=[P8k, bok], w=["tmpSB"])
            p.tt("dve", sout, tmpS.unsqueeze(2).to_broadcast([128, 4, 2, 64]),
                 bdm.unsqueeze(1).to_broadcast([128, 4, 2, 64]), ALU.mult, r=["tmpSB", "cmf"], w=[soutk])
            spb = 1 - spb
        b_rec.append(p.cap_end())
        p.cap_begin()
        yt, ytk = ytot[kb_], ("ytot", kb_)
        p.tt("dve", yt, psyb, yacc[:, gp, :], ALU.add, r=[psybk] + [("yacc", gp, j) for j in range(4)], w=[ytk])
        y3 = v3(yt, 64)
        s1, s2, rs = gst[kb_]
        gk = ("gst", kb_)
        p.add("dve", lambda v, o=s1, i_=y3: v.reduce_sum(out=o, in_=i_, axis=AX.X), r=[ytk], w=[gk])
        p.ts("pool", s1, s1, 1.0 / 64, None, ALU.mult, r=[gk], w=[gk])
        yc, yck = ycen[kb_], ("ycen", kb_)
        p.tt("pool", v3(yc, 64), y3, s1.unsqueeze(2).to_broadcast([128, 8, 64]), ALU.subtract, r=[ytk, gk], w=[yck])
        sq, sqk = ysq[kb_], ("ysq", kb_)
        p.tt("pool", sq, yc, yc, ALU.mult, r=[yck], w=[sqk])
        p.add("dve", lambda v, o=s2, i_=v3(sq, 64): v.reduce_sum(out=o, in_=i_, axis=AX.X), r=[sqk], w=[gk])
        p.ts("dve", rs, s2, 1.0 / 64, LNX_EPS, ALU.mult, ALU.add, r=[gk], w=[gk])
        p.act(rs, rs, AF.Sqrt, r=[gk], w=[gk])
        p.add("dve", lambda v, o=rs: v.reciprocal(out=o, in_=o), r=[gk], w=[gk])
        yn, ynk = ynb[kb_], ("ynb", kb_)
        p.tt("dve", v3(yn, 64), v3(yc, 64), rs.unsqueeze(2).to_broadcast([128, 8, 64]), ALU.mult, r=[yck, gk], w=[ynk])
        pst = v3(psb[4 + kb_][:, 0:256].bitcast(BF16), 128)
        pstk = ("ps", 4 + kb_)
        for j in range(4):
            p.tr(pst[:, j, :], yn[:, j * 128:(j + 1) * 128], ident_b, r=[ynk, "identb"], w=[pstk])
        psg, psgk = psb[6 + kb_], ("ps", 6 + kb_)
        for j in range(4):
            p.mm(psg[:, j * 128:(j + 1) * 128], g2[0:96, j * 128:(j + 1) * 128], sgT[0:96, gp * 128:(gp + 1) * 128],
                 r=["g2"] + [("sgT", 1 + gp // 2)], w=[psgk])
        ln_, lnk = lin[kb_], ("lin", kb_)
        for j in range(4):
            p.act(ln_[:, j, :], pst[:, j, :], AF.Identity, bias=pv[:, PV_LB + j:PV_LB + j + 1],
                  scale=pv[:, PV_LW + j:PV_LW + j + 1], r=[pstk, "pv"], w=[lnk])
        p.tt("pool", ln_, ln_, bonusT[:, :, gp * 128:(gp + 1) * 128], ALU.add,
             r=[lnk] + [("bonus", 1 + gp // 2, j) for j in range(4)], w=[lnk])
        p.tt("dve", yrT[:, :, gp * 128:(gp + 1) * 128], ln_, v3(psg[:, :], 128), ALU.mult, r=[lnk, psgk], w=[("yrT", gp)])
        b_out.append(p.cap_end())
    p.ops.extend(b_rec[0])
    for k_ in range(16):
        if k_ + 1 < 16:
            p.ops.extend(Prog.merge(b_out[k_], b_rec[k_ + 1]))
        else:
            p.ops.extend(b_out[k_])
    dump("yrT", yrT, [128, 4, NMAIN], BF16)
    if stop_after == "B":
        return finish()
    p.barrier()

    w_g = carve(95 * KB, [128, 8, 2048], BF16)
    w_ba = carve(127 * KB, [128, 4, D], BF16)
    w_br = carve(135 * KB, [128, 4, D], BF16)
    L = Bump(143 * KB, 175 * KB)
    mergedT = carve(175 * KB, [128, 8, NMAIN], BF16)
    gat = [[L([128, 512], F32) for _ in range(2)] for _ in range(2)]
    mt = [[L([128, 512], F32) for _ in range(2)] for _ in range(2)]
    w_g_v = w_g_d.rearrange("(kc p) n -> p kc n", p=128)
    for q4 in range(4):
        p.dma("pool", w_g[:, :, q4 * 512:(q4 + 1) * 512], w_g_v[:, :, q4 * 512:(q4 + 1) * 512], w=[("w_g", q4)])
    p.dma("pool", w_ba, w_ba_d.rearrange("(c p) n -> p c n", p=128), w=["w_ba"])
    p.dma("pool", w_br, w_br_d.rearrange("(c p) n -> p c n", p=128), w=["w_br"])
    WG = [("w_g", q4) for q4 in range(4)]
    YA = [("yaT", i, g) for i in range(16) for g in range(2)]
    YR = [("yrT", gp) for gp in range(16)]
    mc = 0
    for gm in range(4):
        tok = MAIN0 + gm * 512
        m0 = gm * 512
        for dc in range(8):
            s_ = mc % 2
            mc += 1
            pga, pgr, pza, pzr = psb[4 * s_], psb[4 * s_ + 1], psb[4 * s_ + 2], psb[4 * s_ + 3]
            kga, kgr, kza, kzr = [("ps", 4 * s_ + i) for i in range(4)]
            for kc in range(8):
                p.mm(pga, w_g[:, kc, dc * 128:(dc + 1) * 128], hT[:, kc, tok:tok + 512], start=(kc == 0), stop=(kc == 7),
                     r=WG + HT_ALL, w=[kga])
            for kc in range(8):
                p.mm(pgr, w_g[:, kc, 1024 + dc * 128:1024 + (dc + 1) * 128], hT[:, kc, tok:tok + 512], start=(kc == 0),
                     stop=(kc == 7), r=WG + HT_ALL, w=[kgr])
            for c in range(4):
                p.mm(pza, w_ba[:, c, dc * 128:(dc + 1) * 128], yaT[:, c, m0:m0 + 512], start=(c == 0), stop=(c == 3),
                     r=["w_ba"] + YA, w=[kza])
            for c in range(4):
                p.mm(pzr, w_br[:, c, dc * 128:(dc + 1) * 128], yrT[:, c, m0:m0 + 512], start=(c == 0), stop=(c == 3),
                     r=["w_br"] + YR, w=[kzr])
            ga, gr = gat[s_]
            p.act(ga, pga, AF.Sigmoid, r=[kga], w=[("gat", s_, 0)])
            p.act(gr, pgr, AF.Sigmoid, r=[kgr], w=[("gat", s_, 1)])
            m1, m2 = mt[s_]
            p.tt("dve", m1, pza, ga, ALU.mult, r=[kza, ("gat", s_, 0)], w=[("mt", s_, 0)])
            p.tt("dve", m2, pzr, gr, ALU.mult, r=[kzr, ("gat", s_, 1)], w=[("mt", s_, 1)])
            p.tt("pool", mergedT[:, dc, m0:m0 + 512], m1, m2, ALU.add, r=[("mt", s_, 0), ("mt", s_, 1)], w=[("merged", gm, dc)])
    dump("mergedT", mergedT, [128, 8, NMAIN], BF16)
    if stop_after == "MERGE":
        return finish()
    p.barrier()

    w_up = carve(8 * KB, [128, 8, 4096], BF16)
    w_down = carve(72 * KB, [128, 32, D], BF16)
    w_out = carve(136 * KB, [128, 8, D], BF16)
    nfbc = carve(152 * KB, [128, D], F32)
    g1bc = carve(8 * KB, [128, D], F32)
    g2bc = carve(12 * KB, [128, D], F32)
    dgt = carve(16 * KB, [128, 128], F32)
    stg = [carve(156 * KB, [128, 2, D], F32), carve(164 * KB, [128, 2, D], F32)]
    x1b = [carve(156 * KB, [128, D], F32), carve(160 * KB, [128, D], F32)]
    xn2b = [carve(164 * KB, [128, D], BF16), carve(166 * KB, [128, D], BF16)]
    h2Tb = [carve(168 * KB, [128, 8, 128], BF16), carve(170 * KB, [128, 8, 128], BF16)]
    actR = [carve(172 * KB, [128, 4, 128], BF16), carve(173 * KB, [128, 4, 128], BF16)]
    p.dma("sp", nfbc, nfbc_d, w=["nfbc"])
    for gi_, (gbc, jbase, gkey) in enumerate(((g1bc, 16, "g1bc"), (g2bc, 40, "g2bc"))):
        for kc in range(8):
            bank = 2 * gi_ + kc // 4
            p.ts("dve", dgt, ident_f, modT[:, jbase + kc, 0:1], None, ALU.mult, r=["cmf"] + MODK, w=["dgt"])
            p.mm(psb[bank][:, (kc % 4) * 128:(kc % 4 + 1) * 128], ones_f, dgt, r=["cmf", "dgt"], w=[("ps", bank)])
        p.cp("act", gbc[:, 0:512], psb[2 * gi_], r=[("ps", 2 * gi_)], w=[gkey])
        p.cp("act", gbc[:, 512:1024], psb[2 * gi_ + 1], r=[("ps", 2 * gi_ + 1)], w=[gkey])
    w_out_v = w_out_d.rearrange("(kc p) n -> p kc n", p=128)
    w_down_v = w_down_d.rearrange("(f p) n -> p f n", p=128)
    w_up_v = w_up_d.rearrange("(kc p) n -> p kc n", p=128)
    for kc in range(2, 8):
        p.dma("pool", w_up[:, kc, :], w_up_v[:, kc, :], w=[("w_up", kc)])
    sc_ = 0
    for q in range(4):
        s_, sk_ = stg[sc_ % 2], ("stg", sc_ % 2)
        sc_ += 1
        p.dma("sp", s_, w_out_v[:, 2 * q:2 * q + 2, :], w=[sk_])
        p.tt("dve", w_out[:, 2 * q:2 * q + 2, :], s_, g1bc.unsqueeze(1).to_broadcast([128, 2, D]), ALU.mult,
             r=[sk_, "g1bc"], w=[("w_out", q)])
    for q in range(16):
        s_, sk_ = stg[sc_ % 2], ("stg", sc_ % 2)
        sc_ += 1
        p.dma("sp", s_, w_down_v[:, 2 * q:2 * q + 2, :], w=[sk_])
        p.tt("dve", w_down[:, 2 * q:2 * q + 2, :], s_, g2bc.unsqueeze(1).to_broadcast([128, 2, D]),
             ALU.mult, r=[sk_, "g2bc"], w=[("w_down", q)])
    for kc in range(2):
        p.dma("pool", w_up[:, kc, :], w_up_v[:, kc, :], r=["g1bc", "g2bc", "dgt"],
              w=[("w_up", kc), "g1bc", "g2bc", "dgt"])
    p.barrier()
    WOUT = [("w_out", q) for q in range(4)]
    WDN = [("w_down", q) for q in range(16)]
    WUP = [("w_up", kc) for kc in range(8)]
    def mlp_head(tI):
        b = tI % 2
        m = tI * 128
        x1, xk = x1b[b], ("x1", b)
        xn2, xnk = xn2b[b], ("xn2", b)
        h2T, hk = h2Tb[b], ("h2T", b)
        p.dma("sp", x1, xin[MAIN0 + m:MAIN0 + m + 128, :], w=[xk])
        for hf in range(2):
            ps_, psk = psb[hf], ("ps", hf)
            for dc in range(8):
                p.mm(ps_, mergedT[:, dc, m:m + 128], w_out[:, dc, hf * 512:(hf + 1) * 512], start=(dc == 0), stop=(dc == 7),
                     r=WOUT + [("merged", tI // 4, dc)], w=[psk])
            p.tt("dve", x1[:, hf * 512:(hf + 1) * 512], ps_, x1[:, hf * 512:(hf + 1) * 512], ALU.add, r=[psk, xk], w=[xk])
        ssq = sstat[b][:, 0:1]
        rms_rstd(x1, ssq, NORM_EPS, 1.0 / D, [xk], ("ssq2", b), xn2, xnk)
        p.ts("dve", xn2, x1, ssq, None, ALU.mult, r=[xk, ("ssq2", b)], w=[xnk])
        transpose_modulate(xn2, xnk, lambda kc: h2T[:, kc, :], sc2, modT[:, 24:32, :], 0, 2, [hk])

    def mlp_mid(tI):
        b = tI % 2
        h2T, hk = h2Tb[b], ("h2T", b)

        UPB = (3, 4, 1)

        def up_mm(fq):
            bank = UPB[fq % 3]
            for kc in range(8):
                p.mm(psb[bank], h2T[:, kc, :], w_up[:, kc, fq * 512:(fq + 1) * 512], start=(kc == 0), stop=(kc == 7),
                     r=WUP + [hk], w=[("ps", bank)])

        up_mm(0)
        up_mm(1)
        for fq in range(8):
            bank = UPB[fq % 3]
            ps_, psk = psb[bank], ("ps", bank)
            r_, rk_ = rl[fq % 2], ("rl", fq % 2)
            p.act(r_, ps_, AF.Relu, r=[psk], w=[rk_])
            if fq + 2 < 8:
                up_mm(fq + 2)
            p.tt("pool", r_, r_, r_, ALU.mult, r=[rk_], w=[rk_])
            pstT = v3(psb[7][:, 0:256].bitcast(BF16), 128)
            for f4 in range(4):
                p.tr(pstT[:, f4, :], r_[:, f4 * 128:(f4 + 1) * 128], ident_b, r=[rk_, "identb"], w=[("ps", 7)])
            aR, aRk = actR[fq % 2], ("actR", fq % 2)
            p.cp("act" if fq % 2 else "dve", aR, pstT, r=[("ps", 7)], w=[aRk])
            for hf in range(2):
                for f4 in range(4):
                    f = fq * 4 + f4
                    p.mm(psb[5 + hf], aR[:, f4, :], w_down[:, f, hf * 512:(hf + 1) * 512], start=(f == 0), stop=(f == 31),
                         r=WDN + [aRk], w=[("ps", 5 + hf)])

    def mlp_tail(tI):
        b = tI % 2
        m = tI * 128
        x1, xk = x1b[b], ("x1", b)
        xn2, xnk = xn2b[b], ("xn2", b)
        for hf in range(2):
            ps_, psk = psb[5 + hf], ("ps", 5 + hf)
            p.tt("dve", x1[:, hf * 512:(hf + 1) * 512], ps_, x1[:, hf * 512:(hf + 1) * 512], ALU.add, r=[psk, xk], w=[xk])
        ssq3 = sstat[2 + b][:, 0:1]
        rms_rstd(x1, ssq3, NORM_EPS, 1.0 / D, [xk], ("ssq3", b), xn2, xnk)
        p.stt(x1, x1, ssq3, nfbc, ALU.mult, ALU.mult, r=[xk, ("ssq3", b), "nfbc"], w=[xk])
        p.dma("sp", out_d[m:m + 128, :], x1, r=[xk], w=[("out", tI)])

    mlp_head(0)
    for tI in range(16):
        mlp_mid(tI)
        if tI + 1 < 16:
            mlp_head(tI + 1)
        mlp_tail(tI)
    return finish()


def _bd(blk):
    m = np.zeros((128, 128), np.float32)
    m[0:64, 0:64] = blk
    m[64:128, 64:128] = blk
    return m


def _const_tables():
    i = np.arange(64)
    row, col = i[:, None], i[None, :]
    sl = (col < row).astype(np.float32)
    su = (col > row).astype(np.float32)
    iu = (col >= row).astype(np.float32)
    il = (col <= row).astype(np.float32)
    cmb = np.zeros((128, CM_BF_N), np.float32)
    kj = np.arange(128)[:, None]
    qi = np.arange(128)[None, :]
    cmb[:, CM_BPREV:CM_BPREV + 512] = np.tile((kj >= qi).astype(np.float32), (1, 4))
    cmb[:, CM_BNEXT:CM_BNEXT + 512] = np.tile((kj <= qi).astype(np.float32), (1, 4))
    cmb[:, CM_MA1:CM_MA1 + 384] = np.concatenate([_bd(sl), _bd(iu), _bd(su)], 1)
    cmb[:, CM_MB1:CM_MB1 + 384] = np.concatenate([_bd(su), _bd(il), _bd(sl)], 1)
    cmb[:, CM_MA2:CM_MA2 + 256] = np.concatenate([_bd(iu), _bd(su)], 1)
    cmb[:, CM_MB2:CM_MB2 + 256] = np.concatenate([_bd(il), _bd(sl)], 1)
    cmb[:, CM_IDB:CM_IDB + 128] = np.eye(128, dtype=np.float32)
    cmb[:, CM_ONESB:CM_ONESB + 128] = 1.0
    cmf = np.zeros((128, CF_N), np.float32)
    rs = np.ones(256, np.float32)
    rs[0::64] = 0.0
    cmf[:, CF_RESET:CF_RESET + 256] = rs[None, :]
    cmf[:, CF_ID:CF_ID + 128] = np.eye(128, dtype=np.float32)
    cmf[:, CF_BONES:CF_BONES + 128] = _bd(np.ones((64, 64), np.float32))
    cmf[:, CF_IDZ:CF_IDZ + 64] = np.tile(np.eye(64, dtype=np.float32), (2, 1))
    cmf[:, CF_ONES:CF_ONES + 128] = 1.0
    return cmb, cmf


def _rope_tables(half):
    m = np.arange(2176)
    pos = m if half == 0 else 4095 - m
    n_freq = 16
    inv_freq = np.power(np.float32(10000.0), -np.arange(n_freq, dtype=np.float32) / n_freq).astype(np.float32)
    rowp = (pos // 64).astype(np.float32)
    colp = (pos % 64).astype(np.float32)
    ang = np.concatenate([rowp[:, None] * inv_freq, colp[:, None] * inv_freq], axis=-1).astype(np.float32)
    cos, sin = np.cos(ang).astype(np.float32), np.sin(ang).astype(np.float32)
    cosd = np.concatenate([cos, cos], 1).T
    sind = np.concatenate([-sin, sin], 1).T
    return np.ascontiguousarray(np.tile(cosd, (2, 1))), np.ascontiguousarray(np.tile(sind, (2, 1)))


def _fm(vec, n):
    return np.ascontiguousarray(np.asarray(vec, np.float32).reshape(n, 128).T)


def prep_inputs(x, c, ctx, c_ctx, w_ada, b_ada, norm1_g, w_in, sink, conv_w, decay_w0, decay_w2,
                iclr_a0, iclr_a2, gate_g2, k_k, k_a, r_k, lnx_w, lnx_b, w_branch_attn, w_branch_rwkv,
                w_out, norm2_g, w_mlp_up, w_mlp_down, norm_f_g):
    f = lambda a: np.asarray(a, np.float32)
    x, c, ctx, c_ctx = f(x), f(c), f(ctx), f(c_ctx)
    w_in = f(w_in)[0]
    conv_w = f(conv_w)[0]
    cmb, cmf = _const_tables()
    QO, KO, VO, RWO, GO = 0, 512, 640, 768, 2464
    def swap_halves(w, nheads):
        w4 = w.reshape(1024, nheads, 2, 32)
        return w4[:, :, ::-1, :].reshape(1024, nheads * 64)
    wq = w_in[:, QO:QO + 512]
    wk = w_in[:, KO:KO + 128]
    wv = w_in[:, VO:VO + 128]
    wkp = swap_halves(wk, 2)
    kd = np.concatenate([wk[:, 0:64], wk[:, 0:64], wk[:, 64:128], wk[:, 64:128]], 1)
    kpd = np.concatenate([wkp[:, 0:64], wkp[:, 0:64], wkp[:, 64:128], wkp[:, 64:128]], 1)
    w_att = np.ascontiguousarray(np.concatenate([wq, swap_halves(wq, 8), kd, kpd, wv], 1))
    rw = w_in[:, RWO:RWO + 1696]
    w_rw = np.zeros((1024, 1792), np.float32)
    rwcol = np.full((14, 128), -1, np.int64)
    for j in range(4):
        for i3 in range(3):
            rwcol[3 * j + i3] = i3 * 512 + j * 128 + np.arange(128)
    rwcol[12, 0:32] = 1536 + np.arange(32)
    rwcol[12, 32:64] = 1568 + np.arange(32)
    rwcol[13, 0:96] = 1600 + np.arange(96)
    for ci in range(14):
        for pp in range(128):
            if rwcol[ci, pp] >= 0:
                w_rw[:, ci * 128 + pp] = rw[:, rwcol[ci, pp]]
    w_g = np.ascontiguousarray(w_in[:, GO:GO + 2048])
    g2 = np.ascontiguousarray(f(gate_g2)[0])
    nfbc = np.ascontiguousarray(np.tile(f(norm_f_g)[None, :], (128, 1)))
    w0, a0 = f(decay_w0)[0], f(iclr_a0)[0]
    w2, a2 = f(decay_w2)[0], f(iclr_a2)[0]
    in_maps = []
    for core in range(8):
        b, half = core // 2, core % 2
        if half == 0:
            xs = x[b, 0:2176]
            cs = ctx[b]
        else:
            xs = x[b, 4095:1919:-1]
            cs = ctx[b, ::-1]
        xin = np.ascontiguousarray(np.concatenate([cs, xs], 0))
        cc = np.zeros((128, 8, 2), np.float32)
        cc[:, :, 0] = c[b].reshape(8, 128).T
        cc[:, :, 1] = c_ctx.reshape(8, 128).T
        pv = np.zeros((128, PV_N), np.float32)
        pv[:, PV_BADA:PV_BADA + 48] = _fm(f(b_ada)[0], 48)
        pv[:, PV_G1:PV_G1 + 8] = _fm(f(norm1_g)[0], 8)
        pv[:, PV_G2:PV_G2 + 8] = _fm(f(norm2_g)[0], 8)
        dirs = (half, 1 - half)
        for dpos, dd in enumerate(dirs):
            pv[:, PV_W0 + dpos:PV_W0 + 8:2] = _fm(w0[dd], 4)
            pv[:, PV_A0 + dpos:PV_A0 + 8:2] = _fm(a0[dd], 4)
        pv[:, PV_KK:PV_KK + 4] = _fm(f(k_k)[0], 4)
        pv[:, PV_KA:PV_KA + 4] = _fm(f(k_a)[0], 4)
        pv[:, PV_RK:PV_RK + 4] = _fm(f(r_k)[0].reshape(512), 4)
        pv[:, PV_LW:PV_LW + 4] = _fm(f(lnx_w)[0], 4)
        pv[:, PV_LB:PV_LB + 4] = _fm(f(lnx_b)[0], 4)
        for ci in range(14):
            for tap in range(3):
                tp_ = tap if half == 0 else 2 - tap
                valid = rwcol[ci] >= 0
                pv[valid, PV_CW + ci * 3 + tap] = conv_w[tp_, rwcol[ci][valid]]
        pv[:, PV_FSEL] = float(half)
        pv[:, PV_FSEL + 1] = float(1 - half)
        loraw = np.zeros((64, 2, 2, 512), np.float32)
        for dpos, dd in enumerate(dirs):
            loraw[0:32, dpos, 0, :] = w2[dd]
            loraw[32:64, dpos, 1, :] = a2[dd]
        sk = f(sink)[0]
        sinkb = np.zeros((128, 2, 2, 128), np.float32)
        for g in range(2):
            for rr in range(2):
                for par in range(2):
                    sinkb[par * 64:(par + 1) * 64, g, rr, :] = sk[4 * g + 2 * rr + par]
        cosT, sinT = _rope_tables(half)
        in_maps.append({
            "xin": xin, "cc": cc.reshape(128, 16), "w_ada": f(w_ada)[0], "pv": pv, "sinkb": sinkb.reshape(128, 512),
            "nfbc": nfbc, "w_att": w_att, "w_rw": w_rw, "loraw": loraw.reshape(64, 2048), "g2": g2, "w_g": w_g,
            "w_ba": f(w_branch_attn)[0], "w_br": f(w_branch_rwkv)[0], "w_out": f(w_out)[0], "w_up": f(w_mlp_up)[0],
            "w_down": f(w_mlp_down)[0], "cosT": cosT, "sinT": sinT, "cmb": cmb, "cmf": cmf,
        })
    return in_maps


_NC_CACHE = {}


def kernel(**inputs):
    in_maps = prep_inputs(**inputs)
    if "nc" not in _NC_CACHE:
        _NC_CACHE["nc"] = build_program()[0]
    nc = _NC_CACHE["nc"]
    res = run_bass_kernel_spmd(nc, in_maps, core_ids=list(range(8)))
    out = np.zeros((4, 4096, 1024), np.float32)
    for core in range(8):
        b, half = core // 2, core % 2
        y = np.asarray(res.results[core]["out"], np.float32)
        if half == 0:
            out[b, 0:2048] = y
        else:
            out[b, 2048:4096] = y[::-1]
    return out
```

```python
import contextlib
import numpy as np
import concourse.bass as bass
import concourse.mybir as mybir
from concourse.bass_utils import run_bass_kernel_spmd

F32 = mybir.dt.float32
BF16 = mybir.dt.bfloat16
U8 = mybir.dt.uint8
AF = mybir.ActivationFunctionType
ALU = mybir.AluOpType
AX = mybir.AxisListType
KB = 1024
NDMASEM = 16
EMBED_WAIT = True

D = 1024
NTOK = 2432
CTX0, MAIN0, HALO0 = 0, 256, 2304
NMAIN = 2048
GT = 256
NORM_EPS = 1e-6
LNX_EPS = 1e-5 * 64

PV_BADA, PV_G1, PV_G2, PV_W0, PV_A0, PV_KK, PV_KA, PV_RK, PV_LW, PV_LB, PV_CW, PV_FSEL, PV_N = \
    0, 48, 56, 64, 72, 80, 84, 88, 92, 96, 100, 142, 144
CM_BPREV, CM_BNEXT, CM_MA1, CM_MB1, CM_MA2, CM_MB2, CM_IDB, CM_ONESB, CM_BF_N = 0, 512, 1024, 1408, 1792, 2048, 2304, 2432, 2560
CF_RESET, CF_ID, CF_BONES, CF_IDZ, CF_ONES, CF_N = 0, 256, 384, 512, 640, 768


class Prog:
    ENGS = ("pe", "act", "dve", "pool", "sp")

    def __init__(self, nc, same_engine_sync=True):
        self.nc = nc
        self.ops = []
        self.same_engine_sync = same_engine_sync

    def add(self, eng, emit, r=(), w=(), dma=False):
        self.ops.append(dict(eng=eng, emit=emit, r=tuple(r), w=tuple(w), dma=dma, bar=False))
        return len(self.ops) - 1

    def cap_begin(self):
        if not hasattr(self, "_cap_stack"):
            self._cap_stack = []
        self._cap_stack.append(self.ops)
        self.ops = []

    def cap_end(self):
        cap = self.ops
        self.ops = self._cap_stack.pop()
        return cap

    @staticmethod
    def merge(a, b):
        out = []
        ia = ib = 0
        na, nb = len(a), len(b)
        while ia < na or ib < nb:
            if ib >= nb or (ia < na and ia * nb <= ib * na):
                out.append(a[ia]); ia += 1
            else:
                out.append(b[ib]); ib += 1
        return out

    def barrier(self):
        for e in self.ENGS:
            self.ops.append(dict(eng=e, emit=None, r=(), w=(), dma=False, bar=True))

    def mm(self, out, lhsT, rhs, start=True, stop=True, tp=None, r=(), w=(), sg=False):
        def e(pe):
            kw = {}
            if tp is not None:
                kw["tile_position"] = tp
            if sg:
                kw["skip_group_check"] = True
            return pe.matmul(out, lhsT=lhsT, rhs=rhs, start=start, stop=stop, **kw)
        return self.add("pe", e, r, w)

    def tr(self, out, in_, ident, r=(), w=()):
        return self.add("pe", lambda pe: pe.transpose(out, in_, ident), r, w)

    def act(self, out, in_, func, bias=None, scale=None, accum_out=None, r=(), w=()):
        def e(a):
            kw = {}
            if bias is not None:
                kw["bias"] = bias
            if scale is not None:
                kw["scale"] = scale
            if accum_out is not None:
                kw["accum_out"] = accum_out
            return a.activation(out=out, in_=in_, func=func, **kw)
        return self.add("act", e, r, w)

    def tt(self, eng, out, in0, in1, op, r=(), w=()):
        return self.add(eng, lambda v: v.tensor_tensor(out=out, in0=in0, in1=in1, op=op), r, w)

    def ts(self, eng, out, in0, s1, s2, op0, op1=None, r=(), w=()):
        def e(v):
            if op1 is None:
                return v.tensor_scalar(out=out, in0=in0, scalar1=s1, scalar2=None, op0=op0)
            return v.tensor_scalar(out=out, in0=in0, scalar1=s1, scalar2=s2, op0=op0, op1=op1)
        return self.add(eng, e, r, w)

    def stt(self, out, in0, scalar, in1, op0, op1, r=(), w=()):
        return self.add("dve", lambda v: v.scalar_tensor_tensor(out=out, in0=in0, scalar=scalar, in1=in1,
                                                               op0=op0, op1=op1), r, w)

    def cp(self, eng, out, in_, r=(), w=()):
        if eng == "act":
            return self.add(eng, lambda a: a.activation(out=out, in_=in_, func=AF.Copy), r, w)
        return self.add(eng, lambda v: v.tensor_copy(out=out, in_=in_), r, w)

    def memset(self, eng, ap, val, r=(), w=()):
        return self.add(eng, lambda v: v.memset(ap, val), r, w)

    def dma(self, eng, out, in_, r=(), w=()):
        return self.add(eng, lambda q: q.dma_start(out=out, in_=in_), r, w, dma=True)

    def emit(self, final_keys=()):
        nc = self.nc
        ops = self.ops
        n = len(ops)
        last_w, readers = {}, {}
        deps = [None] * n
        last_compute = {}
        pending_dma = []
        for i, op in enumerate(ops):
            if op["bar"]:
                dd = [j for e2, j in last_compute.items()
                      if (e2 != op["eng"] or (self.same_engine_sync and e2 != "pe"))]
                dd += pending_dma
                deps[i] = dd
                if op["eng"] == self.ENGS[-1]:
                    pending_dma = []
                continue
            d = set()
            for k in op["r"]:
                if k in last_w:
                    d.add(last_w[k])
            for k in op["w"]:
                if k in last_w:
                    d.add(last_w[k])
                for j in readers.get(k, {}).values():
                    d.add(j)
            d.discard(i)
            dd = []
            for j in d:
                oj = ops[j]
                if (not oj["dma"]) and (not op["dma"]) and oj["eng"] == op["eng"]:
                    if op["eng"] == "pe" or not self.same_engine_sync:
                        continue
                dd.append(j)
            deps[i] = dd
            rk_ = ("dma", i) if op["dma"] else op["eng"]
            for k in op["r"]:
                readers.setdefault(k, {})[rk_] = i
            for k in op["w"]:
                last_w[k] = i
                readers[k] = {}
            if op["dma"]:
                pending_dma.append(i)
            else:
                last_compute[op["eng"]] = i
        signaled = set()
        for i in range(n):
            for j in deps[i]:
                signaled.add(j)
        final_waits = []
        for k in final_keys:
            if k in last_w:
                signaled.add(last_w[k])
                final_waits.append(last_w[k])
        stack = contextlib.ExitStack()
        esem = {e: stack.enter_context(nc.semaphore("sem_" + e)) for e in ("pe", "act", "dve", "pool")}
        dsem = {e: [stack.enter_context(nc.semaphore("dsem_%s%d" % (e, k))) for k in range(NDMASEM)]
                for e in ("sp", "pool", "act")}
        ecount = {e: 0 for e in esem}
        sig, reuse_wait = {}, {}
        dma_n = {"sp": 0, "pool": 0, "act": 0}
        dma_i = 0
        for i, op in enumerate(ops):
            if op["bar"]:
                continue
            if op["dma"]:
                qe = op["eng"]
                di = dma_n[qe]
                k = di % NDMASEM
                val = 16 * (di // NDMASEM + 1)
                if di >= NDMASEM:
                    reuse_wait[i] = (dsem[qe][k], val - 16)
                sig[i] = (dsem[qe][k], val, 16)
                dma_n[qe] += 1
                dma_i += 1
            elif i in signaled:
                ecount[op["eng"]] += 1
                sig[i] = (esem[op["eng"]], ecount[op["eng"]], 1)
        self.stats = dict(nops=n, nsig=dict(ecount), ndma=dma_i)
        per_eng = {e: [] for e in self.ENGS}
        for i, op in enumerate(ops):
            per_eng[op["eng"]].append(i)

        def run_engine(ename, h):
            waited = {}

            def wait(sem, val):
                key = id(sem)
                if waited.get(key, 0) >= val:
                    return
                h.wait_ge(sem, val)
                waited[key] = val
            for i in per_eng[ename]:
                op = ops[i]
                need = {}
                for j in deps[i]:
                    s = sig[j]
                    if waited.get(id(s[0]), 0) < s[1] and need.get(id(s[0]), (None, 0))[1] < s[1]:
                        need[id(s[0])] = (s[0], s[1])
                if i in reuse_wait:
                    s = reuse_wait[i]
                    if waited.get(id(s[0]), 0) < s[1] and need.get(id(s[0]), (None, 0))[1] < s[1]:
                        need[id(s[0])] = (s[0], s[1])
                need = list(need.values())
                embed = None
                if need and EMBED_WAIT and not op["bar"] and not op["dma"]:
                    embed = need.pop()
                for s in need:
                    wait(s[0], s[1])
                if op["bar"]:
                    continue
                ins = op["emit"](h)
                if embed is not None:
                    ins._wait_ge(embed[0], embed[1])
                    waited[id(embed[0])] = embed[1]
                if i in sig:
                    ins.then_inc(sig[i][0], sig[i][2])
            if ename == "sp":
                for j in final_waits:
                    s = sig[j]
                    wait(s[0], s[1])

        with nc.Block() as block:
            @block.sync
            def _(h):
                run_engine("sp", h)

            @block.scalar
            def _(h):
                run_engine("act", h)

            @block.vector
            def _(h):
                run_engine("dve", h)

            @block.gpsimd
            def _(h):
                run_engine("pool", h)

            @block.tensor
            def _(h):
                run_engine("pe", h)
        stack.close()


PHASES = ("H", "RW", "ATT", "B", "MERGE", "MLP")


def build_program(stop_after="MLP", dbg=None, ngroups=None, no_cc=False):
    nc = bass.Bass("TRN2", target_bir_lowering=False)
    es = contextlib.ExitStack()

    def din(name, shape, dt=F32):
        return nc.dram_tensor(name, list(shape), dt, kind="ExternalInput").ap()

    xin = din("xin", [NTOK, D])
    cc_d = din("cc", [128, 16])
    w_ada = din("w_ada", [D, 6144])
    pv_d = din("pv", [128, PV_N])
    sinkb_d = din("sinkb", [128, 512])
    nfbc_d = din("nfbc", [128, D])
    w_att_d = din("w_att", [D, 1664])
    w_rw_d = din("w_rw", [D, 1792])
    loraw_d = din("loraw", [64, 2048])
    g2_d = din("g2", [96, 512])
    w_g_d = din("w_g", [D, 2048])
    w_ba_d = din("w_ba", [512, D])
    w_br_d = din("w_br", [512, D])
    w_out_d = din("w_out", [D, D])
    w_up_d = din("w_up", [D, 4096])
    w_down_d = din("w_down", [4096, D])
    cos_d = din("cosT", [128, 2176])
    sin_d = din("sinT", [128, 2176])
    cmb_d = din("cmb", [128, CM_BF_N])
    cmf_d = din("cmf", [128, CF_N])
    out_d = nc.dram_tensor("out", [NMAIN, D], F32, kind="ExternalOutput").ap()
    bops_d = nc.dram_tensor("bops", [64, 128, 384], BF16, kind="Internal").ap()
    cin_d = nc.dram_tensor("cin", [128, 256], F32, kind="Internal").ap()
    cout_d = nc.dram_tensor("cout", [256, 256], F32, kind="Internal").ap()
    dbg_out = {}

    ARENA = 207 * KB
    arena = es.enter_context(nc.sbuf_tensor("arena", [128, ARENA], U8))
    psb = [es.enter_context(nc.psum_tensor("psb%d" % i, [128, 512], F32))[:, :] for i in range(8)]

    def carve(off, shape, dt):
        esz = 4 if dt == F32 else 2
        n = 1
        for s in shape[1:]:
            n *= s
        assert off % 4 == 0 and off + n * esz <= ARENA, (off, shape)
        ap = arena[0:shape[0], off:off + n * esz].bitcast(dt)
        if len(shape) == 3:
            ap = ap.rearrange("p (a b) -> p a b", b=shape[2])
        elif len(shape) == 4:
            ap = ap.rearrange("p (a b c) -> p a b c", b=shape[2], c=shape[3])
        return ap

    class Bump:
        def __init__(self, lo, hi):
            self.lo, self.hi, self.cur = lo, hi, lo

        def __call__(self, shape, dt):
            esz = 4 if dt == F32 else 2
            n = 1
            for s in shape[1:]:
                n *= s
            nb = (n * esz + 31) // 32 * 32
            assert self.cur + nb <= self.hi, ("SBUF region overflow", self.cur, nb, self.hi)
            ap = carve(self.cur, shape, dt)
            self.cur += nb
            return ap

    p = Prog(nc)
    marks = {}

    def finish():
        import os as _os
        mo = _os.environ.get("MAXOPS")
        if mo:
            print("phase marks", marks, "total", len(p.ops))
            del p.ops[int(mo):]
        keys = [("dbgout", n) for n in dbg_out] + [("out", i) for i in range(16)]
        p.emit(final_keys=keys)
        es.close()
        return nc, dbg_out, p.stats

    def dump(name, ap, shape, dt=F32):
        if dbg is None or name not in dbg:
            return
        t = nc.dram_tensor("dbg_" + name, list(shape), dt, kind="ExternalOutput").ap()
        dbg_out[name] = t
        p.barrier()
        p.dma("sp", t, ap, w=[("dbgout", name)])
        p.barrier()

    def v3(ap, b):
        return ap.rearrange("p (a b) -> p a b", b=b)

    PB = Bump(0, 8 * KB)
    pv = PB([128, PV_N], F32)
    modT = PB([128, 48, 2], F32)
    sc1 = PB([128, 8, 2], F32)
    sc2 = PB([128, 8, 2], F32)
    scT = PB([128, 8, 2], F32)
    cmf = PB([128, CF_N], F32)
    ident_b = PB([128, 128], BF16)
    ones_b = PB([128, 128], BF16)
    sstat = [PB([128, 8], F32) for _ in range(6)]
    rl = [PB([128, 512], BF16) for _ in range(2)]
    cm_reset = cmf[:, CF_RESET:CF_RESET + 256]
    ident_f = cmf[:, CF_ID:CF_ID + 128]
    bones_f = cmf[:, CF_BONES:CF_BONES + 128]
    idz_f = cmf[:, CF_IDZ:CF_IDZ + 128]
    ones_f = cmf[:, CF_ONES:CF_ONES + 128]
    MB_ = Bump(8 * KB, 24 * KB)
    cmb = MB_([128, 2304], BF16)
    band_prev = cmb[:, CM_BPREV:CM_BPREV + 512]
    band_next = cmb[:, CM_BNEXT:CM_BNEXT + 512]
    MASK1 = [cmb[:, CM_MA1:CM_MA1 + 384], cmb[:, CM_MB1:CM_MB1 + 384]]
    MASK2 = [cmb[:, CM_MA2:CM_MA2 + 256], cmb[:, CM_MB2:CM_MB2 + 256]]
    Sst = [MB_([128, 4, 2, 64], BF16) for _ in range(2)]
    tmpS = MB_([128, 4, 64], F32)
    Sf32 = MB_([128, 256], F32)
    cslots = MB_([128, 2, 256], F32)
    hT = carve(24 * KB, [128, 8, NTOK], BF16)
    yaT = carve(63 * KB, [128, 4, NMAIN], BF16)
    yrT = carve(79 * KB, [128, 4, NMAIN], BF16)
    yacc = carve(95 * KB, [128, 16, 512], BF16)
    bonusT = carve(111 * KB, [128, 4, NMAIN], BF16)
    sgT = carve(127 * KB, [128, NMAIN], BF16)

    p.dma("sp", pv, pv_d, w=["pv"])
    p.dma("sp", cmf, cmf_d, w=["cmf"])
    p.dma("pool", cmb, cmb_d[:, 0:2304], w=["cmb"])
    p.dma("pool", ident_b, cmb_d[:, CM_IDB:CM_IDB + 128], w=["identb"])
    p.dma("pool", ones_b, cmb_d[:, CM_ONESB:CM_ONESB + 128], w=["onesb"])

    L = Bump(131 * KB, 207 * KB)
    w_rw_early = L([128, 8, 1792], BF16)
    w_rw_v = w_rw_d.rearrange("(kc p) n -> p kc n", p=128)
    for q4 in range(4):
        p.dma("pool", w_rw_early[:, :, q4 * 448:(q4 + 1) * 448], w_rw_v[:, :, q4 * 448:(q4 + 1) * 448], w=[("w_rw", q4)])
    ccs = L([128, 8, 2], F32)
    wst = [L([128, 8, 512], F32) for _ in range(2)]
    xst = [L([128, D], F32) for _ in range(2)]
    xnb = [L([128, D], BF16) for _ in range(2)]
    junk = L([128, D], BF16)
    p.dma("sp", ccs, cc_d.rearrange("p (a b) -> p a b", b=2), w=["ccs"])
    p.act(scT, ccs, AF.Silu, r=["ccs"], w=["scT"])
    w_ada_v = w_ada.rearrange("(kc p) n -> p kc n", p=128)
    ps_ada_lo = v3(psb[7][:, 0:32], 2)
    ps_ada_hi = v3(psb[6][:, 0:64], 2)

    def ada_ps(jj):
        return (ps_ada_lo[:, jj, :], ("ps", 7)) if jj < 16 else (ps_ada_hi[:, jj - 16, :], ("ps", 6))

    def ada_finish(j0, j1):
        src = ps_ada_lo[:, 0:16, :] if j0 == 0 else ps_ada_hi[:, 0:32, :]
        p.tt("dve", modT[:, j0:j1, :], src,
             pv[:, PV_BADA + j0:PV_BADA + j1].unsqueeze(2).to_broadcast([128, j1 - j0, 2]), ALU.add,
             r=[("ps", 7 if j0 == 0 else 6), "pv"], w=[("modT", j0)])

    def ada_block(jg):
        st_ = wst[jg % 2]
        p.dma("sp", st_, w_ada_v[:, :, jg * 512:(jg + 1) * 512], w=[("wst", jg % 2)])
        for j in range(4):
            jj = jg * 4 + j
            for kc in range(8):
                apo, apk = ada_ps(jj)
                p.mm(apo, st_[:, kc, j * 128:(j + 1) * 128], scT[:, kc, :],
                     start=(kc == 0), stop=(kc == 7), r=[("wst", jg % 2), "scT"], w=[apk])

    for jg in range(4):
        ada_block(jg)
    ada_finish(0, 16)
    p.stt(sc1, modT[:, 8:16, :], 1.0, pv[:, PV_G1:PV_G1 + 8].unsqueeze(2).to_broadcast([128, 8, 2]),
          ALU.add, ALU.mult, r=[("modT", 0), "pv"], w=["sc1"])

    def ada_tail():
        ada_finish(16, 48)
        p.stt(sc2, modT[:, 32:40, :], 1.0, pv[:, PV_G2:PV_G2 + 8].unsqueeze(2).to_broadcast([128, 8, 2]),
              ALU.add, ALU.mult, r=[("modT", 16), "pv"], w=["sc2"])
    MODK = ["sc1", "sc2", ("modT", 0), ("modT", 16)]

    def rms_rstd(src, ssq, eps, scale, keys_r, key_w, junk_ap, junk_key):
        p.act(junk_ap, src, AF.Square, accum_out=ssq, r=keys_r, w=[key_w, junk_key])
        p.ts("dve", ssq, ssq, scale, eps, ALU.mult, ALU.add, r=[key_w], w=[key_w])
        p.act(ssq, ssq, AF.Sqrt, r=[key_w], w=[key_w])
        p.add("dve", lambda v: v.reciprocal(out=ssq, in_=ssq), r=[key_w], w=[key_w])

    def transpose_modulate(xn_ap, xn_key, dst_fn, sc, sh, col, bank, dst_keys):
        pst = v3(psb[bank][:].bitcast(BF16), 128)
        for kc in range(8):
            p.tr(pst[:, kc, :], xn_ap[:, kc * 128:(kc + 1) * 128], ident_b, r=[xn_key, "identb"], w=[("ps", bank)])
        for kc in range(8):
            if kc % 2 == 0:
                p.act(dst_fn(kc), pst[:, kc, :], AF.Identity, bias=sh[:, kc, col:col + 1],
                      scale=sc[:, kc, col:col + 1], r=[("ps", bank)] + MODK, w=dst_keys)
            else:
                p.ts("dve", dst_fn(kc), pst[:, kc, :], sc[:, kc, col:col + 1], sh[:, kc, col:col + 1],
                     ALU.mult, ALU.add, r=[("ps", bank)] + MODK, w=dst_keys)

    for tt_ in range(19):
        xs = xst[tt_ % 2]
        xk = ("xst", tt_ % 2)
        ssq = sstat[tt_ % 3][:, 0:1]
        sk = ("sst", tt_ % 3)
        p.dma("sp", xs, xin[tt_ * 128:(tt_ + 1) * 128, :], w=[xk])
        rms_rstd(xs, ssq, NORM_EPS, 1.0 / D, [xk], sk, junk, "junk")
        xn = xnb[tt_ % 2]
        nk = ("xnb", tt_ % 2)
        p.ts("dve", xn, xs, ssq, None, ALU.mult, r=[xk, sk], w=[nk])
        col = 1 if tt_ < 2 else 0
        transpose_modulate(xn, nk, lambda kc, t=tt_: hT[:, kc, t * 128:(t + 1) * 128], sc1, modT[:, 0:8, :], col,
                           tt_ % 2, [("hT", tt_)])
        if tt_ % 2 == 1 and 4 + tt_ // 2 < 12:
            ada_block(4 + tt_ // 2)
    ada_tail()
    HT_ALL = [("hT", t) for t in range(19)]
    dump("hT", hT, [128, 8, NTOK], BF16)
    dump("modT", modT, [128, 48, 2], F32)
    if stop_after == "H":
        return finish()
    p.barrier()

    marks["RW"] = len(p.ops)
    L = Bump(131 * KB, 207 * KB)
    L2 = Bump(63 * KB, 95 * KB)
    w_rw = L([128, 8, 1792], BF16)
    loraw = L([64, 2048], BF16)
    ur = [L([128, 260], F32) for _ in range(2)]
    Cb = [[L([128, GT], F32) for _ in range(3)] for _ in range(2)]
    lo_t = L([128, GT], F32)
    hg_t = L([128, GT], F32)
    loraT = L([64, GT], BF16)
    T = [L([128, GT], F32) for _ in range(11)]
    BH = [L([128, GT], BF16) for _ in range(2)]
    KH = [L([128, GT], BF16) for _ in range(2)]
    vb = L([128, GT], BF16)
    WC = [[L([128, 4], F32) for _ in range(2)] for _ in range(2)]
    FM = [[[L2([128, 4, 128], BF16) for _ in range(2)] for _ in range(2)] for _ in range(2)]
    TMt = [[L([128, 7, 128], BF16) for _ in range(2)] for _ in range(2)]
    Wk = [[L2([128, 384], BF16) for _ in range(2)] for _ in range(4)]
    TX = [L2([128, GT], F32) for _ in range(2)]
    RbT = [L2([128, 128], BF16) for _ in range(4)]
    A2 = [L2([128, 256], BF16) for _ in range(4)]
    GU = [L2([128, 128], BF16) for _ in range(4)]
    BHBD = [L2([128, 128], BF16) for _ in range(4)]
    VBD = [L2([128, 128], BF16) for _ in range(4)]
    U0BD = [L2([128, 128], BF16) for _ in range(4)]
    bdm = v3(bones_f, 64)
    PPA = [L([128, 2, 128], BF16) for _ in range(2)]
    QTA = [L([128, 128], BF16) for _ in range(2)]
    BO = [L([128, 384], BF16) for _ in range(4)]

    WRW = [("w_rw", q4) for q4 in range(4)]
    p.dma("pool", loraw, loraw_d, w=["loraw"])
    p.memset("pool", Sst[0], 0.0, w=[("S", 0, j) for j in range(4)])
    sp_par = [0, 0, 0, 0]

    groups = [("ctx", 0)] + [("main", MAIN0 + GT * i) for i in range(NMAIN // GT)]
    if ngroups is not None:
        groups = groups[:ngroups]
    ucnt = [0]
    unit_cnt = [0]
    ycnt = [0]
    bocnt = [0]
    pmcnt = [0]

    def pm_region():
        k = pmcnt[0] % 2
        pmcnt[0] += 1
        return psb[1][:, k * 256:(k + 1) * 256], ("ps", 1)


    def unit_gen(st_, buf, j, pr, half, dpos, is_main, ppa, ppak, qta, qtak, bo, bok):
        hp = slice(half * 64, (half + 1) * 64)
        hc = hp
        ub, ubk = psb[4 + st_], ("ps", 4 + st_)
        fm = FM[buf][dpos][pr]
        fk = ("FM", buf, dpos, pr)
        tm = TMt[buf][pr]
        tmk = ("TM", buf, pr)
        R_, KKn, Bt, Kt, RK = fm[hp, 0, :], fm[hp, 1, :], fm[hp, 2, :], fm[hp, 3, :], fm[hp, 0:2, :]
        tpk = (half * 64, 0)
        rbt, rbtk = RbT[st_], ("RbT", st_)
        a2, a2k = A2[st_], ("A2", st_)
        gu, guk = GU[st_], ("GU", st_)
        wk0, wk1 = Wk[st_]
        wkk = [("Wk", st_, 0), ("Wk", st_, 1)]
        M1, M2 = MASK1[dpos], MASK2[dpos]
        p14 = ub[:, 0:384]
        p.mm(p14[:, 0:128], KKn, Bt, tp=tpk, r=[fk], w=[ubk])
        p.mm(p14[:, 128:384], Bt, RK, tp=tpk, r=[fk], w=[ubk])
        yield
        p.tt("dve", v3(wk0, 128)[:, 0:3:2, :], v3(p14, 128)[:, 0:3:2, :], v3(M1, 128)[:, 0:3:2, :], ALU.mult,
             r=[ubk, "cmb"], w=[wkk[0]])
        p.tt("dve", rbt, p14[:, 128:256], M1[:, 128:256], ALU.mult, r=[ubk, "cmb"], w=[rbtk])
        p25 = ub[:, 0:256]
        p.mm(p25, Kt, RK, tp=tpk, r=[fk], w=[ubk])
        p.tt("pool", v3(BHBD[st_], 64), tm[:, dpos * 3 + 1, hc].unsqueeze(1).to_broadcast([128, 2, 64]), bdm, ALU.mult,
             r=[tmk, "cmf"], w=[("BHBD", st_)])
        p.tt("pool", v3(VBD[st_], 64), tm[:, 6, hc].unsqueeze(1).to_broadcast([128, 2, 64]), bdm, ALU.mult,
             r=[tmk, "cmf"], w=[("VBD", st_)])
        yield
        p.tt("dve", a2, p25, M2, ALU.mult, r=[ubk, "cmb"], w=[a2k])
        P3 = ub[:, 256:320]
        p.mm(P3, a2[:, 128:256], tm[:, 6, hc], r=[a2k, tmk], w=[ubk])
        p.cp("act", wk0[:, 128:192], tm[:, dpos * 3, hc], r=[tmk], w=[wkk[0]])
        yield
        p.cp("act", wk0[:, 192:256], P3, r=[ubk], w=[wkk[0]])
        wks = [wk0, wk1]
        for lv in range(5):
            cur, nxt = wks[lv % 2], wks[(lv + 1) % 2]
            ck_, nk_ = wkk[lv % 2], wkk[(lv + 1) % 2]
            p.mm(p14[:, 0:256], cur[:, 256:384], cur[:, 0:256], r=[ck_], w=[ubk])
            p.mm(p14[:, 256:384], cur[:, 0:128], cur[:, 256:384], r=[ck_], w=[ubk])
            yield
            p.cp("act", v3(nxt, 128)[:, 0:3:2, :], v3(p14, 128)[:, 0:3:2, :], r=[ubk], w=[nk_])
            p.tt("dve", nxt[:, 128:256], p14[:, 128:256], cur[:, 128:256], ALU.add, r=[ubk, ck_], w=[nk_])
        cur, ck_ = wks[1], wkk[1]
        p.mm(p14[:, 128:256], cur[:, 256:384], cur[:, 128:256], r=[ck_], w=[ubk])
        yield
        p.tt("dve", gu, p14[:, 128:256], cur[:, 128:256], ALU.add, r=[ubk, ck_], w=[guk])
        bh, vb_, ub_ = BHBD[st_], VBD[st_], U0BD[st_]
        bhk, vbk, ubk_ = ("BHBD", st_), ("VBD", st_), ("U0BD", st_)
        p.tt("pool", v3(ub_, 64), gu[:, 64:128].unsqueeze(1).to_broadcast([128, 2, 64]), bdm, ALU.mult,
             r=[guk, "cmf"], w=[ubk_])
        tp2 = (0, half * 64)
        p.mm(p25[hp, 0:128], gu[:, 0:64], bh, tp=tp2, r=[guk, bhk], w=[ubk])
        p.mm(p25[hp, 128:256], tm[:, dpos * 3 + 1, hc], ub_, start=True, stop=False, tp=tp2, r=[tmk, ubk_], w=[ubk])
        p.mm(p25[hp, 128:256], tm[:, dpos * 3 + 2, hc], vb_, start=False, stop=True, tp=tp2, r=[tmk, vbk], w=[ubk])
        P6 = ub[:, 256:384]
        if is_main:
            p.mm(P6[hp, :], gu[:, 0:64], rbt, tp=(0, half * 64), r=[guk, rbtk], w=[ubk])
        yield
        for c2 in range(2):
            ci = pr * 2 + c2
            if dpos == 0:
                dst, dk = ppa[hp, c2, 0:64], ppak
            else:
                dst, dk = bo[hp, c2 * 128:c2 * 128 + 64], bok
            p.stt(dst, idz_f[hp, 0:64], WC[buf][dpos][hp, ci:ci + 1], p25[hp, c2 * 64:(c2 + 1) * 64], ALU.mult, ALU.add,
                  r=["cmf", ("WC", buf, dpos), ubk], w=[dk])
        if dpos == 0:
            dst, dk = ppa[hp, :, 64:128], ppak
        else:
            dst, dk = v3(bo[hp, 0:256], 128)[:, :, 64:128], bok
        p.cp("act", dst, v3(p25[hp, 128:256], 64), r=[ubk], w=[dk])
        if is_main:
            if dpos == 0:
                dst, dk = qta[hp, :], qtak
            else:
                dst, dk = bo[hp, 256:384], bok
            p.tt("dve", dst, P6[hp, :], R_, ALU.add, r=[ubk, fk], w=[dk])

    pbcnt = [0]
    LD = 0.6065306597126334
    stage_seq = []
    carry_post = [None]
    for gi, (kind, t0) in enumerate(groups):
        n = GT
        is_main = kind == "main"
        left_ok = is_main and t0 > MAIN0
        right_ok = is_main
        m0 = t0 - MAIN0
        dirs = (0, 1) if is_main else (0,)
        hkeys = [("hT", t0 // 128 + i) for i in range(3) if t0 // 128 + i < 19]
        if left_ok:
            hkeys.append(("hT", t0 // 128 - 1))

        def project_conv(cidx, dst, dst_key):
            k = ucnt[0] % 2
            ucnt[0] += 1
            lo_ = 1 if left_ok else 0
            ro_ = 1 if right_ok else 0
            ncol = n + lo_ + ro_
            psu = psb[0][:, 0:ncol]
            puk = ("ps", 0)
            u = ur[k]
            uk = ("ur", k)
            for kc in range(8):
                p.mm(psu, w_rw[:, kc, cidx * 128:(cidx + 1) * 128], hT[:, kc, t0 - lo_:t0 + n + ro_], start=(kc == 0),
                     stop=(kc == 7), r=WRW + hkeys, w=[puk])
            p.cp("act", u[:, 1 - lo_:1 - lo_ + ncol], psu, r=[puk], w=[uk])
            if not left_ok:
                p.memset("pool", u[:, 0:1], 0.0, w=[uk])
            if not right_ok:
                p.memset("pool", u[:, n + 1:n + 2], 0.0, w=[uk])
            cw = PV_CW + cidx * 3
            p.act(dst, u[:, 1:n + 1], AF.Identity, scale=pv[:, cw + 1:cw + 2], r=[uk, "pv"], w=[dst_key])
            p.stt(dst, u[:, 0:n], pv[:, cw:cw + 1], dst, ALU.mult, ALU.add, r=[uk, "pv", dst_key], w=[dst_key])
            p.stt(dst, u[:, 2:n + 2], pv[:, cw + 2:cw + 3], dst, ALU.mult, ALU.add, r=[uk, "pv", dst_key], w=[dst_key])

        p.cap_begin()
        project_conv(12, lo_t, "lo")
        p.act(loraT[0:32, :], lo_t[0:32, :], AF.Tanh, r=["lo"], w=["loraT"])
        p.cp("dve", loraT[32:64, :], lo_t[32:64, :], r=["lo"], w=["loraT"])
        if is_main:
            project_conv(13, hg_t, "hg")
            p.act(sgT[0:96, m0:m0 + n], hg_t[0:96, :], AF.Sigmoid, r=["hg"], w=[("sgT", gi)])

        for j in range(4):
            if j > 0:
                p.cap_begin()
            buf = (gi * 4 + j) % 2
            r_, k_, v_ = Cb[buf]
            ck = [("C", buf, i) for i in range(3)]
            for i3 in range(3):
                project_conv(3 * j + i3, Cb[buf][i3], ck[i3])
            tk = [("T", i) for i in range(11)]
            p.cp("act", vb, v_, r=[ck[2]], w=["vb"])
            SW, SWk = [T[1], TX[0]], [tk[1], ("TX", 0)]
            AA, AAk = [T[2], TX[1]], [tk[2], ("TX", 1)]
            for dpos in dirs:
                pmw, pmwk = pm_region()
                p.mm(pmw, loraw[0:64, (dpos * 2) * 512 + j * 128:(dpos * 2) * 512 + (j + 1) * 128], loraT[0:64, :],
                     r=["loraw", "loraT"], w=[pmwk])
                p.act(SW[dpos], pmw, AF.Sigmoid, bias=pv[:, PV_W0 + j * 2 + dpos:PV_W0 + j * 2 + dpos + 1],
                      r=[pmwk, "pv"], w=[SWk[dpos]])
                pma, pmak = pm_region()
                p.mm(pma, loraw[0:64, (dpos * 2 + 1) * 512 + j * 128:(dpos * 2 + 1) * 512 + (j + 1) * 128], loraT[0:64, :],
                     r=["loraw", "loraT"], w=[pmak])
                p.act(AA[dpos], pma, AF.Sigmoid, bias=pv[:, PV_A0 + j * 2 + dpos:PV_A0 + j * 2 + dpos + 1],
                      r=[pmak, "pv"], w=[AAk[dpos]])
            p.act(T[0], k_, AF.Identity, scale=pv[:, PV_KK + j:PV_KK + j + 1], r=[ck[1], "pv"], w=[tk[0]])
            p.act(T[4], T[0], AF.Square, r=[tk[0]], w=[tk[4]])
            pm, pmk = pm_region()
            p.mm(pm, bones_f, T[4], r=["cmf", tk[4]], w=[pmk])
            p.ts("dve", T[4], pm, 1e-24, None, ALU.max, r=[pmk], w=[tk[4]])
            p.act(T[4], T[4], AF.Ln, r=[tk[4]], w=[tk[4]])
            p.act(T[4], T[4], AF.Exp, scale=-0.5, r=[tk[4]], w=[tk[4]])
            p.tt("pool", T[0], T[0], T[4], ALU.mult, r=[tk[0], tk[4]], w=[tk[0]])
            for dpos in dirs:
                SWt, SWk_, AAt, AAk_ = SW[dpos], SWk[dpos], AA[dpos], AAk[dpos]
                p.add("dve", lambda v, o=T[3], a=cm_reset, b=SWt: v.tensor_tensor_scan(
                    out=o, data0=a, data1=b, initial=0.0, op0=ALU.mult, op1=ALU.add), r=["cmf", SWk_], w=[tk[3]])
                c3 = v3(T[3], 64)
                tot_bc = c3[:, :, 63:64].to_broadcast([128, 4, 64])
                if dpos == 0:
                    clu, cluk = T[3], tk[3]
                    p.act(T[5], T[3], AF.Exp, scale=-LD, r=[tk[3]], w=[tk[5]])
                    p.tt("pool", T[6], T[3], SWt, ALU.subtract, r=[tk[3], SWk_], w=[tk[6]])
                    p.act(T[6], T[6], AF.Exp, scale=-LD, r=[tk[6]], w=[tk[6]])
                    p.act(T[7], T[3], AF.Exp, scale=LD, r=[tk[3]], w=[tk[7]])
                    p.tt("pool", v3(T[8], 64), tot_bc, c3, ALU.subtract, r=[tk[3]], w=[tk[8]])
                    p.act(T[8], T[8], AF.Exp, scale=-LD, r=[tk[8]], w=[tk[8]])
                else:
                    p.tt("pool", T[4], SWt, T[3], ALU.subtract, r=[tk[3], SWk_], w=[tk[4]])
                    p.tt("pool", v3(T[4], 64), v3(T[4], 64), tot_bc, ALU.add, r=[tk[3], tk[4]], w=[tk[4]])
                    p.act(T[5], T[4], AF.Exp, scale=-LD, r=[tk[4]], w=[tk[5]])
                    p.tt("pool", v3(T[6], 64), tot_bc, c3, ALU.subtract, r=[tk[3]], w=[tk[6]])
                    p.act(T[6], T[6], AF.Exp, scale=-LD, r=[tk[6]], w=[tk[6]])
                    p.act(T[7], T[4], AF.Exp, scale=LD, r=[tk[4]], w=[tk[7]])
                    p.tt("pool", T[8], T[3], SWt, ALU.subtract, r=[tk[3], SWk_], w=[tk[8]])
                    p.act(T[8], T[8], AF.Exp, scale=-LD, r=[tk[8]], w=[tk[8]])
                wc = WC[buf][dpos]
                wck = ("WC", buf, dpos)
                p.act(wc.unsqueeze(2), c3[:, :, 63:64], AF.Exp, scale=-LD, r=[tk[3]], w=[wck])
                p.tt("pool", T[9], T[0], AAt, ALU.mult, r=[tk[0], AAk_], w=[tk[9]])
                p.ts("dve", AAt, AAt, -1.0, pv[:, PV_KA + j:PV_KA + j + 1], ALU.add, ALU.mult, r=[AAk_, "pv"], w=[AAk_])
                if dpos == 0:
                    KD, KDk = T[10], tk[10]
                else:
                    KD, KDk = AAt, AAk_
                p.stt(KD, AAt, 1.0, k_, ALU.add, ALU.mult, r=[AAk_, ck[1]], w=[KDk])
                for pr in range(2):
                    sl = slice(pr * 128, (pr + 1) * 128)
                    fm = FM[buf][dpos][pr]
                    fk = ("FM", buf, dpos, pr)
                    p.tt("pool", fm[:, 0, :], r_[:, sl], T[5][:, sl], ALU.mult, r=[ck[0], tk[5]], w=[fk])
                    p.stt(fm[:, 1, :], T[0][:, sl], -1.0, T[6][:, sl], ALU.mult, ALU.mult, r=[tk[0], tk[6]], w=[fk])
                    p.tt("pool", fm[:, 2, :], T[9][:, sl], T[7][:, sl], ALU.mult, r=[tk[9], tk[7]], w=[fk])
                    p.tt("dve", fm[:, 3, :], KD[:, sl], T[7][:, sl], ALU.mult, r=[KDk, tk[7]], w=[fk])
                p.tt("pool", BH[dpos], T[9], T[8], ALU.mult, r=[tk[9], tk[8]], w=[("BH", dpos)])
                p.tt("dve", KH[dpos], KD, T[8], ALU.mult, r=[KDk, tk[8]], w=[("KH", dpos)])
                if dpos == 1:
                    p.tt("pool", T[10], T[10], AAt, ALU.add, r=[tk[10], AAk_], w=[tk[10]])
            if is_main:
                p.stt(T[1], r_, pv[:, PV_RK + j:PV_RK + j + 1], T[10], ALU.mult, ALU.mult, r=[ck[0], "pv", tk[10]], w=[tk[1]])
                pm, pmk = pm_region()
                p.mm(pm, bones_f, T[1], r=["cmf", tk[1]], w=[pmk])
                p.tt("dve", bonusT[:, j, m0:m0 + n], pm, v_, ALU.mult, r=[pmk, ck[2]], w=[("bonus", gi, j)])
            for pr in range(2):
                sl = slice(pr * 128, (pr + 1) * 128)
                pst = v3(psb[1][:, 0:448].bitcast(BF16), 128)
                pstk = ("ps", 1)
                tmk = ("TM", buf, pr)
                srcs = [(0, FM[buf][0][pr][:, 1, :], ("FM", buf, 0, pr)), (1, BH[0][:, sl], ("BH", 0)),
                        (2, KH[0][:, sl], ("KH", 0)), (6, vb[:, sl], "vb")]
                if is_main:
                    srcs += [(3, FM[buf][1][pr][:, 1, :], ("FM", buf, 1, pr)), (4, BH[1][:, sl], ("BH", 1)),
                             (5, KH[1][:, sl], ("KH", 1))]
                for slot, src, skey in srcs:
                    p.tr(pst[:, slot, :], src, ident_b, r=[skey, "identb"], w=[pstk])
                if is_main:
                    p.cp("act" if pr == 0 else "dve", TMt[buf][pr], pst, r=[pstk], w=[tmk])
                else:
                    p.cp("act", TMt[buf][pr][:, 0:3, :], pst[:, 0:3, :], r=[pstk], w=[tmk])
                    p.cp("dve", TMt[buf][pr][:, 6, :], pst[:, 6, :], r=[pstk], w=[tmk])
            stage_seq.append(p.cap_end())
            p.cap_begin()
            if is_main:
                batches = [[(pr, half, dpos) for half in range(2) for dpos in (0, 1)] for pr in range(2)]
            else:
                batches = [[(pr, half, 0) for pr in range(2) for half in range(2)]]
            bp_list = []
            for batch in batches:
                p.cap_begin()
                gens = []
                uinfo = []
                pbuf = pbcnt[0] % 2
                pbcnt[0] += 1
                bos = None
                if is_main:
                    bos = bocnt[0] % 4
                    bocnt[0] += 1
                for st_, (pr, half, dpos) in enumerate(batch):
                    uinfo.append((st_, pr, half, dpos))
                    gens.append(unit_gen(st_, buf, j, pr, half, dpos, is_main, PPA[pr if not is_main else pbuf],
                                         ("PPA", pr if not is_main else pbuf), QTA[pbuf], ("QTA", pbuf),
                                         BO[bos] if is_main else None, ("BO", bos)))
                alive = list(gens)
                while alive:
                    nxt_alive = []
                    for g_ in alive:
                        try:
                            next(g_)
                            nxt_alive.append(g_)
                        except StopIteration:
                            pass
                    alive = nxt_alive
                bp_list.append(p.cap_end())
                p.cap_begin()
                prs = sorted(set(pr for (_, pr, _, _) in uinfo))
                for pr in prs:
                    gp = m0 // 128 + pr
                    tm = TMt[buf][pr]
                    tmk = ("TM", buf, pr)
                    ppa_i = pr if not is_main else pbuf
                    ppa, ppak = PPA[ppa_i], ("PPA", ppa_i)
                    qta, qtak = QTA[pbuf], ("QTA", pbuf)
                    ybank, ybk = psb[2], ("ps", 2)
                    p8bank, p8k = psb[3], ("ps", 3)
                    psy = ybank[:, 0:128]
                    first_y = [True]
                    if is_main:
                        for (st_, pr_u, half, dpos) in uinfo:
                            hc = slice(half * 64, (half + 1) * 64)
                            p.mm(psy[:, hc], RbT[st_], GU[st_][:, 64:128], start=first_y[0], stop=False, sg=True,
                                 r=[("RbT", st_), ("GU", st_)], w=[ybk])
                            first_y[0] = False
                            p.mm(psy[:, hc], A2[st_][:, 0:128], tm[:, 6, hc], start=False, stop=False, sg=True,
                                 r=[("A2", st_), tmk], w=[ybk])
                    if pr == prs[0]:
                        bp_list.append(p.cap_end())
                        p.cap_begin()
                    for c2 in range(2):
                        cr = slice(c2 * 64, (c2 + 1) * 64)
                        sin_, sout = Sst[sp_par[j]], Sst[1 - sp_par[j]]
                        sink_, soutk = ("S", sp_par[j], j), ("S", 1 - sp_par[j], j)
                        P8 = p8bank[:, c2 * 64:(c2 + 1) * 64]
                        p.mm(P8[0:64, :], ppa[0:64, c2, 0:64], sin_[0:64, j, 0, :], tp=(0, 0), r=[ppak, sink_], w=[p8k])
                        p.mm(P8[64:128, :], ppa[64:128, c2, 0:64], sin_[64:128, j, 1, :], tp=(64, 64), r=[ppak, sink_], w=[p8k])
                        if is_main:
                            p.mm(psy[cr, :], qta[:, c2 * 64:(c2 + 1) * 64], sin_[:, j].rearrange("p a b -> p (a b)"),
                                 start=False, sg=True, stop=(c2 == 1), tp=(0, c2 * 64), r=[qtak, sink_], w=[ybk])
                        p.tt("dve", tmpS[:, j, :], P8, ppa[:, c2, 64:128], ALU.add, r=[p8k, ppak], w=[("tmpS", j)])
                        p.tt("dve", sout[:, j], tmpS[:, j, :].unsqueeze(1).to_broadcast([128, 2, 64]), bdm, ALU.mult,
                             r=[("tmpS", j), "cmf"], w=[soutk])
                        sp_par[j] = 1 - sp_par[j]
                    if is_main:
                        p.cp("act", yacc[:, gp, j * 128:(j + 1) * 128], psy, r=[ybk], w=[("yacc", gp, j)])
                        p.dma("sp", bops_d[gp * 4 + j], BO[bos], r=[("BO", bos)], w=[("bops", gp, j)])
                bp_list.append(p.cap_end())
            p.ops.extend(Prog.merge(carry_post[0], bp_list[0]) if carry_post[0] else bp_list[0])
            p.ops.extend(bp_list[1])
            if len(bp_list) == 6:
                p.ops.extend(Prog.merge(bp_list[2], bp_list[3]))
                p.ops.extend(bp_list[4])
                carry_post[0] = bp_list[5]
            else:
                carry_post[0] = bp_list[2]
            stage_seq.append(p.cap_end())
    p.ops.extend(stage_seq[0])
    for k_ in range(1, len(stage_seq) - 1, 2):
        p.ops.extend(Prog.merge(stage_seq[k_], stage_seq[k_ + 1]))
    p.ops.extend(stage_seq[-1])
    p.ops.extend(carry_post[0])
    dump("yacc", yacc, [128, 16, 512], BF16)
    dump("bonusT", bonusT, [128, 4, NMAIN], BF16)
    for j in range(4):
        p.tt("dve", Sf32[:, j * 64:(j + 1) * 64], Sst[sp_par[j]][:, j, 0, :], Sst[sp_par[j]][:, j, 1, :], ALU.add,
             r=[("S", sp_par[j], j)], w=["Sf32"])
    dump("SA", Sf32, [128, 256], F32)
    p.dma("pool", cin_d, Sf32, r=["Sf32"], w=["cin"])
    groups_cc = [[0, 1], [2, 3], [4, 5], [6, 7]]
    if not no_cc:
      p.add("pool", lambda g: g.collective_compute("AllGather", ALU.bypass, replica_groups=groups_cc, ins=[cin_d],
                                                 outs=[cout_d]), r=["cin"], w=["cout"])
    p.dma("pool", cslots, cout_d.rearrange("(r p) c -> p r c", r=2), r=["cout"], w=["cslots"])
    if stop_after == "RW":
        return finish()
    p.barrier()

    L = Bump(131 * KB, 207 * KB)
    wch = [L([128, 8, 128], BF16) for _ in range(4)]
    qT = L([128, 4, NMAIN], BF16)
    kT = L([128, 2, 2176], BF16)
    kcT = L([128, 2, 256], BF16)
    v_sb = L([128, 19, 128], BF16)
    cosT = L([128, 2176], F32)
    sinT = L([128, 2176], F32)
    rt = [[L([128, 512], F32) for _ in range(2)] for _ in range(2)]
    PT = [L([128, 512], BF16) for _ in range(4)]
    esink = L([128, 512], F32)
    den = [L([128, 256], F32) for _ in range(2)]
    qbd = [L([128, 2, 128], BF16) for _ in range(4)]
    bdm128 = v3(bones_f, 64)[:, :, 0:1].to_broadcast([128, 2, 128])
    p.dma("sp", cosT, cos_d, w=["cosT"])
    p.dma("sp", sinT, sin_d, w=["sinT"])
    p.dma("sp", esink, sinkb_d, w=["esink"])
    p.act(esink, esink, AF.Exp, r=["esink"], w=["esink"])
    w_att_v = w_att_d.rearrange("(kc p) n -> p kc n", p=128)
    wcnt = [0]

    def load_wch(col0):
        k = wcnt[0] % 4
        wcnt[0] += 1
        p.dma("pool", wch[k], w_att_v[:, :, col0:col0 + 128], w=[("wch", k)])
        return wch[k], ("wch", k)

    pcnt = [0]

    def proj_rope(wa, wak, wb, wbk, tok0, ntok, dst, dst_key):
        k = pcnt[0] % 2
        pcnt[0] += 1
        pa, pak = psb[2 * k][:, 0:ntok], ("ps", 2 * k)
        pb, pbk = psb[2 * k + 1][:, 0:ntok], ("ps", 2 * k + 1)
        for kc in range(8):
            p.mm(pa, wa[:, kc, :], hT[:, kc, tok0:tok0 + ntok], start=(kc == 0), stop=(kc == 7), r=[wak] + HT_ALL, w=[pak])
        for kc in range(8):
            p.mm(pb, wb[:, kc, :], hT[:, kc, tok0:tok0 + ntok], start=(kc == 0), stop=(kc == 7), r=[wbk] + HT_ALL, w=[pbk])
        m = tok0 - MAIN0
        t1, t2 = rt[k]
        p.tt("dve", t1[:, 0:ntok], pa, cosT[:, m:m + ntok], ALU.mult, r=[pak, "cosT"], w=[("rt", k, 0)])
        p.tt("dve", t2[:, 0:ntok], pb, sinT[:, m:m + ntok], ALU.mult, r=[pbk, "sinT"], w=[("rt", k, 1)])
        p.tt("pool", dst, t1[:, 0:ntok], t2[:, 0:ntok], ALU.add, r=[("rt", k, 0), ("rt", k, 1)], w=[dst_key])

    wloads = [(c * 128, 512 + c * 128) for c in range(4)] + [(1024 + g * 128, 1280 + g * 128) for g in range(2)] + [(1536,)]
    wloaded = {}

    def ensure_w(n):
        if n < len(wloads) and n not in wloaded:
            wloaded[n] = [load_wch(col) for col in wloads[n]]

    ensure_w(0)
    for c in range(4):
        ensure_w(c + 1)
        (wa, wak), (wb, wbk) = wloaded[c]
        for tg in range(4):
            proj_rope(wa, wak, wb, wbk, MAIN0 + tg * 512, 512, qT[:, c, tg * 512:(tg + 1) * 512], ("qT", c, tg))
    for g in range(2):
        ensure_w(4 + g + 1)
        (wa, wak), (wb, wbk) = wloaded[4 + g]
        for tg in range(5):
            nt = 512 if tg < 4 else 128
            proj_rope(wa, wak, wb, wbk, MAIN0 + tg * 512, nt, kT[:, g, tg * 512:tg * 512 + nt], ("kT", g, tg))
        k = pcnt[0] % 2
        pcnt[0] += 1
        pa, pak = psb[2 * k][:, 0:256], ("ps", 2 * k)
        for kc in range(8):
            p.mm(pa, wa[:, kc, :], hT[:, kc, 0:256], start=(kc == 0), stop=(kc == 7), r=[wak] + HT_ALL, w=[pak])
        p.cp("act", kcT[:, g, :], pa, r=[pak], w=[("kcT", g)])
    (wv, wvk), = wloaded[6]
    for tt_ in range(19):
        k = tt_ % 2
        pa, pak = psb[4 + k][:, 0:128], ("ps", 4 + k)
        for kc in range(8):
            p.mm(pa, hT[:, kc, tt_ * 128:(tt_ + 1) * 128], wv[:, kc, :], start=(kc == 0), stop=(kc == 7),
                 r=[wvk] + HT_ALL, w=[pak])
        p.cp("act" if k == 0 else "dve", v_sb[:, tt_, :], pa, r=[pak], w=[("v_sb", tt_)])
    dump("qT", qT, [128, 4, NMAIN], BF16)
    dump("kT", kT, [128, 2, 2176], BF16)
    dump("v_sb", v_sb, [128, 19, 128], BF16)
    QK_ALL = [("qT", c, tg) for c in range(4) for tg in range(4)] + [("kT", g, tg) for g in range(2) for tg in range(5)] + \
             [("kcT", g) for g in range(2)] + [("v_sb", t) for t in range(19)]
    scnt = [0]
    ocnt = [0]
    def build_qb(i, g, ko):
        out_ = []
        for cqi in range(2):
            qi_ = (2 * ko + cqi) % 4
            p.tt("pool", qbd[qi_], qT[:, 2 * g + cqi, i * 128:(i + 1) * 128].unsqueeze(1).to_broadcast([128, 2, 128]),
                 bdm128, ALU.mult, r=QK_ALL + ["cmf"], w=[("qbd", qi_)])
            out_.append((qbd[qi_].rearrange("p a b -> p (a b)"), ("qbd", qi_)))
        return out_

    ig_list = [(i, g) for i in range(16) for g in range(2)]
    qb_next = build_qb(0, 0, 0)
    for n_ig, (i, g) in enumerate(ig_list):
        if True:
            kbs = [("c", 0), ("c", 1)] + ([("l", i - 1)] if i > 0 else []) + [("l", i), ("l", i + 1)]
            ko = ocnt[0] % 2
            ocnt[0] += 1
            pso, psok = psb[6 + ko], ("ps", 6 + ko)
            qb = qb_next
            def emit_scores(kk_, kb):
                ks = scnt[0] % 3
                kp = scnt[0] % 4
                scnt[0] += 1
                pss, pssk = psb[ks], ("ps", ks)
                if kk_ == "c":
                    ksrc = kcT[:, g, kb * 128:(kb + 1) * 128]
                    vt = v_sb[:, kb, g * 64:(g + 1) * 64]
                else:
                    ksrc = kT[:, g, kb * 128:(kb + 1) * 128]
                    vt = v_sb[:, 2 + kb, g * 64:(g + 1) * 64]
                for cqi in range(2):
                    p.mm(pss[:, cqi * 256:(cqi + 1) * 256], ksrc, qb[cqi][0], r=QK_ALL + [qb[cqi][1]], w=[pssk])
                return pss, pssk, kp, vt

            pendq = [emit_scores(*kbs[0]), emit_scores(*kbs[1])]
            for bi, (kk_, kb) in enumerate(kbs):
                pss, pssk, kp, vt = pendq.pop(0)
                if bi + 2 < len(kbs):
                    pendq.append(emit_scores(*kbs[bi + 2]))
                if bi + 3 == len(kbs) and n_ig + 1 < len(ig_list):
                    qb_next = build_qb(ig_list[n_ig + 1][0], ig_list[n_ig + 1][1], 1 - ko)
                pt, ptk = PT[kp], ("PT", kp)
                p.act(pt, pss, AF.Exp, scale=0.125, r=[pssk], w=[ptk])
                if kk_ == "l" and kb == i - 1:
                    p.tt("dve", pt, pt, band_prev, ALU.mult, r=[ptk, "cmb"], w=[ptk])
                if kk_ == "l" and kb == i + 1:
                    p.tt("dve", pt, pt, band_next, ALU.mult, r=[ptk, "cmb"], w=[ptk])
                pt4 = v3(pt, 128)
                first, last = bi == 0, bi == len(kbs) - 1
                for par in range(2):
                    ps_ = slice(par * 64, (par + 1) * 64)
                    p.mm(pso[ps_, 0:256], vt, pt4[:, par:4:2, :], start=first, stop=False, tp=(0, par * 64), sg=True,
                         r=[ptk] + QK_ALL, w=[psok])
                    p.mm(pso[ps_, 256:512], ones_b[:, 0:64], pt4[:, par:4:2, :], start=False, stop=last, tp=(0, par * 64),
                         sg=True, r=[ptk, "onesb"], w=[psok])
            dn, dnk = den[ko], ("den", ko)
            p.tt("dve", dn, pso[:, 256:512], esink[:, g * 256:(g + 1) * 256], ALU.add, r=[psok, "esink"], w=[dnk])
            p.add("dve", lambda v, o=dn: v.reciprocal(out=o, in_=o), r=[dnk], w=[dnk])
            p.tt("dve", yaT[:, 2 * g:2 * g + 2, i * 128:(i + 1) * 128], v3(pso[:, 0:256], 128), v3(dn, 128), ALU.mult,
                 r=[psok, dnk], w=[("yaT", i, g)])
    dump("yaT", yaT, [128, 4, NMAIN], BF16)
    if stop_after == "ATT":
        return finish()
    p.barrier()

    L = Bump(131 * KB, 207 * KB)
    g2 = L([96, 512], BF16)
    BOs = [L([128, 4, 384], BF16) for _ in range(3)]
    ytot = [L([128, 512], F32) for _ in range(2)]
    ycen = [L([128, 512], F32) for _ in range(2)]
    ysq = [L([128, 512], F32) for _ in range(2)]
    ynb = [L([128, 512], BF16) for _ in range(2)]
    lin = [L([128, 4, 128], F32) for _ in range(2)]
    gst = [[L([128, 8], F32) for _ in range(3)] for _ in range(2)]
    p.dma("pool", g2, g2_d, w=["g2"])
    p.ts("dve", Sf32, cslots[:, 0, :], pv[:, PV_FSEL:PV_FSEL + 1], None, ALU.mult, r=["cslots", "pv"], w=["Sf32"])
    p.stt(Sf32, cslots[:, 1, :], pv[:, PV_FSEL + 1:PV_FSEL + 2], Sf32, ALU.mult, ALU.add,
          r=["cslots", "pv", "Sf32"], w=["Sf32"])
    for j in range(4):
        p.tt("pool", Sst[0][:, j], Sf32[:, j * 64:(j + 1) * 64].unsqueeze(1).to_broadcast([128, 2, 64]), bdm, ALU.mult,
             r=["Sf32", "cmf"], w=["SB0"])
    dump("SB0", Sf32, [128, 256], F32)
    spb = 0
    SBK = ["SB0", "SB1"]
    b_rec, b_out = [], []
    for gi_, gp in enumerate(range(15, -1, -1)):
        p.cap_begin()
        sl_ = gi_ % 3
        bo, bok = BOs[sl_], ("BOs", sl_)
        p.dma("sp", bo, bops_d[gp * 4:(gp + 1) * 4].rearrange("j p c -> p j c"),
              r=[("bops", gp, j) for j in range(4)], w=[bok])
        kb_ = gi_ % 2
        psyb, psybk = psb[kb_], ("ps", kb_)
        for c2 in (1, 0):
            cr = slice(c2 * 64, (c2 + 1) * 64)
            sin_, sout = Sst[spb], Sst[1 - spb]
            sink_, soutk = SBK[spb], SBK[1 - spb]
            P8 = psb[2 + c2][:, 0:256]
            P8k = ("ps", 2 + c2)
            for j in range(4):
                p.mm(psyb[cr, j * 128:(j + 1) * 128], bo[:, j, 256 + c2 * 64:256 + (c2 + 1) * 64],
                     sin_[:, j].rearrange("p a b -> p (a b)"), tp=(0, c2 * 64), r=[bok, sink_], w=[psybk])
            for j in range(4):
                for half in range(2):
                    hp = slice(half * 64, (half + 1) * 64)
                    p.mm(P8[hp, j * 64:(j + 1) * 64], bo[hp, j, c2 * 128:c2 * 128 + 64], sin_[hp, j, half, :],
                         tp=(half * 64, half * 64), r=[bok, sink_], w=[P8k])
            p.tt("dve", tmpS, v3(P8, 64), bo[:, :, c2 * 128 + 64:c2 * 128 + 128], ALU.add, r=[P8k, bok], w=["tmpSB"])
            p.tt("dve", sout, tmpS.unsqueeze(2).to_broadcast([128, 4, 2, 64]),
                 bdm.unsqueeze(1).to_broadcast([128, 4, 2, 64]), ALU.mult, r=["tmpSB", "cmf"], w=[soutk])
            spb = 1 - spb
        b_rec.append(p.cap_end())
        p.cap_begin()
        yt, ytk = ytot[kb_], ("ytot", kb_)
        p.tt("dve", yt, psyb, yacc[:, gp, :], ALU.add, r=[psybk] + [("yacc", gp, j) for j in range(4)], w=[ytk])
        y3 = v3(yt, 64)
        s1, s2, rs = gst[kb_]
        gk = ("gst", kb_)
        p.add("dve", lambda v, o=s1, i_=y3: v.reduce_sum(out=o, in_=i_, axis=AX.X), r=[ytk], w=[gk])
        p.ts("pool", s1, s1, 1.0 / 64, None, ALU.mult, r=[gk], w=[gk])
        yc, yck = ycen[kb_], ("ycen", kb_)
        p.tt("pool", v3(yc, 64), y3, s1.unsqueeze(2).to_broadcast([128, 8, 64]), ALU.subtract, r=[ytk, gk], w=[yck])
        sq, sqk = ysq[kb_], ("ysq", kb_)
        p.tt("pool", sq, yc, yc, ALU.mult, r=[yck], w=[sqk])
        p.add("dve", lambda v, o=s2, i_=v3(sq, 64): v.reduce_sum(out=o, in_=i_, axis=AX.X), r=[sqk], w=[gk])
        p.ts("dve", rs, s2, 1.0 / 64, LNX_EPS, ALU.mult, ALU.add, r=[gk], w=[gk])
        p.act(rs, rs, AF.Sqrt, r=[gk], w=[gk])
        p.add("dve", lambda v, o=rs: v.reciprocal(out=o, in_=o), r=[gk], w=[gk])
        yn, ynk = ynb[kb_], ("ynb", kb_)
        p.tt("dve", v3(yn, 64), v3(yc, 64), rs.unsqueeze(2).to_broadcast([128, 8, 64]), ALU.mult, r=[yck, gk], w=[ynk])
        pst = v3(psb[4 + kb_][:, 0:256].bitcast(BF16), 128)
        pstk = ("ps", 4 + kb_)
        for j in range(4):
            p.tr(pst[:, j, :], yn[:, j * 128:(j + 1) * 128], ident_b, r=[ynk, "identb"], w=[pstk])
        psg, psgk = psb[6 + kb_], ("ps", 6 + kb_)
        for j in range(4):
            p.mm(psg[:, j * 128:(j + 1) * 128], g2[0:96, j * 128:(j + 1) * 128], sgT[0:96, gp * 128:(gp + 1) * 128],
                 r=["g2"] + [("sgT", 1 + gp // 2)], w=[psgk])
        ln_, lnk = lin[kb_], ("lin", kb_)
        for j in range(4):
            p.act(ln_[:, j, :], pst[:, j, :], AF.Identity, bias=pv[:, PV_LB + j:PV_LB + j + 1],
                  scale=pv[:, PV_LW + j:PV_LW + j + 1], r=[pstk, "pv"], w=[lnk])
        p.tt("pool", ln_, ln_, bonusT[:, :, gp * 128:(gp + 1) * 128], ALU.add,
             r=[lnk] + [("bonus", 1 + gp // 2, j) for j in range(4)], w=[lnk])
        p.tt("dve", yrT[:, :, gp * 128:(gp + 1) * 128], ln_, v3(psg[:, :], 128), ALU.mult, r=[lnk, psgk], w=[("yrT", gp)])
        b_out.append(p.cap_end())
    p.ops.extend(b_rec[0])
    for k_ in range(16):
        if k_ + 1 < 16:
            p.ops.extend(Prog.merge(b_out[k_], b_rec[k_ + 1]))
        else:
            p.ops.extend(b_out[k_])
    dump("yrT", yrT, [128, 4, NMAIN], BF16)
    if stop_after == "B":
        return finish()
    p.barrier()

    w_g = carve(95 * KB, [128, 8, 2048], BF16)
    w_ba = carve(127 * KB, [128, 4, D], BF16)
    w_br = carve(135 * KB, [128, 4, D], BF16)
    L = Bump(143 * KB, 175 * KB)
    mergedT = carve(175 * KB, [128, 8, NMAIN], BF16)
    gat = [[L([128, 512], F32) for _ in range(2)] for _ in range(2)]
    mt = [[L([128, 512], F32) for _ in range(2)] for _ in range(2)]
    w_g_v = w_g_d.rearrange("(kc p) n -> p kc n", p=128)
    for q4 in range(4):
        p.dma("pool", w_g[:, :, q4 * 512:(q4 + 1) * 512], w_g_v[:, :, q4 * 512:(q4 + 1) * 512], w=[("w_g", q4)])
    p.dma("pool", w_ba, w_ba_d.rearrange("(c p) n -> p c n", p=128), w=["w_ba"])
    p.dma("pool", w_br, w_br_d.rearrange("(c p) n -> p c n", p=128), w=["w_br"])
    WG = [("w_g", q4) for q4 in range(4)]
    YA = [("yaT", i, g) for i in range(16) for g in range(2)]
    YR = [("yrT", gp) for gp in range(16)]
    mc = 0
    for gm in range(4):
        tok = MAIN0 + gm * 512
        m0 = gm * 512
        for dc in range(8):
            s_ = mc % 2
            mc += 1
            pga, pgr, pza, pzr = psb[4 * s_], psb[4 * s_ + 1], psb[4 * s_ + 2], psb[4 * s_ + 3]
            kga, kgr, kza, kzr = [("ps", 4 * s_ + i) for i in range(4)]
            for kc in range(8):
                p.mm(pga, w_g[:, kc, dc * 128:(dc + 1) * 128], hT[:, kc, tok:tok + 512], start=(kc == 0), stop=(kc == 7),
                     r=WG + HT_ALL, w=[kga])
            for kc in range(8):
                p.mm(pgr, w_g[:, kc, 1024 + dc * 128:1024 + (dc + 1) * 128], hT[:, kc, tok:tok + 512], start=(kc == 0),
                     stop=(kc == 7), r=WG + HT_ALL, w=[kgr])
            for c in range(4):
                p.mm(pza, w_ba[:, c, dc * 128:(dc + 1) * 128], yaT[:, c, m0:m0 + 512], start=(c == 0), stop=(c == 3),
                     r=["w_ba"] + YA, w=[kza])
            for c in range(4):
                p.mm(pzr, w_br[:, c, dc * 128:(dc + 1) * 128], yrT[:, c, m0:m0 + 512], start=(c == 0), stop=(c == 3),
                     r=["w_br"] + YR, w=[kzr])
            ga, gr = gat[s_]
            p.act(ga, pga, AF.Sigmoid, r=[kga], w=[("gat", s_, 0)])
            p.act(gr, pgr, AF.Sigmoid, r=[kgr], w=[("gat", s_, 1)])
            m1, m2 = mt[s_]
            p.tt("dve", m1, pza, ga, ALU.mult, r=[kza, ("gat", s_, 0)], w=[("mt", s_, 0)])
            p.tt("dve", m2, pzr, gr, ALU.mult, r=[kzr, ("gat", s_, 1)], w=[("mt", s_, 1)])
            p.tt("pool", mergedT[:, dc, m0:m0 + 512], m1, m2, ALU.add, r=[("mt", s_, 0), ("mt", s_, 1)], w=[("merged", gm, dc)])
    dump("mergedT", mergedT, [128, 8, NMAIN], BF16)
    if stop_after == "MERGE":
        return finish()
    p.barrier()

    w_up = carve(8 * KB, [128, 8, 4096], BF16)
    w_down = carve(72 * KB, [128, 32, D], BF16)
    w_out = carve(136 * KB, [128, 8, D], BF16)
    nfbc = carve(152 * KB, [128, D], F32)
    g1bc = carve(8 * KB, [128, D], F32)
    g2bc = carve(12 * KB, [128, D], F32)
    dgt = carve(16 * KB, [128, 128], F32)
    stg = [carve(156 * KB, [128, 2, D], F32), carve(164 * KB, [128, 2, D], F32)]
    x1b = [carve(156 * KB, [128, D], F32), carve(160 * KB, [128, D], F32)]
    xn2b = [carve(164 * KB, [128, D], BF16), carve(166 * KB, [128, D], BF16)]
    h2Tb = [carve(168 * KB, [128, 8, 128], BF16), carve(170 * KB, [128, 8, 128], BF16)]
    actR = [carve(172 * KB, [128, 4, 128], BF16), carve(173 * KB, [128, 4, 128], BF16)]
    p.dma("sp", nfbc, nfbc_d, w=["nfbc"])
    for gi_, (gbc, jbase, gkey) in enumerate(((g1bc, 16, "g1bc"), (g2bc, 40, "g2bc"))):
        for kc in range(8):
            bank = 2 * gi_ + kc // 4
            p.ts("dve", dgt, ident_f, modT[:, jbase + kc, 0:1], None, ALU.mult, r=["cmf"] + MODK, w=["dgt"])
            p.mm(psb[bank][:, (kc % 4) * 128:(kc % 4 + 1) * 128], ones_f, dgt, r=["cmf", "dgt"], w=[("ps", bank)])
        p.cp("act", gbc[:, 0:512], psb[2 * gi_], r=[("ps", 2 * gi_)], w=[gkey])
        p.cp("act", gbc[:, 512:1024], psb[2 * gi_ + 1], r=[("ps", 2 * gi_ + 1)], w=[gkey])
    w_out_v = w_out_d.rearrange("(kc p) n -> p kc n", p=128)
    w_down_v = w_down_d.rearrange("(f p) n -> p f n", p=128)
    w_up_v = w_up_d.rearrange("(kc p) n -> p kc n", p=128)
    for kc in range(2, 8):
        p.dma("pool", w_up[:, kc, :], w_up_v[:, kc, :], w=[("w_up", kc)])
    sc_ = 0
    for q in range(4):
        s_, sk_ = stg[sc_ % 2], ("stg", sc_ % 2)
        sc_ += 1
        p.dma("sp", s_, w_out_v[:, 2 * q:2 * q + 2, :], w=[sk_])
        p.tt("dve", w_out[:, 2 * q:2 * q + 2, :], s_, g1bc.unsqueeze(1).to_broadcast([128, 2, D]), ALU.mult,
             r=[sk_, "g1bc"], w=[("w_out", q)])
    for q in range(16):
        s_, sk_ = stg[sc_ % 2], ("stg", sc_ % 2)
        sc_ += 1
        p.dma("sp", s_, w_down_v[:, 2 * q:2 * q + 2, :], w=[sk_])
        p.tt("dve", w_down[:, 2 * q:2 * q + 2, :], s_, g2bc.unsqueeze(1).to_broadcast([128, 2, D]),
             ALU.mult, r=[sk_, "g2bc"], w=[("w_down", q)])
    for kc in range(2):
        p.dma("pool", w_up[:, kc, :], w_up_v[:, kc, :], r=["g1bc", "g2bc", "dgt"],
              w=[("w_up", kc), "g1bc", "g2bc", "dgt"])
    p.barrier()
    WOUT = [("w_out", q) for q in range(4)]
    WDN = [("w_down", q) for q in range(16)]
    WUP = [("w_up", kc) for kc in range(8)]
    def mlp_head(tI):
        b = tI % 2
        m = tI * 128
        x1, xk = x1b[b], ("x1", b)
        xn2, xnk = xn2b[b], ("xn2", b)
        h2T, hk = h2Tb[b], ("h2T", b)
        p.dma("sp", x1, xin[MAIN0 + m:MAIN0 + m + 128, :], w=[xk])
        for hf in range(2):
            ps_, psk = psb[hf], ("ps", hf)
            for dc in range(8):
                p.mm(ps_, mergedT[:, dc, m:m + 128], w_out[:, dc, hf * 512:(hf + 1) * 512], start=(dc == 0), stop=(dc == 7),
                     r=WOUT + [("merged", tI // 4, dc)], w=[psk])
            p.tt("dve", x1[:, hf * 512:(hf + 1) * 512], ps_, x1[:, hf * 512:(hf + 1) * 512], ALU.add, r=[psk, xk], w=[xk])
        ssq = sstat[b][:, 0:1]
        rms_rstd(x1, ssq, NORM_EPS, 1.0 / D, [xk], ("ssq2", b), xn2, xnk)
        p.ts("dve", xn2, x1, ssq, None, ALU.mult, r=[xk, ("ssq2", b)], w=[xnk])
        transpose_modulate(xn2, xnk, lambda kc: h2T[:, kc, :], sc2, modT[:, 24:32, :], 0, 2, [hk])

    def mlp_mid(tI):
        b = tI % 2
        h2T, hk = h2Tb[b], ("h2T", b)

        def up_mm(fq):
            bank = 3 + fq % 2
            for kc in range(8):
                p.mm(psb[bank], h2T[:, kc, :], w_up[:, kc, fq * 512:(fq + 1) * 512], start=(kc == 0), stop=(kc == 7),
                     r=WUP + [hk], w=[("ps", bank)])

        up_mm(0)
        for fq in range(8):
            bank = 3 + fq % 2
            ps_, psk = psb[bank], ("ps", bank)
            r_, rk_ = rl[fq % 2], ("rl", fq % 2)
            p.act(r_, ps_, AF.Relu, r=[psk], w=[rk_])
            if fq + 1 < 8:
                up_mm(fq + 1)
            p.tt("pool", r_, r_, r_, ALU.mult, r=[rk_], w=[rk_])
            pstT = v3(psb[7][:, 0:256].bitcast(BF16), 128)
            for f4 in range(4):
                p.tr(pstT[:, f4, :], r_[:, f4 * 128:(f4 + 1) * 128], ident_b, r=[rk_, "identb"], w=[("ps", 7)])
            aR, aRk = actR[fq % 2], ("actR", fq % 2)
            p.cp("act" if fq % 2 else "dve", aR, pstT, r=[("ps", 7)], w=[aRk])
            for hf in range(2):
                for f4 in range(4):
                    f = fq * 4 + f4
                    p.mm(psb[5 + hf], aR[:, f4, :], w_down[:, f, hf * 512:(hf + 1) * 512], start=(f == 0), stop=(f == 31),
                         r=WDN + [aRk], w=[("ps", 5 + hf)])

    def mlp_tail(tI):
        b = tI % 2
        m = tI * 128
        x1, xk = x1b[b], ("x1", b)
        xn2, xnk = xn2b[b], ("xn2", b)
        for hf in range(2):
            ps_, psk = psb[5 + hf], ("ps", 5 + hf)
            p.tt("dve", x1[:, hf * 512:(hf + 1) * 512], ps_, x1[:, hf * 512:(hf + 1) * 512], ALU.add, r=[psk, xk], w=[xk])
        ssq3 = sstat[2 + b][:, 0:1]
        rms_rstd(x1, ssq3, NORM_EPS, 1.0 / D, [xk], ("ssq3", b), xn2, xnk)
        p.stt(x1, x1, ssq3, nfbc, ALU.mult, ALU.mult, r=[xk, ("ssq3", b), "nfbc"], w=[xk])
        p.dma("sp", out_d[m:m + 128, :], x1, r=[xk], w=[("out", tI)])

    mlp_head(0)
    for tI in range(16):
        mlp_mid(tI)
        if tI + 1 < 16:
            mlp_head(tI + 1)
        mlp_tail(tI)
    return finish()


def _bd(blk):
    m = np.zeros((128, 128), np.float32)
    m[0:64, 0:64] = blk
    m[64:128, 64:128] = blk
    return m


def _const_tables():
    i = np.arange(64)
    row, col = i[:, None], i[None, :]
    sl = (col < row).astype(np.float32)
    su = (col > row).astype(np.float32)
    iu = (col >= row).astype(np.float32)
    il = (col <= row).astype(np.float32)
    cmb = np.zeros((128, CM_BF_N), np.float32)
    kj = np.arange(128)[:, None]
    qi = np.arange(128)[None, :]
    cmb[:, CM_BPREV:CM_BPREV + 512] = np.tile((kj >= qi).astype(np.float32), (1, 4))
    cmb[:, CM_BNEXT:CM_BNEXT + 512] = np.tile((kj <= qi).astype(np.float32), (1, 4))
    cmb[:, CM_MA1:CM_MA1 + 384] = np.concatenate([_bd(sl), _bd(iu), _bd(su)], 1)
    cmb[:, CM_MB1:CM_MB1 + 384] = np.concatenate([_bd(su), _bd(il), _bd(sl)], 1)
    cmb[:, CM_MA2:CM_MA2 + 256] = np.concatenate([_bd(iu), _bd(su)], 1)
    cmb[:, CM_MB2:CM_MB2 + 256] = np.concatenate([_bd(il), _bd(sl)], 1)
    cmb[:, CM_IDB:CM_IDB + 128] = np.eye(128, dtype=np.float32)
    cmb[:, CM_ONESB:CM_ONESB + 128] = 1.0
    cmf = np.zeros((128, CF_N), np.float32)
    rs = np.ones(256, np.float32)
    rs[0::64] = 0.0
    cmf[:, CF_RESET:CF_RESET + 256] = rs[None, :]
    cmf[:, CF_ID:CF_ID + 128] = np.eye(128, dtype=np.float32)
    cmf[:, CF_BONES:CF_BONES + 128] = _bd(np.ones((64, 64), np.float32))
    cmf[:, CF_IDZ:CF_IDZ + 64] = np.tile(np.eye(64, dtype=np.float32), (2, 1))
    cmf[:, CF_ONES:CF_ONES + 128] = 1.0
    return cmb, cmf


def _rope_tables(half):
    m = np.arange(2176)
    pos = m if half == 0 else 4095 - m
    n_freq = 16
    inv_freq = np.power(np.float32(10000.0), -np.arange(n_freq, dtype=np.float32) / n_freq).astype(np.float32)
    rowp = (pos // 64).astype(np.float32)
    colp = (pos % 64).astype(np.float32)
    ang = np.concatenate([rowp[:, None] * inv_freq, colp[:, None] * inv_freq], axis=-1).astype(np.float32)
    cos, sin = np.cos(ang).astype(np.float32), np.sin(ang).astype(np.float32)
    cosd = np.concatenate([cos, cos], 1).T
    sind = np.concatenate([-sin, sin], 1).T
    return np.ascontiguousarray(np.tile(cosd, (2, 1))), np.ascontiguousarray(np.tile(sind, (2, 1)))


def _fm(vec, n):
    return np.ascontiguousarray(np.asarray(vec, np.float32).reshape(n, 128).T)


def prep_inputs(x, c, ctx, c_ctx, w_ada, b_ada, norm1_g, w_in, sink, conv_w, decay_w0, decay_w2,
                iclr_a0, iclr_a2, gate_g2, k_k, k_a, r_k, lnx_w, lnx_b, w_branch_attn, w_branch_rwkv,
                w_out, norm2_g, w_mlp_up, w_mlp_down, norm_f_g):
    f = lambda a: np.asarray(a, np.float32)
    x, c, ctx, c_ctx = f(x), f(c), f(ctx), f(c_ctx)
    w_in = f(w_in)[0]
    conv_w = f(conv_w)[0]
    cmb, cmf = _const_tables()
    QO, KO, VO, RWO, GO = 0, 512, 640, 768, 2464
    def swap_halves(w, nheads):
        w4 = w.reshape(1024, nheads, 2, 32)
        return w4[:, :, ::-1, :].reshape(1024, nheads * 64)
    wq = w_in[:, QO:QO + 512]
    wk = w_in[:, KO:KO + 128]
    wv = w_in[:, VO:VO + 128]
    wkp = swap_halves(wk, 2)
    kd = np.concatenate([wk[:, 0:64], wk[:, 0:64], wk[:, 64:128], wk[:, 64:128]], 1)
    kpd = np.concatenate([wkp[:, 0:64], wkp[:, 0:64], wkp[:, 64:128], wkp[:, 64:128]], 1)
    w_att = np.ascontiguousarray(np.concatenate([wq, swap_halves(wq, 8), kd, kpd, wv], 1))
    rw = w_in[:, RWO:RWO + 1696]
    w_rw = np.zeros((1024, 1792), np.float32)
    rwcol = np.full((14, 128), -1, np.int64)
    for j in range(4):
        for i3 in range(3):
            rwcol[3 * j + i3] = i3 * 512 + j * 128 + np.arange(128)
    rwcol[12, 0:32] = 1536 + np.arange(32)
    rwcol[12, 32:64] = 1568 + np.arange(32)
    rwcol[13, 0:96] = 1600 + np.arange(96)
    for ci in range(14):
        for pp in range(128):
            if rwcol[ci, pp] >= 0:
                w_rw[:, ci * 128 + pp] = rw[:, rwcol[ci, pp]]
    w_g = np.ascontiguousarray(w_in[:, GO:GO + 2048])
    g2 = np.ascontiguousarray(f(gate_g2)[0])
    nfbc = np.ascontiguousarray(np.tile(f(norm_f_g)[None, :], (128, 1)))
    w0, a0 = f(decay_w0)[0], f(iclr_a0)[0]
    w2, a2 = f(decay_w2)[0], f(iclr_a2)[0]
    in_maps = []
    for core in range(8):
        b, half = core // 2, core % 2
        if half == 0:
            xs = x[b, 0:2176]
            cs = ctx[b]
        else:
            xs = x[b, 4095:1919:-1]
            cs = ctx[b, ::-1]
        xin = np.ascontiguousarray(np.concatenate([cs, xs], 0))
        cc = np.zeros((128, 8, 2), np.float32)
        cc[:, :, 0] = c[b].reshape(8, 128).T
        cc[:, :, 1] = c_ctx.reshape(8, 128).T
        pv = np.zeros((128, PV_N), np.float32)
        pv[:, PV_BADA:PV_BADA + 48] = _fm(f(b_ada)[0], 48)
        pv[:, PV_G1:PV_G1 + 8] = _fm(f(norm1_g)[0], 8)
        pv[:, PV_G2:PV_G2 + 8] = _fm(f(norm2_g)[0], 8)
        dirs = (half, 1 - half)
        for dpos, dd in enumerate(dirs):
            pv[:, PV_W0 + dpos:PV_W0 + 8:2] = _fm(w0[dd], 4)
            pv[:, PV_A0 + dpos:PV_A0 + 8:2] = _fm(a0[dd], 4)
        pv[:, PV_KK:PV_KK + 4] = _fm(f(k_k)[0], 4)
        pv[:, PV_KA:PV_KA + 4] = _fm(f(k_a)[0], 4)
        pv[:, PV_RK:PV_RK + 4] = _fm(f(r_k)[0].reshape(512), 4)
        pv[:, PV_LW:PV_LW + 4] = _fm(f(lnx_w)[0], 4)
        pv[:, PV_LB:PV_LB + 4] = _fm(f(lnx_b)[0], 4)
        for ci in range(14):
            for tap in range(3):
                tp_ = tap if half == 0 else 2 - tap
                valid = rwcol[ci] >= 0
                pv[valid, PV_CW + ci * 3 + tap] = conv_w[tp_, rwcol[ci][valid]]
        pv[:, PV_FSEL] = float(half)
        pv[:, PV_FSEL + 1] = float(1 - half)
        loraw = np.zeros((64, 2, 2, 512), np.float32)
        for dpos, dd in enumerate(dirs):
            loraw[0:32, dpos, 0, :] = w2[dd]
            loraw[32:64, dpos, 1, :] = a2[dd]
        sk = f(sink)[0]
        sinkb = np.zeros((128, 2, 2, 128), np.float32)
        for g in range(2):
            for rr in range(2):
                for par in range(2):
                    sinkb[par * 64:(par + 1) * 64, g, rr, :] = sk[4 * g + 2 * rr + par]
        cosT, sinT = _rope_tables(half)
        in_maps.append({
            "xin": xin, "cc": cc.reshape(128, 16), "w_ada": f(w_ada)[0], "pv": pv, "sinkb": sinkb.reshape(128, 512),
            "nfbc": nfbc, "w_att": w_att, "w_rw": w_rw, "loraw": loraw.reshape(64, 2048), "g2": g2, "w_g": w_g,
            "w_ba": f(w_branch_attn)[0], "w_br": f(w_branch_rwkv)[0], "w_out": f(w_out)[0], "w_up": f(w_mlp_up)[0],
            "w_down": f(w_mlp_down)[0], "cosT": cosT, "sinT": sinT, "cmb": cmb, "cmf": cmf,
        })
    return in_maps


_NC_CACHE = {}


def kernel(**inputs):
    in_maps = prep_inputs(**inputs)
    if "nc" not in _NC_CACHE:
        _NC_CACHE["nc"] = build_program()[0]
    nc = _NC_CACHE["nc"]
    res = run_bass_kernel_spmd(nc, in_maps, core_ids=list(range(8)))
    out = np.zeros((4, 4096, 1024), np.float32)
    for core in range(8):
        b, half = core // 2, core % 2
        y = np.asarray(res.results[core]["out"], np.float32)
        if half == 0:
            out[b, 0:2048] = y
        else:
            out[b, 2048:4096] = y[::-1]
    return out
```
